# Optimizing a Trainium2 kernel written in Bass

```python
import math
import jax, jax.numpy as jnp
from jax import lax
import numpy as np

D_MODEL = 1024
BATCH = 8
SEQ = 2048
DEPTH = 4

N_MEM = 256
HEAD_DIM = 64
RWKV_HEADS = 16
RWKV_DIM = RWKV_HEADS * HEAD_DIM
DECAY_LORA = 64
ICLR_LORA = 64
GATE_LORA = 128
RWKV_GN_EPS = 64e-5
WIN_Q_HEADS = 16
WIN_KV_HEADS = 4
WIN_GROUP = WIN_Q_HEADS // WIN_KV_HEADS
WINDOW = 128
BLOCK = 128
DIFF_HEADS = 8
DIFF_V_DIM = 2 * HEAD_DIM
MEM_HEADS = 4
MEM_WIDTH = MEM_HEADS * HEAD_DIM
NUM_BUCKETS = 32
MAX_DISTANCE = 128
N_EXPERTS = 16
CAPACITY_FACTOR = 2
D_EXPERT = 2 * D_MODEL
DN_ALPHA = (2 * DEPTH) ** 0.25
DN_BETA = (8 * DEPTH) ** -0.25
N_BRANCHES = 4

RWKV_COLS = 3 * RWKV_DIM + DECAY_LORA + ICLR_LORA + GATE_LORA
WIN_Q_COLS = WIN_Q_HEADS * HEAD_DIM
WIN_KV_COLS = WIN_KV_HEADS * HEAD_DIM
WIN_COLS = WIN_Q_COLS + 2 * WIN_KV_COLS
DIFF_QK_COLS = 2 * DIFF_HEADS * HEAD_DIM
DIFF_V_COLS = DIFF_HEADS * DIFF_V_DIM
DIFF_COLS = 2 * DIFF_QK_COLS + DIFF_V_COLS
GATE_COLS = N_BRANCHES * D_MODEL
IN_COLS = RWKV_COLS + WIN_COLS + DIFF_COLS + MEM_WIDTH + GATE_COLS
IN_SPLITS = (RWKV_COLS, RWKV_COLS + WIN_COLS, RWKV_COLS + WIN_COLS + DIFF_COLS,
             RWKV_COLS + WIN_COLS + DIFF_COLS + MEM_WIDTH)
RWKV_SPLITS = (RWKV_DIM, 2 * RWKV_DIM, 3 * RWKV_DIM, 3 * RWKV_DIM + DECAY_LORA,
               3 * RWKV_DIM + DECAY_LORA + ICLR_LORA)
BRANCH_ROWS = RWKV_DIM + WIN_Q_COLS + DIFF_V_COLS + MEM_WIDTH
BRANCH_SPLITS = (RWKV_DIM, RWKV_DIM + WIN_Q_COLS, RWKV_DIM + WIN_Q_COLS + DIFF_V_COLS)
NEG_INF = -1e30

kernel_name = 'hybrid_rwkv7_window_diffattn_ecmoe_encoder'


def layer_norm(x, g, b, eps=1e-5):
    xf = x.astype(jnp.float32)
    mu = xf.mean(-1, keepdims=True)
    var = jnp.square(xf - mu).mean(-1, keepdims=True)
    return ((xf - mu) * lax.rsqrt(var + eps) * g + b).astype(x.dtype)


def t5_bucket(rel):
    half = NUM_BUCKETS // 2
    exact = half // 2
    n = jnp.abs(rel)
    nf = jnp.maximum(n, 1).astype(jnp.float32)
    large = exact + (jnp.log(nf / exact) / math.log(MAX_DISTANCE / exact)
                     * (half - exact)).astype(jnp.int32)
    large = jnp.minimum(large, half - 1)
    return jnp.where(rel > 0, half, 0) + jnp.where(n < exact, n, large)


def centred_shift(p, mu):
    prev = jnp.pad(p[:, :-1], ((0, 0), (1, 0), (0, 0)))
    nxt = jnp.pad(p[:, 1:], ((0, 0), (0, 1), (0, 0)))
    return p + mu[0] * (prev - p) + mu[1] * (nxt - p)


def rwkv7_branch(p, mu, w0, w_up, a0, a_up, g_up, k_k, k_a, r_k, gn_g, gn_b):
    B, S, _ = p.shape
    H, N = RWKV_HEADS, HEAD_DIM
    f32 = jnp.float32
    p = centred_shift(p, mu)
    r, k, v, dw, da, dg = jnp.split(p, RWKV_SPLITS, axis=-1)
    w_logit = (w0[:, None, None, :] + jnp.einsum('bsr,zrc->zbsc', jnp.tanh(dw), w_up)).astype(f32)
    decay = jnp.exp(-jnp.exp(-jax.nn.softplus(-w_logit) - 0.5))
    a = jax.nn.sigmoid((a0[:, None, None, :] + jnp.einsum('bsr,zrc->zbsc', da, a_up)).astype(f32))
    g = jnp.einsum('bsr,rc->bsc', jax.nn.sigmoid(dg), g_up).astype(f32)
    kk = (k * k_k).astype(f32).reshape(B, S, H, N)
    kk = kk * lax.rsqrt(jnp.maximum(jnp.sum(kk * kk, -1, keepdims=True), 1e-12))
    kk = kk.reshape(B, S, RWKV_DIM)
    kd = k.astype(f32)[None] * (1.0 + (a - 1.0) * k_a.astype(f32))
    rf, vf = r.astype(f32), v.astype(f32)

    def both(z):
        return jnp.stack([z, z[:, ::-1]])

    def rev1(z):
        return jnp.stack([z[0], z[1, :, ::-1]])

    def to_t(z):
        return jnp.moveaxis(z.reshape(2, B, S, H, N), 2, 0)

    kk2 = both(kk)
    xs = (to_t(both(rf)), to_t(rev1(decay)), to_t(rev1(kd)), to_t(both(vf)),
          to_t(kk2), to_t(rev1(a) * kk2))

    def step(state, inp):
        r_t, w_t, k_t, v_t, kk_t, kka_t = inp
        sa = jnp.einsum('zbhvk,zbhk->zbhv', state, kk_t)
        state = (state * w_t[..., None, :] - sa[..., None] * kka_t[..., None, :]
                 + v_t[..., None] * k_t[..., None, :])
        return state, jnp.einsum('zbhvk,zbhk->zbhv', state, r_t)

    state0 = jnp.zeros((2, B, H, N, N), f32)
    _, ys = lax.scan(step, state0, xs)
    ys = jnp.moveaxis(ys, 0, 2)
    y = ys[0] + ys[1, :, ::-1]
    mu_y = y.mean(-1, keepdims=True)
    var_y = jnp.square(y - mu_y).mean(-1, keepdims=True)
    y = ((y - mu_y) * lax.rsqrt(var_y + RWKV_GN_EPS)).reshape(B, S, RWKV_DIM) * gn_g + gn_b
    r_h = rf.reshape(B, S, H, N)
    bonus = jnp.sum(r_h * (kd[0] + kd[1]).reshape(B, S, H, N) * r_k.astype(f32), -1, keepdims=True) \
        * vf.reshape(B, S, H, N)
    return ((y + bonus.reshape(B, S, RWKV_DIM)) * g).astype(p.dtype)


def window_branch(p, sink, rel_bias):
    B, S, _ = p.shape
    nb = S // BLOCK
    Hkv, G, dh = WIN_KV_HEADS, WIN_GROUP, HEAD_DIM
    q = p[..., :WIN_Q_COLS].reshape(B, nb, BLOCK, Hkv, G, dh)
    k = p[..., WIN_Q_COLS:WIN_Q_COLS + WIN_KV_COLS].reshape(B, S, Hkv, dh)
    v = p[..., WIN_Q_COLS + WIN_KV_COLS:].reshape(B, S, Hkv, dh)

    def band(z):
        zp = jnp.pad(z, ((0, 0), (BLOCK, BLOCK), (0, 0), (0, 0))).reshape(B, nb + 2, BLOCK, Hkv, dh)
        return jnp.concatenate([zp[:, :-2], zp[:, 1:-1], zp[:, 2:]], axis=2)

    kb, vb = band(k), band(v)
    s = jnp.einsum('bnqhgd,bnkhd->bnhgqk', q, kb).astype(jnp.float32) * (dh ** -0.5)
    qi = jnp.arange(BLOCK)[:, None]
    kj = jnp.arange(3 * BLOCK)[None, :]
    rel = kj - BLOCK - qi
    bias = rel_bias[t5_bucket(rel)][..., :WIN_Q_HEADS].astype(jnp.float32)
    bias = bias.transpose(2, 0, 1).reshape(Hkv, G, BLOCK, 3 * BLOCK)
    kpos = jnp.arange(nb)[:, None] * BLOCK + kj - BLOCK
    valid = (jnp.abs(rel) <= WINDOW)[None] & ((kpos >= 0) & (kpos < S))[:, None, :]
    s = jnp.where(valid[None, :, None, None], s + bias, NEG_INF)
    sink_b = sink.astype(jnp.float32).reshape(Hkv, G)[..., None, None]
    m = jnp.maximum(s.max(-1, keepdims=True), sink_b)
    e = jnp.exp(s - m)
    pr = (e / (e.sum(-1, keepdims=True) + jnp.exp(sink_b - m))).astype(v.dtype)
    o = jnp.einsum('bnhgqk,bnkhd->bnqhgd', pr, vb)
    return o.reshape(B, S, WIN_Q_COLS)


def diff_branch(p, lam, subln_g, rel_bias, lambda_init):
    B, S, _ = p.shape
    H, d = DIFF_HEADS, HEAD_DIM
    nb = S // BLOCK
    q = p[..., :DIFF_QK_COLS].reshape(B, S, H, 2, d)
    k = p[..., DIFF_QK_COLS:2 * DIFF_QK_COLS].reshape(B, S, H, 2, d)
    v = p[..., 2 * DIFF_QK_COLS:].reshape(B, S, H, DIFF_V_DIM)
    lamf = lam.astype(jnp.float32)
    lam_full = (jnp.exp(jnp.sum(lamf[0] * lamf[1])) - jnp.exp(jnp.sum(lamf[2] * lamf[3]))
                + lambda_init)
    bias_tab = rel_bias[:, WIN_Q_HEADS:].astype(jnp.float32)
    kpos = jnp.arange(S)
    qb = q.reshape(B, nb, BLOCK, H, 2, d).transpose(1, 0, 2, 3, 4, 5)

    def one_block(args):
        q_blk, i0 = args
        s = jnp.einsum('bqhcd,bkhcd->bhcqk', q_blk, k).astype(jnp.float32) * (d ** -0.5)
        rel = kpos[None, :] - (i0 + jnp.arange(BLOCK))[:, None]
        bias = bias_tab[t5_bucket(rel)].transpose(2, 0, 1)
        a = jax.nn.softmax(s + bias[None, :, None], axis=-1)
        attn = a[:, :, 0] - lam_full * a[:, :, 1]
        return jnp.einsum('bhqk,bkhe->bqhe', attn.astype(v.dtype), v)

    o = lax.map(one_block, (qb, jnp.arange(nb) * BLOCK))
    o = o.transpose(1, 0, 2, 3, 4).reshape(B, S, H, DIFF_V_DIM).astype(jnp.float32)
    o = o * lax.rsqrt(jnp.mean(o * o, -1, keepdims=True) + 1e-5) * subln_g * (1.0 - lambda_init)
    return o.reshape(B, S, DIFF_V_COLS).astype(p.dtype)


def memory_branch(q_cols, mem, w_kv):
    B, S, _ = q_cols.shape
    q = q_cols.reshape(B, S, MEM_HEADS, HEAD_DIM)
    kv = jnp.einsum('bmd,dc->bmc', mem, w_kv)
    k = kv[..., :MEM_WIDTH].reshape(B, N_MEM, MEM_HEADS, HEAD_DIM)
    v = kv[..., MEM_WIDTH:].reshape(B, N_MEM, MEM_HEADS, HEAD_DIM)
    s = jnp.einsum('bshd,bmhd->bhsm', q, k).astype(jnp.float32) * (HEAD_DIM ** -0.5)
    a = jax.nn.softmax(s, axis=-1).astype(v.dtype)
    return jnp.einsum('bhsm,bmhd->bshd', a, v).reshape(B, S, MEM_WIDTH)


def expert_choice_ffn(x, router, w_gate, w_up, w_down):
    B, S, D = x.shape
    cap = CAPACITY_FACTOR * S // N_EXPERTS
    aff = jax.nn.softmax(jnp.einsum('bsd,de->bse', x, router).astype(jnp.float32), axis=-1)
    gate, idx = lax.top_k(aff.transpose(0, 2, 1), cap)
    xe = jax.vmap(lambda xb, ib: xb[ib])(x, idx)
    h = jax.nn.silu(jnp.einsum('becd,edf->becf', xe, w_gate)) * jnp.einsum('becd,edf->becf', xe, w_up)
    ye = jnp.einsum('becf,efd->becd', h, w_down) * gate[..., None].astype(x.dtype)
    flat = (idx + jnp.arange(B)[:, None, None] * S).reshape(-1)
    y = jnp.zeros((B * S, D), x.dtype).at[flat].add(ye.reshape(-1, D))
    return y.reshape(B, S, D)


def setup_inputs(seed: int = 0) -> dict:
    key = jax.random.key(seed)
    ks = jax.random.split(key, 32)
    L, D = DEPTH, D_MODEL

    def nrm(k, shape, s):
        return jax.random.normal(k, shape, jnp.float32) * s

    return {
        'x': nrm(ks[0], (BATCH, SEQ, D), 1.0),
        'mem': nrm(ks[1], (BATCH, N_MEM, D), 1.0),
        'rel_bias': nrm(ks[2], (NUM_BUCKETS, WIN_Q_HEADS + DIFF_HEADS), 0.5),
        'w_in': nrm(ks[3], (L, D, IN_COLS), D ** -0.5),
        'rwkv_mu': jax.random.uniform(ks[4], (L, 2, RWKV_COLS), jnp.float32, 0.0, 0.5),
        'rwkv_w0': jax.random.uniform(ks[5], (L, 2, RWKV_DIM), jnp.float32, -6.0, 1.0),
        'rwkv_w_up': nrm(ks[6], (L, 2, DECAY_LORA, RWKV_DIM), 0.1),
        'rwkv_a0': nrm(ks[7], (L, 2, RWKV_DIM), 0.5),
        'rwkv_a_up': nrm(ks[8], (L, 2, ICLR_LORA, RWKV_DIM), ICLR_LORA ** -0.5),
        'rwkv_g_up': nrm(ks[9], (L, GATE_LORA, RWKV_DIM), GATE_LORA ** -0.5),
        'rwkv_k_k': 0.85 + nrm(ks[10], (L, RWKV_DIM), 0.05),
        'rwkv_k_a': 1.0 + nrm(ks[11], (L, RWKV_DIM), 0.05),
        'rwkv_r_k': nrm(ks[12], (L, RWKV_HEADS, HEAD_DIM), 0.1),
        'rwkv_gn_g': 1.0 + nrm(ks[13], (L, RWKV_DIM), 0.1),
        'rwkv_gn_b': nrm(ks[14], (L, RWKV_DIM), 0.01),
        'win_sink': nrm(ks[15], (L, WIN_Q_HEADS), 0.5),
        'diff_lambda': nrm(ks[16], (L, 4, HEAD_DIM), 0.1),
        'diff_subln_g': 1.0 + nrm(ks[17], (L, DIFF_V_DIM), 0.1),
        'mem_w_kv': nrm(ks[18], (L, D, 2 * MEM_WIDTH), D ** -0.5),
        'w_branch': nrm(ks[19], (L, BRANCH_ROWS, D), (RWKV_DIM ** -0.5) * DN_BETA),
        'w_out': nrm(ks[20], (L, D, D), (D ** -0.5) * DN_BETA),
        'ln1_g': 1.0 + nrm(ks[21], (L, D), 0.1),
        'ln1_b': nrm(ks[22], (L, D), 0.01),
        'router': nrm(ks[23], (L, D, N_EXPERTS), D ** -0.5),
        'exp_w_gate': nrm(ks[24], (L, N_EXPERTS, D, D_EXPERT), (D ** -0.5) * DN_BETA),
        'exp_w_up': nrm(ks[25], (L, N_EXPERTS, D, D_EXPERT), (D ** -0.5) * DN_BETA),
        'exp_w_down': nrm(ks[26], (L, N_EXPERTS, D_EXPERT, D), (D_EXPERT ** -0.5) * DN_BETA),
        'ln2_g': 1.0 + nrm(ks[27], (L, D), 0.1),
        'ln2_b': nrm(ks[28], (L, D), 0.01),
    }


def reference(x, mem, rel_bias, w_in, rwkv_mu, rwkv_w0, rwkv_w_up, rwkv_a0, rwkv_a_up,
              rwkv_g_up, rwkv_k_k, rwkv_k_a, rwkv_r_k, rwkv_gn_g, rwkv_gn_b, win_sink,
              diff_lambda, diff_subln_g, mem_w_kv, w_branch, w_out, ln1_g, ln1_b, router,
              exp_w_gate, exp_w_up, exp_w_down, ln2_g, ln2_b):
    B, S, D = x.shape
    for l in range(DEPTH):
        lambda_init = 0.8 - 0.6 * math.exp(-0.3 * l)
        p = jnp.einsum('bsd,dc->bsc', x, w_in[l])
        p_rwkv, p_win, p_diff, p_mem, p_gate = jnp.split(p, IN_SPLITS, axis=-1)
        y_rwkv = rwkv7_branch(p_rwkv, rwkv_mu[l], rwkv_w0[l], rwkv_w_up[l], rwkv_a0[l],
                              rwkv_a_up[l], rwkv_g_up[l], rwkv_k_k[l], rwkv_k_a[l],
                              rwkv_r_k[l], rwkv_gn_g[l], rwkv_gn_b[l])
        y_win = window_branch(p_win, win_sink[l], rel_bias)
        y_diff = diff_branch(p_diff, diff_lambda[l], diff_subln_g[l], rel_bias, lambda_init)
        y_mem = memory_branch(p_mem, mem, mem_w_kv[l])
        wb_rwkv, wb_win, wb_diff, wb_mem = jnp.split(w_branch[l], BRANCH_SPLITS, axis=0)
        gates = jax.nn.sigmoid(p_gate.reshape(B, S, N_BRANCHES, D))
        merged = (gates[:, :, 0] * (y_rwkv @ wb_rwkv) + gates[:, :, 1] * (y_win @ wb_win)
                  + gates[:, :, 2] * (y_diff @ wb_diff) + gates[:, :, 3] * (y_mem @ wb_mem))
        x = layer_norm(DN_ALPHA * x + merged @ w_out[l], ln1_g[l], ln1_b[l])
        ffn = expert_choice_ffn(x, router[l], exp_w_gate[l], exp_w_up[l], exp_w_down[l])
        x = layer_norm(DN_ALPHA * x + ffn, ln2_g[l], ln2_b[l])
    return x
```

```python
import math
from contextlib import ExitStack, contextmanager

import numpy as np
import concourse.bass as bass
import concourse.mybir as mybir
from concourse.bass_utils import run_bass_kernel_spmd

F32 = mybir.dt.float32
BF16 = mybir.dt.bfloat16
I32 = mybir.dt.int32
AF = mybir.ActivationFunctionType
ALU = mybir.AluOpType
AX = mybir.AxisListType

T = 2048
D = 1024
NT = 16
DC = 8
NL = 4
NCORES = 8
SEM_LIMIT = 30000
DN_ALPHA = (2 * NL) ** 0.25
C_DECAY = math.exp(-0.5)

O_RWKV = 0
O_WINQ = 3328
O_WINK = 4352
O_WINV = 4608
O_DQ = 4864
O_DK = 5888
O_DV = 6912
O_MEMQ = 7936
O_GATE = 8192


class Buf:
    __slots__ = ("name", "w", "r", "dsem", "dcnt")

    def __init__(self, name):
        self.name = name
        self.w = {}
        self.r = {}
        self.dsem = None
        self.dcnt = 0


class Prog:
    ENGS = ("tensor", "vector", "scalar", "gpsimd", "sync")

    def __init__(self, nc):
        self.nc = nc
        self.es = ExitStack()
        self.q = {n: [] for n in self.ENGS}
        self.esem = {}
        self.ecnt = {}
        self.waited = {n: {} for n in self.ENGS}
        self.nsem = 0
        self.pe_sems = set()
        self.semobj = {}
        self.free_dsems = []
        self.scopes = [[]]
        self.stacks = [self.es]
        self.n_inst = 0
        self.nname = 0
        self.pstate = {}
        for n in self.ENGS:
            self._new_esem(n)

    def sem(self, name):
        self.nsem += 1
        s = self.es.enter_context(self.nc.semaphore(f"{name}_{self.nsem}"))
        self.semobj[id(s)] = s
        return s

    def _new_esem(self, n):
        self.esem[n] = self.sem("e" + n)
        self.ecnt[n] = 0
        if n == "tensor":
            self.pe_sems.add(id(self.esem[n]))

    def sb(self, name, shape, dt):
        self.nname += 1
        return self.stacks[-1].enter_context(self.nc.sbuf_tensor(f"{name}_s{self.nname}", shape, dt))

    def ps(self, name, shape, dt=F32):
        self.nname += 1
        return self.stacks[-1].enter_context(self.nc.psum_tensor(f"{name}_p{self.nname}", shape, dt))

    def buf(self, name):
        b = Buf(name)
        self.scopes[-1].append(b)
        return b

    def tile(self, name, shape, dt):
        return self.sb(name, shape, dt), self.buf(name)

    def ptile(self, name, shape, dt=F32):
        return self.ps(name, shape, dt), self.buf(name)

    @contextmanager
    def scope(self):
        st = ExitStack()
        self.stacks.append(st)
        self.scopes.append([])
        try:
            yield
        finally:
            self.barrier()
            for b in self.scopes.pop():
                if b.dsem is not None:
                    self.free_dsems.append((b.dsem, b.dcnt))
                    b.dsem = None
            self.stacks.pop()
            st.close()

    def _deps(self, reads, writes, skip=()):
        deps = {}

        def add(d):
            for k, v in d.items():
                if k in skip:
                    continue
                if deps.get(k, 0) < v:
                    deps[k] = v

        for b in reads:
            add(b.w)
        for b in writes:
            add(b.w)
            add(b.r)
        return deps

    def _waits(self, eng, deps):
        out = []
        wd = self.waited[eng]
        for k, v in deps.items():
            if wd.get(k, 0) >= v:
                continue
            wd[k] = v
            out.append((self.semobj[k], v))
        return out

    def op(self, eng, fn, reads=(), writes=()):
        if self.ecnt[eng] >= SEM_LIMIT:
            self._new_esem(eng)
        skip = self.pe_sems if eng == "tensor" else ()
        waits = self._waits(eng, self._deps(reads, writes, skip))
        self.ecnt[eng] += 1
        s = self.esem[eng]
        v = self.ecnt[eng]
        k = id(s)
        self.q[eng].append((waits, fn, s, 1))
        self.n_inst += 1
        for b in reads:
            if b.r.get(k, 0) < v:
                b.r[k] = v
        for b in writes:
            b.w = {k: v}
            b.r = {}

    def _get_dsem(self, dst):
        if dst.dsem is None or dst.dcnt >= SEM_LIMIT:
            if self.free_dsems and dst.dsem is None:
                dst.dsem, dst.dcnt = self.free_dsems.pop()
                if dst.dcnt >= SEM_LIMIT:
                    dst.dsem = self.sem("d")
                    dst.dcnt = 0
            else:
                dst.dsem = self.sem("d")
                dst.dcnt = 0

    def dma(self, eng, out, in_, reads=(), writes=(), **kw):
        dst = writes[0]
        old = dst.dsem
        self._get_dsem(dst)
        skip = (id(dst.dsem),) if old is dst.dsem else ()
        waits = self._waits(eng, self._deps(reads, writes, skip))
        dst.dcnt += 16
        s, v = dst.dsem, dst.dcnt
        k = id(s)
        self.q[eng].append((waits, (lambda e, o=out, i=in_, kw=kw: e.dma_start(out=o, in_=i, **kw)), s, 16))
        self.n_inst += 1
        for b in reads:
            if b.r.get(k, 0) < v:
                b.r[k] = v
        for b in writes:
            b.w = {k: v}
            b.r = {}

    def pstart(self, B, bank, pk):
        d = self.pstate.setdefault(id(B), set())
        keys = [(bank, "l"), (bank, "h")] if pk == "f" else [(bank, pk)]
        st = not all(k in d for k in keys)
        d.update(keys)
        return st

    def preset(self, B, pk=None):
        d = self.pstate.get(id(B))
        if d is None:
            return
        if pk is None:
            d.clear()
        else:
            for k in [k for k in d if k[1] == pk]:
                d.discard(k)

    def barrier(self):
        deps = {}
        for n in self.ENGS:
            if self.ecnt[n] > 0:
                deps[id(self.esem[n])] = self.ecnt[n]
        for sc in self.scopes:
            for b in sc:
                if b.dsem is not None and b.dcnt > 0:
                    k = id(b.dsem)
                    if deps.get(k, 0) < b.dcnt:
                        deps[k] = b.dcnt
        for n in self.ENGS:
            own = id(self.esem[n])
            d = {k: v for k, v in deps.items() if k != own}
            waits = self._waits(n, d)
            if waits:
                self.q[n].append((waits, None, None, 0))

    def emit(self):
        nc = self.nc
        with nc.Block() as block:
            def mk(name):
                def body(e):
                    for waits, fn, s, n in self.q[name]:
                        for ws, wv in waits:
                            e.wait_ge(ws, wv)
                        if fn is not None:
                            fn(e).then_inc(s, n)
                return body
            block.sync(mk("sync"))
            block.tensor(mk("tensor"))
            block.vector(mk("vector"))
            block.scalar(mk("scalar"))
            block.gpsimd(mk("gpsimd"))
        self.es.close()


class Ctx:
    pass


def mm(P, out, lhsT, rhs, start, stop, reads, writes):
    P.op("tensor", lambda e: e.matmul(out, lhsT, rhs, start=start, stop=stop), reads=reads, writes=writes)


def mma(P, out, lhsT, rhs, Bps, bank, pk, reads, stop=False):
    mm(P, out, lhsT, rhs, P.pstart(Bps, bank, pk), stop, reads, [Bps])


def tr(P, out, in_, ident, reads, writes):
    P.op("tensor", lambda e: e.transpose(out, in_, ident), reads=reads, writes=writes)


def act(P, out, in_, func, reads, writes, bias=None, scale=None, accum_out=None):
    kw = {}
    if bias is not None:
        kw["bias"] = bias
    if scale is not None:
        kw["scale"] = scale
    if accum_out is not None:
        kw["accum_out"] = accum_out
    P.op("scalar", lambda e: e.activation(out=out, in_=in_, func=func, **kw), reads=reads, writes=writes)


def vcopy(P, out, in_, reads, writes, eng="vector"):
    P.op(eng, lambda e: e.tensor_copy(out=out, in_=in_), reads=reads, writes=writes)


def tt(P, out, in0, in1, op, reads, writes, eng="vector"):
    P.op(eng, lambda e: e.tensor_tensor(out=out, in0=in0, in1=in1, op=op), reads=reads, writes=writes)


def ts(P, out, in0, s1, op0, reads, writes, s2=None, op1=None, eng="vector", accum_out=None):
    kw = {}
    if op1 is not None:
        kw["op1"] = op1
    if accum_out is not None:
        kw["accum_out"] = accum_out
    P.op(eng, lambda e: e.tensor_scalar(out=out, in0=in0, scalar1=s1, scalar2=s2, op0=op0, **kw), reads=reads, writes=writes)


def stt(P, out, in0, scalar, in1, op0, op1, reads, writes):
    P.op("vector", lambda e: e.scalar_tensor_tensor(out=out, in0=in0, scalar=scalar, in1=in1, op0=op0, op1=op1),
         reads=reads, writes=writes)


def rsqrt(P, cx, out, in_, B, scale, eps):
    ts(P, out, in_, scale, ALU.mult, [B], [B], s2=eps, op1=ALU.add)
    act(P, out, out, AF.Sqrt, [B], [B])
    P.op("vector", lambda e: e.reciprocal(out=out, in_=out), reads=[B], writes=[B])


def memset(P, ap, val, writes, eng="vector"):
    P.op(eng, lambda e: e.memset(ap, val), writes=writes)


def setup_consts(P, cx):
    nc = P.nc
    dif_i, Bd = P.tile("dif_i", [128, 128], I32)
    P.op("gpsimd", lambda e: e.iota(dif_i[:], pattern=[[1, 128]], base=0, channel_multiplier=-1), writes=[Bd])
    dif, Bdf = P.tile("dif_f", [128, 128], F32)
    vcopy(P, dif[:], dif_i[:], [Bd], [Bdf])
    cx.ident_f, cx.Bident_f = P.tile("ident_f", [128, 128], F32)
    ts(P, cx.ident_f[:], dif[:], 0.0, ALU.is_equal, [Bdf], [cx.Bident_f])
    cx.ident_b, cx.Bident_b = P.tile("ident_b", [128, 128], BF16)
    vcopy(P, cx.ident_b[:], cx.ident_f[:], [cx.Bident_f], [cx.Bident_b])
    cx.dif = dif
    cx.Bdif = Bdf
    ci_i, Bci = P.tile("ci_i", [128, 256], I32)
    P.op("gpsimd", lambda e: e.iota(ci_i[:], pattern=[[1, 256]], base=0, channel_multiplier=0), writes=[Bci])
    cx.colidx, cx.Bcolidx = P.tile("colidx", [128, 256], F32)
    vcopy(P, cx.colidx[:], ci_i[:], [Bci], [cx.Bcolidx])
    pi_i, Bpi = P.tile("pi_i", [128, 1], I32)
    P.op("gpsimd", lambda e: e.iota(pi_i[:], pattern=[[0, 1]], base=0, channel_multiplier=1), writes=[Bpi])
    cx.pidx, cx.Bpidx = P.tile("pidx", [128, 1], F32)
    vcopy(P, cx.pidx[:], pi_i[:], [Bpi], [cx.Bpidx])
    cx.blk, cx.Bblk = P.tile("blk", [128, 128], F32)
    memset(P, cx.blk[:], 0.0, [cx.Bblk])
    memset(P, cx.blk[0:64, 0:64], 1.0, [cx.Bblk])
    memset(P, cx.blk[64:128, 64:128], 1.0, [cx.Bblk])
    cx.blk64, cx.Bblk64 = P.tile("blk64", [128, 128], F32)
    ts(P, cx.blk64[:], cx.blk[:], 1.0 / 64.0, ALU.mult, [cx.Bblk], [cx.Bblk64])
    cx.ones_b, cx.Bones_b = P.tile("ones_b", [128, 128], BF16)
    memset(P, cx.ones_b[:], 1.0, [cx.Bones_b])


def build_xT(P, cx, src_dram):
    with P.scope():
        xin = [P.tile(f"xin{i}", [128, D], F32) for i in range(2)]
        pst = [P.ptile(f"xtp{i}", [128, 512], F32) for i in range(2)]
        k = 0
        for i in range(NT):
            xt_, Bx = xin[i % 2]
            P.dma("sync", xt_[:], src_dram[i * 128:(i + 1) * 128, :], writes=[Bx])
            for half in range(2):
                pt, Bp = pst[k % 2]
                k += 1
                for j in range(4):
                    c = half * 4 + j
                    tr(P, pt[:, j * 128:(j + 1) * 128], xt_[:, c * 128:(c + 1) * 128], cx.ident_f[:],
                       [Bx, cx.Bident_f], [Bp])
                dst = cx.xT[:, half * 4:half * 4 + 4, i * 128:(i + 1) * 128]
                src = pt[:].rearrange("p (j t) -> p j t", j=4)
                if half == 0:
                    vcopy(P, dst, src, [Bp], [cx.BxT])
                else:
                    act(P, dst, src, AF.Copy, [Bp], [cx.BxT])


def load_w(P, dst_tile, Bdst, w2d, c0, ncols, r0=0, nchunks=DC, eng="gpsimd"):
    src = w2d[r0:r0 + nchunks * 128, c0:c0 + ncols].rearrange("(c p) n -> p c n", p=128)
    P.dma(eng, dst_tile, src, writes=[Bdst])


def proj_fm(P, cx, dst_fn, W, BW, wcol0, M, evac, nchunks=DC, rhs=None, Brhs=None, tlen=T, pst=None, extra_reads=()):
    rhs = cx.xT if rhs is None else rhs
    Brhs = cx.BxT if Brhs is None else Brhs
    nb = tlen // 512
    for tb in range(nb):
        pt, Bp = pst[tb % len(pst)]
        for c in range(nchunks):
            mm(P, pt[0:M, :], W[:, c, wcol0:wcol0 + M], rhs[:, c, tb * 512:(tb + 1) * 512], c == 0, c == nchunks - 1,
               [BW, Brhs], [Bp])
        evac(tb, pt, Bp)


class Rot:
    def __init__(self, items):
        self.items = items
        self.i = 0

    def next(self):
        it = self.items[self.i % len(self.items)]
        self.i += 1
        return it


def attn_block(P, cx, acc, Bacc, dvp, q_ap, Bq, k_ap, Bk, v_ap, Bv, plan, pss, tmps, ets):
    started = set()
    slot_w = acc.shape[2]
    for (j, qlo, qhi, kind, arg, firsts, lasts) in plan:
        ncol = (qhi - qlo + 1) * 128
        ps_s, Bps = pss.next()
        mm(P, ps_s[:, 0:ncol], k_ap(j), q_ap(qlo, ncol), True, True, [Bk, Bq], [Bps])
        eT, Be = ets.next()
        if kind == "tbl":
            tmp, Bt = tmps.next()
            tab, Btab = arg
            stt(P, tmp[:, 0:ncol], ps_s[:, 0:ncol], 0.125, tab, ALU.mult, ALU.add, [Bps, Btab], [Bt])
            act(P, eT[:, 0:ncol], tmp[:, 0:ncol], AF.Exp, [Bt], [Be])
        elif kind == "far":
            col, Bcol = arg
            act(P, eT[:, 0:ncol], ps_s[:, 0:ncol], AF.Exp, [Bps, Bcol], [Be], bias=col, scale=0.125)
        else:
            act(P, eT[:, 0:ncol], ps_s[:, 0:ncol], AF.Exp, [Bps], [Be], scale=0.125)
        for s in range(qlo, qhi + 1):
            bank = (s * slot_w) // 512
            st = bank not in started
            started.add(bank)
            mm(P, acc[:, s, 0:dvp], eT[:, (s - qlo) * 128:(s - qlo + 1) * 128], v_ap(j), st, s in lasts,
               [Be, Bv], [Bacc])


def ytok_to_dram(P, cx, ytok, Bytok, yT, ByT, pst, chunk):
    for q4 in range(4):
        pt, Bp = pst.next()
        ptb = pt[:].bitcast(BF16)
        for j in range(4):
            i = q4 * 4 + j
            tr(P, ptb[:, j * 128:(j + 1) * 128], ytok[:, i, :], cx.ident_b[:], [Bytok, cx.Bident_b], [Bp])
        if q4 % 2 == 0:
            vcopy(P, yT[:, q4 * 512:(q4 + 1) * 512], ptb[:, 0:512], [Bp], [ByT])
        else:
            act(P, yT[:, q4 * 512:(q4 + 1) * 512], ptb[:, 0:512], AF.Copy, [Bp], [ByT])
    P.dma("sync", cx.ybrT[chunk * 128:(chunk + 1) * 128, :], yT[:], reads=[ByT], writes=[cx.BybrT[chunk]])


def phase_window(P, cx, l):
    w_in = cx.w_in[l]
    with P.scope():
        W, BW = P.tile("winW", [128, DC, 1536], BF16)
        for k in range(3):
            load_w(P, W[:, :, k * 512:(k + 1) * 512], BW, w_in, O_WINQ + k * 512, 512)
        pst = Rot([P.ptile(f"wps{i}", [128, 512], F32) for i in range(2)])
        pss = Rot([P.ptile(f"wss{i}", [128, 512], F32) for i in range(2)])
        accs = Rot([P.ptile(f"wacc{i}", [128, 4, 128], F32) for i in range(2)])
        tmps = Rot([P.tile(f"wtmp{i}", [128, 512], F32) for i in range(2)])
        ets = Rot([P.tile(f"wet{i}", [128, 512], BF16) for i in range(3)])
        kT, BkT = P.tile("winkT", [64, 4, T], BF16)
        for h in range(4):
            def ev(tb, pt, Bp, h=h):
                vcopy(P, kT[:, h, tb * 512:(tb + 1) * 512], pt[0:64, :], [Bp], [BkT])
            proj_fm(P, cx, None, W, BW, 1024 + h * 64, 64, ev, pst=pst.items)
        vaug, Bva = P.tile("winv", [128, NT, 4, 65], BF16)
        memset(P, vaug[:, :, :, 64:65], 1.0, [Bva], eng="gpsimd")
        for i in range(NT):
            pt, Bp = pst.next()
            for c in range(DC):
                mm(P, pt[:, 0:256], cx.xT[:, c, i * 128:(i + 1) * 128], W[:, c, 1280:1536], c == 0, c == DC - 1,
                   [cx.BxT, BW], [Bp])
            act(P, vaug[:, i, :, 0:64], pt[:, 0:256].rearrange("p (h d) -> p h d", h=4), AF.Copy, [Bp], [Bva])
        sk, Bsk = P.tile("sinkb", [128, 16], F32)
        P.dma("sync", sk[:], cx.win_sink[l, :].partition_broadcast(128), writes=[Bsk])
        esk, Besk = P.tile("esink", [128, 16], F32)
        act(P, esk[:], sk[:], AF.Exp, [Bsk], [Besk])
        qTs = Rot([P.tile(f"winq{i}", [64, T], BF16) for i in range(2)])
        Gs = Rot([P.tile(f"winG{i}", [128, 1152], F32) for i in range(2)])
        ytoks = Rot([P.tile(f"wytok{i}", [128, NT, 128], BF16) for i in range(2)])
        yTs = Rot([P.tile(f"wyT{i}", [128, T], BF16) for i in range(2)])
        den, Bden = P.tile("wden", [128, 4], F32)
        for g in range(8):
            ytok, Byt = ytoks.next()
            for hh in range(2):
                hq = 2 * g + hh
                hkv = hq // 4
                qT, BqT = qTs.next()

                def ev(tb, pt, Bp, qT=qT, BqT=BqT):
                    act(P, qT[:, tb * 512:(tb + 1) * 512], pt[0:64, :], AF.Copy, [Bp], [BqT])
                proj_fm(P, cx, None, W, BW, hq * 64, 64, ev, pst=pst.items)
                G, BG = Gs.next()
                P.dma("sync", G[:], cx.tblG[hq], writes=[BG])
                for Q in range(4):
                    acc, Bacc = accs.next()
                    plan = []
                    for delta in range(-1, 5):
                        j = 4 * Q + delta
                        if j < 0 or j > 15:
                            continue
                        qlo = max(0, delta - 1)
                        qhi = min(3, delta + 1)
                        firsts = [s for s in range(qlo, qhi + 1) if j == max(4 * Q + s - 1, 0)]
                        lasts = [s for s in range(qlo, qhi + 1) if j == min(4 * Q + s + 1, 15)]
                        c0 = 128 * (4 - delta) + qlo * 128
                        ncol = (qhi - qlo + 1) * 128
                        plan.append((j, qlo, qhi, "tbl", (G[:, c0:c0 + ncol], BG), firsts, lasts))
                    attn_block(P, cx, acc, Bacc, 65,
                               lambda qlo, ncol, qT=qT, Q=Q: qT[:, Q * 512 + qlo * 128:Q * 512 + qlo * 128 + ncol], BqT,
                               lambda j, hkv=hkv: kT[:, hkv, j * 128:(j + 1) * 128], BkT,
                               lambda j, hkv=hkv: vaug[:, j, hkv, :], Bva, plan, pss, tmps, ets)
                    ts(P, den[:], acc[:, :, 64], esk[:, hq:hq + 1], ALU.add, [Bacc, Besk], [Bden])
                    P.op("vector", lambda e: e.reciprocal(out=den[:], in_=den[:]), reads=[Bden], writes=[Bden])
                    tt(P, ytok[:, Q * 4:(Q + 1) * 4, hh * 64:(hh + 1) * 64], acc[:, :, 0:64],
                       den[:].unsqueeze(2).to_broadcast([128, 4, 64]), ALU.mult, [Bacc, Bden], [Byt])
            yT, ByT = yTs.next()
            ytok_to_dram(P, cx, ytok, Byt, yT, ByT, pst, 8 + g)


def phase_diff(P, cx, l):
    w_in = cx.w_in[l]
    lam_init = 0.8 - 0.6 * math.exp(-0.3 * l)
    with P.scope():
        lamb, Blam = P.tile("lamb", [128, 4, 64], F32)
        P.dma("sync", lamb[:], cx.diff_lambda[l].partition_broadcast(128), writes=[Blam])
        lprod, Blp = P.tile("lprod", [128, 2, 64], F32)
        tt(P, lprod[:, 0, :], lamb[:, 0, :], lamb[:, 1, :], ALU.mult, [Blam], [Blp])
        tt(P, lprod[:, 1, :], lamb[:, 2, :], lamb[:, 3, :], ALU.mult, [Blam], [Blp])
        lsum, Bls = P.tile("lsum", [128, 2], F32)
        P.op("vector", lambda e: e.tensor_reduce(out=lsum[:], in_=lprod[:], axis=AX.X, op=ALU.add), reads=[Blp], writes=[Bls])
        act(P, lsum[:], lsum[:], AF.Exp, [Bls], [Bls])
        neglam, Bnl = P.tile("neglam", [128, 1], F32)
        tt(P, neglam[:], lsum[:, 1:2], lsum[:, 0:1], ALU.subtract, [Bls], [Bnl])
        ts(P, neglam[:], neglam[:], -lam_init, ALU.add, [Bnl], [Bnl])
        gsub, Bgs = P.tile("gsub", [128, 128], F32)
        P.dma("sync", gsub[:], cx.diff_subln_g[l, :].partition_broadcast(128), writes=[Bgs])
        ts(P, gsub[:], gsub[:], 1.0 - lam_init, ALU.mult, [Bgs], [Bgs])
        farb, Bfar = P.tile("farb_sb", [128, 16], F32)
        P.dma("sync", farb[:], cx.farb, writes=[Bfar])

        pst = Rot([P.ptile(f"dps{i}", [128, 512], F32) for i in range(1)])
        pss = Rot([P.ptile(f"dss{i}", [128, 512], F32) for i in range(3)])
        acc0, Bacc0 = P.ptile("dacc0", [128, 4, 256], F32)
        acc1, Bacc1 = P.ptile("dacc1", [128, 4, 256], F32)
        tmps = Rot([P.tile(f"dtmp{i}", [128, 512], F32) for i in range(2)])
        ets = Rot([P.tile(f"det{i}", [128, 512], BF16) for i in range(3)])
        Ws = Rot([P.tile(f"dW{i}", [128, DC, 384], BF16) for i in range(2)])
        qTs = Rot([P.tile(f"dq{i}", [64, 2, T], BF16) for i in range(2)])
        kTs = Rot([P.tile(f"dk{i}", [64, 2, T], BF16) for i in range(2)])
        vas = Rot([P.tile(f"dv{i}", [128, NT, 129], BF16) for i in range(2)])
        Gs = Rot([P.tile(f"dG{i}", [128, 1152], F32) for i in range(2)])
        ytoks = Rot([P.tile(f"dytok{i}", [128, NT, 128], BF16) for i in range(2)])
        yTs = Rot([P.tile(f"dyT{i}", [128, T], BF16) for i in range(2)])
        r0, Br0 = P.tile("dr0", [128, 4], F32)
        r1, Br1 = P.tile("dr1", [128, 4], F32)
        o0, Bo0 = P.tile("do0", [128, 4, 128], F32)
        o1, Bo1 = P.tile("do1", [128, 4, 128], F32)
        sq, Bsq = P.tile("dsq", [128, 4, 128], F32)
        ss, Bss = P.tile("dss_", [128, 4], F32)
        for h in range(8):
            W, BW = Ws.next()
            load_w(P, W[:, :, 0:128], BW, w_in, O_DQ + h * 128, 128)
            load_w(P, W[:, :, 128:256], BW, w_in, O_DK + h * 128, 128)
            load_w(P, W[:, :, 256:384], BW, w_in, O_DV + h * 128, 128)
            qT, BqT = qTs.next()
            kT, BkT = kTs.next()
            for (dst, Bdst, wc) in ((qT, BqT, 0), (kT, BkT, 128)):
                def ev(tb, pt, Bp, dst=dst, Bdst=Bdst):
                    vcopy(P, dst[:, 0, tb * 512:(tb + 1) * 512], pt[0:64, :], [Bp], [Bdst])
                    act(P, dst[:, 1, tb * 512:(tb + 1) * 512], pt[64:128, :], AF.Copy, [Bp], [Bdst])
                proj_fm(P, cx, None, W, BW, wc, 128, ev, pst=pst.items)
            vaug, Bva = vas.next()
            memset(P, vaug[:, :, 128:129], 1.0, [Bva], eng="gpsimd")
            for i in range(NT):
                pt, Bp = pst.next()
                for c in range(DC):
                    mm(P, pt[:, 0:128], cx.xT[:, c, i * 128:(i + 1) * 128], W[:, c, 256:384], c == 0, c == DC - 1,
                       [cx.BxT, BW], [Bp])
                vcopy(P, vaug[:, i, 0:128], pt[:, 0:128], [Bp], [Bva])
            G, BG = Gs.next()
            P.dma("sync", G[:], cx.tblG[16 + h], writes=[BG])
            ytok, Byt = ytoks.next()
            for Q in range(4):
                for comp, (acc, Bacc) in enumerate(((acc0, Bacc0), (acc1, Bacc1))):
                    plan = []
                    for j in range(16):
                        delta = j - 4 * Q
                        fl = [0, 1, 2, 3] if j == 0 else []
                        ll = [0, 1, 2, 3] if j == 15 else []
                        if -1 <= delta <= 4:
                            c0 = 128 * (4 - delta)
                            plan.append((j, 0, 3, "tbl", (G[:, c0:c0 + 512], BG), fl, ll))
                        else:
                            ci = 2 * h + (1 if delta > 0 else 0)
                            plan.append((j, 0, 3, "far", (farb[:, ci:ci + 1], Bfar), fl, ll))
                    attn_block(P, cx, acc, Bacc, 129,
                               lambda qlo, ncol, qT=qT, Q=Q, comp=comp: qT[:, comp, Q * 512:(Q + 1) * 512], BqT,
                               lambda j, kT=kT, comp=comp: kT[:, comp, j * 128:(j + 1) * 128], BkT,
                               lambda j, vaug=vaug: vaug[:, j, :], Bva, plan, pss, tmps, ets)
                P.op("vector", lambda e: e.reciprocal(out=r0[:], in_=acc0[:, :, 128]), reads=[Bacc0], writes=[Br0])
                P.op("vector", lambda e: e.reciprocal(out=r1[:], in_=acc1[:, :, 128]), reads=[Bacc1], writes=[Br1])
                ts(P, r1[:], r1[:], neglam[:, 0:1], ALU.mult, [Br1, Bnl], [Br1])
                tt(P, o0[:], acc0[:, :, 0:128], r0[:].unsqueeze(2).to_broadcast([128, 4, 128]), ALU.mult, [Bacc0, Br0], [Bo0])
                tt(P, o1[:], acc1[:, :, 0:128], r1[:].unsqueeze(2).to_broadcast([128, 4, 128]), ALU.mult, [Bacc1, Br1], [Bo1])
                tt(P, o0[:], o0[:], o1[:], ALU.add, [Bo0, Bo1], [Bo0], eng="gpsimd")
                act(P, sq[:], o0[:], AF.Square, [Bo0], [Bsq])
                P.op("vector", lambda e: e.tensor_reduce(out=ss[:], in_=sq[:], axis=AX.X, op=ALU.add), reads=[Bsq], writes=[Bss])
                rsqrt(P, cx, ss[:], ss[:], Bss, 1.0 / 128.0, 1e-5)
                tt(P, o0[:], o0[:], ss[:].unsqueeze(2).to_broadcast([128, 4, 128]), ALU.mult, [Bo0, Bss], [Bo0])
                tt(P, ytok[:, Q * 4:(Q + 1) * 4, :], o0[:], gsub[:].unsqueeze(1).to_broadcast([128, 4, 128]), ALU.mult,
                   [Bo0, Bgs], [Byt], eng="gpsimd")
            yT, ByT = yTs.next()
            ytok_to_dram(P, cx, ytok, Byt, yT, ByT, pst, 16 + h)


def phase_mem(P, cx, l):
    w_in = cx.w_in[l]
    with P.scope():
        Wkv, BWkv = P.tile("mWkv", [128, DC, 512], BF16)
        load_w(P, Wkv[:], BWkv, cx.mem_w_kv[l], 0, 512)
        Wq, BWq = P.tile("mWq", [128, DC, 256], BF16)
        load_w(P, Wq[:], BWq, w_in, O_MEMQ, 256)
        pst = Rot([P.ptile(f"mps{i}", [128, 512], F32) for i in range(2)])
        pss = Rot([P.ptile(f"mss{i}", [128, 512], F32) for i in range(2)])
        accs = Rot([P.ptile(f"macc{i}", [128, 4, 128], F32) for i in range(2)])
        ets = Rot([P.tile(f"met{i}", [128, 512], BF16) for i in range(3)])
        kmT, Bkm = P.tile("kmT", [64, 4, 256], BF16)
        for h in range(4):
            pt, Bp = pst.next()
            for c in range(DC):
                mm(P, pt[0:64, 0:256], Wkv[:, c, h * 64:(h + 1) * 64], cx.memT[:, c, :], c == 0, c == DC - 1,
                   [BWkv, cx.BmemT], [Bp])
            vcopy(P, kmT[:, h, :], pt[0:64, 0:256], [Bp], [Bkm])
        vm, Bvm = P.tile("vmaug", [128, 2, 4, 65], BF16)
        memset(P, vm[:, :, :, 64:65], 1.0, [Bvm], eng="gpsimd")
        for mt in range(2):
            pt, Bp = pst.next()
            for c in range(DC):
                mm(P, pt[:, 0:256], cx.memT[:, c, mt * 128:(mt + 1) * 128], Wkv[:, c, 256:512], c == 0, c == DC - 1,
                   [cx.BmemT, BWkv], [Bp])
            vcopy(P, vm[:, mt, :, 0:64], pt[:, 0:256].rearrange("p (h d) -> p h d", h=4), [Bp], [Bvm])
        qTs = Rot([P.tile(f"memq{i}", [64, T], BF16) for i in range(2)])
        ytoks = Rot([P.tile(f"mytok{i}", [128, NT, 128], BF16) for i in range(2)])
        yTs = Rot([P.tile(f"myT{i}", [128, T], BF16) for i in range(2)])
        den, Bden = P.tile("mden", [128, 4], F32)
        for g in range(2):
            ytok, Byt = ytoks.next()
            for hh in range(2):
                h = 2 * g + hh
                qT, BqT = qTs.next()

                def ev(tb, pt, Bp, qT=qT, BqT=BqT):
                    act(P, qT[:, tb * 512:(tb + 1) * 512], pt[0:64, :], AF.Copy, [Bp], [BqT])
                proj_fm(P, cx, None, Wq, BWq, h * 64, 64, ev, pst=pst.items)
                for Q in range(4):
                    acc, Bacc = accs.next()
                    plan = [(mt, 0, 3, "none", None, [0, 1, 2, 3] if mt == 0 else [], [0, 1, 2, 3] if mt == 1 else [])
                            for mt in range(2)]
                    attn_block(P, cx, acc, Bacc, 65,
                               lambda qlo, ncol, qT=qT, Q=Q: qT[:, Q * 512:(Q + 1) * 512], BqT,
                               lambda j, h=h: kmT[:, h, j * 128:(j + 1) * 128], Bkm,
                               lambda j, h=h: vm[:, j, h, :], Bvm, plan, pss, None, ets)
                    P.op("vector", lambda e, acc=acc: e.reciprocal(out=den[:], in_=acc[:, :, 64]), reads=[Bacc], writes=[Bden])
                    tt(P, ytok[:, Q * 4:(Q + 1) * 4, hh * 64:(hh + 1) * 64], acc[:, :, 0:64],
                       den[:].unsqueeze(2).to_broadcast([128, 4, 64]), ALU.mult, [Bacc, Bden], [Byt])
            yT, ByT = yTs.next()
            ytok_to_dram(P, cx, ytok, Byt, yT, ByT, pst, 24 + g)


def _t5_bucket_np(rel):
    half, exact = 16, 8
    n = np.abs(rel)
    nf = np.maximum(n, 1).astype(np.float32)
    large = exact + (np.log(nf / exact) / math.log(128 / exact) * (half - exact)).astype(np.int32)
    large = np.minimum(large, half - 1)
    return np.where(rel > 0, half, 0) + np.where(n < exact, n, large)


def host_tables(rel_bias):
    p = np.arange(128)[:, None]
    c = np.arange(1152)[None, :]
    rel = p - c + 512
    idx = _t5_bucket_np(rel)
    G = np.ascontiguousarray(np.transpose(rel_bias[idx], (2, 0, 1))).astype(np.float32)
    mask = (np.abs(rel) > 128)
    G[:16][:, mask] = -30000.0
    farb = np.empty((128, 16), np.float32)
    for h in range(8):
        farb[:, 2 * h] = rel_bias[15, 16 + h]
        farb[:, 2 * h + 1] = rel_bias[31, 16 + h]
    return G, farb


def declare_inputs(nc, cx, names_shapes):
    for nm, shp in names_shapes:
        setattr(cx, nm, nc.dram_tensor(nm, list(shp), F32, kind="ExternalInput").ap())


INPUT_SHAPES = [
    ("x", (T, D)), ("mem", (256, D)), ("w_in", (NL, D, 12288)), ("rwkv_w_up", (NL, 2, 64, 1024)),
    ("rwkv_a_up", (NL, 2, 64, 1024)), ("rwkv_g_up", (NL, 128, 1024)), ("win_sink", (NL, 16)),
    ("diff_lambda", (NL, 4, 64)), ("diff_subln_g", (NL, 128)), ("mem_w_kv", (NL, D, 512)),
    ("w_branch", (NL, 3328, D)), ("w_out", (NL, D, D)), ("ln1_g", (NL, D)), ("ln1_b", (NL, D)),
    ("router", (NL, D, 16)), ("exp_w_gate", (NL, 16, D, 2048)), ("exp_w_up", (NL, 16, D, 2048)),
    ("exp_w_down", (NL, 16, 2048, D)), ("ln2_g", (NL, D)), ("ln2_b", (NL, D)),
    ("tblG", (24, 128, 1152)), ("farb", (128, 16)), ("colp", (NL, 128, 160)),
]


def build_memT(P, cx):
    with P.scope():
        xin = [P.tile(f"min{i}", [128, D], F32) for i in range(2)]
        pst = [P.ptile(f"mtp{i}", [128, 512], F32) for i in range(2)]
        k = 0
        for i in range(2):
            xt_, Bx = xin[i]
            P.dma("sync", xt_[:], cx.mem[i * 128:(i + 1) * 128, :], writes=[Bx])
            for half in range(2):
                pt, Bp = pst[k % 2]
                k += 1
                for j in range(4):
                    c = half * 4 + j
                    tr(P, pt[:, j * 128:(j + 1) * 128], xt_[:, c * 128:(c + 1) * 128], cx.ident_f[:],
                       [Bx, cx.Bident_f], [Bp])
                vcopy(P, cx.memT[:, half * 4:half * 4 + 4, i * 128:(i + 1) * 128],
                      pt[:].rearrange("p (j t) -> p j t", j=4), [Bp], [cx.BmemT])


BR_CHUNKS = [(0, 8), (8, 16), (16, 24), (24, 26)]


def layer_norm_tile(P, cx, pre, Bpre, gb, Bgb, bb, Bbb, stats, Bst, mv, Bmv, out, Bout):
    for k in range(2):
        P.op("vector", lambda e, k=k: e.bn_stats(out=stats[:, k, :], in_=pre[:, k * 512:(k + 1) * 512]),
             reads=[Bpre], writes=[Bst])
    P.op("vector", lambda e: e.bn_aggr(out=mv[:, 0:2], in_=stats[:].rearrange("p a b -> p (a b)")), reads=[Bst], writes=[Bmv])
    ts(P, mv[:, 2:3], mv[:, 1:2], 1.0, ALU.mult, [Bmv], [Bmv], s2=1e-5, op1=ALU.add)
    act(P, mv[:, 2:3], mv[:, 2:3], AF.Sqrt, [Bmv], [Bmv])
    P.op("vector", lambda e: e.reciprocal(out=mv[:, 2:3], in_=mv[:, 2:3]), reads=[Bmv], writes=[Bmv])
    ts(P, pre, pre, mv[:, 0:1], ALU.subtract, [Bpre, Bmv], [Bpre], s2=mv[:, 2:3], op1=ALU.mult)
    tt(P, pre, pre, gb, ALU.mult, [Bpre, Bgb], [Bpre], eng="gpsimd")
    tt(P, out, pre, bb, ALU.add, [Bpre, Bbb], [Bout])


def phase_merge(P, cx, l, xres_dram):
    w_in = cx.w_in[l]
    with P.scope():
        Wo, BWo = P.tile("Wo", [128, DC, 1024], BF16)
        load_w(P, Wo[:, :, 0:512], BWo, cx.w_out[l], 0, 512)
        load_w(P, Wo[:, :, 512:1024], BWo, cx.w_out[l], 512, 512)
        Rb, BRb = P.tile("Rb", [128, DC, 16], BF16)
        load_w(P, Rb[:], BRb, cx.router[l], 0, 16)
        g1, Bg1 = P.tile("g1b", [128, D], F32)
        b1, Bb1 = P.tile("b1b", [128, D], F32)
        P.dma("sync", g1[:], cx.ln1_g[l, :].partition_broadcast(128), writes=[Bg1])
        P.dma("sync", b1[:], cx.ln1_b[l, :].partition_broadcast(128), writes=[Bb1])
        ybr, Bybr = P.tile("ybr", [128, 26, 1024], BF16)
        mT, BmT = P.tile("mergedT", [128, DC, 1024], BF16)
        Wbs = Rot([P.tile(f"Wb{i}", [128, 26, 128], BF16) for i in range(2)])
        Wgs = Rot([P.tile(f"Wg{i}", [128, DC, 4, 128], BF16) for i in range(2)])
        psZ = Rot([P.ptile(f"psZ{i}", [128, 512], F32) for i in range(2)])
        psG = Rot([P.ptile(f"psG{i}", [128, 512], F32) for i in range(2)])
        psO, BpsO = P.ptile("psO", [128, 1024], F32)
        psT, BpsT = P.ptile("psT", [128, 512], F32)
        psR, BpsR = P.ptile("psR", [128, 512], F32)
        sgs = Rot([P.tile(f"sg{i}", [128, 512], F32) for i in range(2)])
        macc, Bmacc = P.tile("macc", [128, 512], F32)
        mtmp, Bmtmp = P.tile("mtmp", [128, 512], F32)
        xrs = Rot([P.tile(f"xr{i}", [128, D], F32) for i in range(2)])
        pres = Rot([P.tile(f"pre{i}", [128, D], F32) for i in range(2)])
        x1s = Rot([P.tile(f"x1o{i}", [128, D], F32) for i in range(2)])
        x1b, Bx1b = P.tile("x1b_t", [128, D], BF16)
        x1T, Bx1T = P.tile("x1T_t", [128, DC, 128], BF16)
        stats, Bst = P.tile("lnstats", [128, 2, 6], F32)
        mv, Bmv = P.tile("lnmv", [128, 4], F32)
        sm, Bsm = P.tile("rsm", [128, 4], F32)
        ex, Bex = P.tile("rex", [128, 16], F32)
        ybr_src = cx.ybrT.rearrange("(c p) t -> p c t", p=128)
        for hf in range(2):
            t0 = hf * 1024
            for c0 in range(0, 26, 13):
                P.dma("sync", ybr[:, c0:c0 + 13, :], ybr_src[:, c0:c0 + 13, t0:t0 + 1024], reads=cx.BybrT, writes=[Bybr])
            for dc in range(DC):
                Wb, BWb = Wbs.next()
                load_w(P, Wb[:], BWb, cx.w_branch[l], dc * 128, 128, nchunks=26)
                Wg, BWg = Wgs.next()
                for b in range(4):
                    load_w(P, Wg[:, :, b, :], BWg, w_in, O_GATE + b * 1024 + dc * 128, 128)
                for tb in range(2):
                    for b in range(4):
                        ca, cb = BR_CHUNKS[b]
                        pz, Bpz = psZ.next()
                        for c in range(ca, cb):
                            mm(P, pz[:], Wb[:, c, :], ybr[:, c, tb * 512:(tb + 1) * 512], c == ca, c == cb - 1, [BWb, Bybr], [Bpz])
                        pg, Bpg = psG.next()
                        for c in range(DC):
                            mm(P, pg[:], Wg[:, c, b, :], cx.xT[:, c, t0 + tb * 512:t0 + (tb + 1) * 512], c == 0, c == DC - 1,
                               [BWg, cx.BxT], [Bpg])
                        sg, Bsg = sgs.next()
                        act(P, sg[:], pg[:], AF.Sigmoid, [Bpg], [Bsg])
                        if b == 0:
                            tt(P, macc[:], pz[:], sg[:], ALU.mult, [Bpz, Bsg], [Bmacc])
                        else:
                            tt(P, mtmp[:], pz[:], sg[:], ALU.mult, [Bpz, Bsg], [Bmtmp])
                            if b < 3:
                                tt(P, macc[:], macc[:], mtmp[:], ALU.add, [Bmacc, Bmtmp], [Bmacc], eng="gpsimd")
                            else:
                                tt(P, mT[:, dc, tb * 512:(tb + 1) * 512], macc[:], mtmp[:], ALU.add, [Bmacc, Bmtmp], [BmT],
                                   eng="gpsimd")
            for i in range(8):
                ti = hf * 8 + i
                for dh in range(2):
                    for dc in range(DC):
                        mm(P, psO[:, dh * 512:(dh + 1) * 512], mT[:, dc, i * 128:(i + 1) * 128], Wo[:, dc, dh * 512:(dh + 1) * 512],
                           dc == 0, dc == DC - 1, [BmT, BWo], [BpsO])
                xr, Bxr = xrs.next()
                P.dma("sync", xr[:], xres_dram[ti * 128:(ti + 1) * 128, :], writes=[Bxr])
                pre, Bpre = pres.next()
                stt(P, pre[:], xr[:], DN_ALPHA, psO[:], ALU.mult, ALU.add, [Bxr, BpsO], [Bpre])
                x1, Bx1 = x1s.next()
                layer_norm_tile(P, cx, pre[:], Bpre, g1[:], Bg1, b1[:], Bb1, stats, Bst, mv, Bmv, x1[:], Bx1)
                P.dma("sync", cx.x1res[ti * 128:(ti + 1) * 128, :], x1[:], reads=[Bx1], writes=[cx.Bx1res])
                act(P, x1b[:], x1[:], AF.Copy, [Bx1], [Bx1b])
                ptb = psT[:].bitcast(BF16)
                for c in range(DC):
                    tr(P, ptb[:, c * 128:(c + 1) * 128], x1b[:, c * 128:(c + 1) * 128], cx.ident_b[:], [Bx1b, cx.Bident_b], [BpsT])
                vcopy(P, x1T[:], ptb[:].rearrange("p (c t) -> p c t", c=DC), [BpsT], [Bx1T])
                for c in range(DC):
                    mm(P, psR[:, 0:16], x1T[:, c, :], Rb[:, c, :], c == 0, c == DC - 1, [Bx1T, BRb], [BpsR])
                P.op("vector", lambda e: e.tensor_reduce(out=sm[:, 0:1], in_=psR[:, 0:16], axis=AX.X, op=ALU.max),
                     reads=[BpsR], writes=[Bsm])
                ts(P, sm[:, 1:2], sm[:, 0:1], -1.0, ALU.mult, [Bsm], [Bsm])
                act(P, ex[:], psR[:, 0:16], AF.Exp, [BpsR, Bsm], [Bex], bias=sm[:, 1:2], scale=1.0)
                P.op("vector", lambda e: e.tensor_reduce(out=sm[:, 2:3], in_=ex[:], axis=AX.X, op=ALU.add), reads=[Bex], writes=[Bsm])
                P.op("vector", lambda e: e.reciprocal(out=sm[:, 2:3], in_=sm[:, 2:3]), reads=[Bsm], writes=[Bsm])
                ts(P, cx.aff_tok[:, ti, :], ex[:], sm[:, 2:3], ALU.mult, [Bex, Bsm], [cx.Baff_tok])
                tr(P, psR[0:16, 128:256], cx.aff_tok[:, ti, :], cx.ident_f[:], [cx.Baff_tok, cx.Bident_f], [BpsR])
                vcopy(P, cx.affT[0:16, ti * 128:(ti + 1) * 128], psR[0:16, 128:256], [BpsR], [cx.BaffT])


def phase_moe(P, cx, l, out_dram, Bout):
    with P.scope():
        acc, Bacc = P.tile("moe_acc", [128, NT, D], F32)
        x1b, Bx1b = P.tile("moe_x1b", [128, NT, D], BF16)
        posb, Bposb = P.tile("tk_posb", [16, T], BF16)
        pgen = Rot([P.ptile(f"pgen{i}", [128, 512], F32) for i in range(2)])
        ptok, Bptok = P.tile("tk_ptok", [128, NT, 16], F32)
        Eall, BEall = P.tile("tk_E", [16, 16, 128], BF16)
        jidx, Bjidx = P.tile("tk_jidx", [128, 2], F32)
        with P.scope():
            xrs = Rot([P.tile(f"mxr{i}", [128, D], F32) for i in range(2)])
            for i in range(NT):
                xr, Bxr = xrs.next()
                P.dma("sync", xr[:], cx.x1res[i * 128:(i + 1) * 128, :], reads=[cx.Bx1res], writes=[Bxr])
                act(P, acc[:, i, :], xr[:], AF.Copy, [Bxr], [Bacc], scale=DN_ALPHA)
                vcopy(P, x1b[:, i, :], xr[:], [Bxr], [Bx1b])
            affT = cx.affT
            lo, Blo = P.tile("tk_lo", [16, 1], F32)
            hi, Bhi = P.tile("tk_hi", [16, 1], F32)
            mid, Bmid = P.tile("tk_mid", [16, 1], F32)
            cnt, Bcnt = P.tile("tk_cnt", [16, 1], F32)
            ge, Bge = P.tile("tk_ge", [16, 1], F32)
            dd, Bdd = P.tile("tk_d", [16, 1], F32)
            junk, Bjunk = P.tile("tk_junk", [16, T], F32)
            memset(P, lo[:], 0.0, [Blo])
            memset(P, hi[:], 1.0, [Bhi])
            for it in range(34):
                tt(P, mid[:], lo[:], hi[:], ALU.add, [Blo, Bhi], [Bmid])
                ts(P, mid[:], mid[:], 0.5, ALU.mult, [Bmid], [Bmid])
                ts(P, junk[:], affT[0:16, :], mid[:, 0:1], ALU.is_ge, [cx.BaffT, Bmid], [Bjunk, Bcnt], s2=0.0, op1=ALU.add, accum_out=cnt[:])
                ts(P, ge[:], cnt[:], 255.5, ALU.is_ge, [Bcnt], [Bge])
                tt(P, dd[:], mid[:], lo[:], ALU.subtract, [Bmid, Blo], [Bdd])
                stt(P, lo[:], dd[:], ge[:, 0:1], lo[:], ALU.mult, ALU.add, [Bdd, Bge, Blo], [Blo])
                tt(P, dd[:], hi[:], mid[:], ALU.subtract, [Bhi, Bmid], [Bdd])
                stt(P, hi[:], dd[:], ge[:, 0:1], mid[:], ALU.mult, ALU.add, [Bdd, Bge, Bmid], [Bhi])
            mask, Bmask = P.tile("tk_mask", [16, T], F32)
            ts(P, mask[:], affT[0:16, :], lo[:, 0:1], ALU.is_ge, [cx.BaffT, Blo], [Bmask])
            ones16, Bo16 = P.tile("tk_ones", [16, T], F32)
            memset(P, ones16[:], 1.0, [Bo16])
            posm, Bposm = P.tile("tk_posm", [16, T], F32)
            P.op("vector", lambda e: e.tensor_tensor_scan(out=posm[:], data0=ones16[:], data1=mask[:], initial=0.0, op0=ALU.mult, op1=ALU.add),
                 reads=[Bo16, Bmask], writes=[Bposm])
            tt(P, posm[:], posm[:], mask[:], ALU.mult, [Bposm, Bmask], [Bposm])
            ts(P, posm[:], posm[:], -1.0, ALU.add, [Bposm], [Bposm])
            vcopy(P, posb[:], posm[:], [Bposm], [Bposb])
            for half in range(2):
                pt, Bp = pgen.next()
                for j in range(8):
                    i = half * 8 + j
                    tr(P, pt[:, j * 16:(j + 1) * 16], posm[0:16, i * 128:(i + 1) * 128], cx.ident_f[0:16, 0:16], [Bposm, cx.Bident_f], [Bp])
                vcopy(P, ptok[:, half * 8:(half + 1) * 8, :], pt[:, 0:128].rearrange("p (j e) -> p j e", j=8), [Bp], [Bptok])
            vcopy(P, Eall[:], cx.ident_b[0:16, 0:16].unsqueeze(2).to_broadcast([16, 16, 128]), [cx.Bident_b], [BEall])
            vcopy(P, jidx[:, 0:1], cx.pidx[:], [cx.Bpidx], [Bjidx])
            ts(P, jidx[:, 1:2], cx.pidx[:], 128.0, ALU.add, [cx.Bpidx], [Bjidx])

        with P.scope():
            Sel, BSel = P.tile("Sel", [128, NT, 256], BF16)
            SelT, BSelT = P.tile("SelT", [128, 2, T], BF16)
            xeT, BxeT = P.tile("xeT", [128, DC, 256], BF16)
            hT, BhT = P.tile("hT", [128, 16, 256], BF16)
            ye, Bye = P.tile("ye_sb", [128, 2, D], BF16)
            Wgs = Rot([P.tile(f"eWg{i}", [128, DC, 512], BF16) for i in range(2)])
            Wus = Rot([P.tile(f"eWu{i}", [128, DC, 512], BF16) for i in range(2)])
            Wds = Rot([P.tile(f"eWd{i}", [128, 4, D], BF16) for i in range(2)])
            pgu = Rot([P.ptile(f"pgu{i}", [128, 512], F32) for i in range(2)])
            psY = [P.ptile(f"psY{i}", [128, D], F32) for i in range(2)]
            sgs = Rot([P.tile(f"esg{i}", [128, 256], F32) for i in range(2)])
            for e in range(16):
                tt(P, Sel[:], cx.colidx[:, 0:256].unsqueeze(1).to_broadcast([128, NT, 256]),
                   ptok[:, :, e:e + 1].to_broadcast([128, NT, 256]), ALU.is_equal, [cx.Bcolidx, Bptok], [BSel])
                for c in range(DC):
                    pt, Bp = pgen.next()
                    for i in range(NT):
                        mm(P, pt[:, 0:256], x1b[:, i, c * 128:(c + 1) * 128], Sel[:, i, :], i == 0, i == NT - 1, [Bx1b, BSel], [Bp])
                    if c % 2 == 0:
                        vcopy(P, xeT[:, c, :], pt[:, 0:256], [Bp], [BxeT])
                    else:
                        act(P, xeT[:, c, :], pt[:, 0:256], AF.Copy, [Bp], [BxeT])
                for tb in range(4):
                    pt, Bp = pgen.next()
                    mm(P, pt[:], Eall[:, e, :], posb[0:16, tb * 512:(tb + 1) * 512], True, True, [BEall, Bposb], [Bp])
                    for jt in range(2):
                        ts(P, SelT[:, jt, tb * 512:(tb + 1) * 512], pt[:], jidx[:, jt:jt + 1], ALU.is_equal, [Bp, Bjidx], [BSelT])
                for fq in range(4):
                    Wg, BWg = Wgs.next()
                    Wu, BWu = Wus.next()
                    Wd, BWd = Wds.next()
                    load_w(P, Wg[:], BWg, cx.exp_w_gate[l, e], fq * 512, 512)
                    load_w(P, Wu[:], BWu, cx.exp_w_up[l, e], fq * 512, 512)
                    load_w(P, Wd[:], BWd, cx.exp_w_down[l, e], 0, 1024, r0=fq * 512, nchunks=4)
                    for fc in range(4):
                        F = fq * 4 + fc
                        pG, BpG = pgu.next()
                        for c in range(DC):
                            mm(P, pG[:, 0:256], Wg[:, c, fc * 128:(fc + 1) * 128], xeT[:, c, :], c == 0, c == DC - 1, [BWg, BxeT], [BpG])
                        pU, BpU = pgu.next()
                        for c in range(DC):
                            mm(P, pU[:, 0:256], Wu[:, c, fc * 128:(fc + 1) * 128], xeT[:, c, :], c == 0, c == DC - 1, [BWu, BxeT], [BpU])
                        sg, Bsg = sgs.next()
                        act(P, sg[:], pG[:, 0:256], AF.Silu, [BpG], [Bsg])
                        tt(P, hT[:, F, :], pU[:, 0:256], sg[:], ALU.mult, [BpU, Bsg], [BhT])
                    for jt in range(2):
                        pY, BpY = psY[jt]
                        for dh in range(2):
                            for fc in range(4):
                                mm(P, pY[:, dh * 512:(dh + 1) * 512], hT[:, fq * 4 + fc, jt * 128:(jt + 1) * 128],
                                   Wd[:, fc, dh * 512:(dh + 1) * 512], fq == 0 and fc == 0, fq == 3 and fc == 3, [BhT, BWd], [BpY])
                for jt in range(2):
                    pY, BpY = psY[jt]
                    if jt == 0:
                        vcopy(P, ye[:, jt, :], pY[:], [BpY], [Bye])
                    else:
                        act(P, ye[:, jt, :], pY[:], AF.Copy, [BpY], [Bye])
                for i in range(NT):
                    for dh in range(2):
                        pt, Bp = pgen.next()
                        for jt in range(2):
                            mm(P, pt[:], SelT[:, jt, i * 128:(i + 1) * 128], ye[:, jt, dh * 512:(dh + 1) * 512], jt == 0, jt == 1,
                               [BSelT, Bye], [Bp])
                        stt(P, acc[:, i, dh * 512:(dh + 1) * 512], pt[:], cx.aff_tok[:, i, e:e + 1], acc[:, i, dh * 512:(dh + 1) * 512],
                            ALU.mult, ALU.add, [Bp, cx.Baff_tok, Bacc], [Bacc])
        g2, Bg2 = P.tile("g2b", [128, D], F32)
        b2, Bb2 = P.tile("b2b", [128, D], F32)
        P.dma("sync", g2[:], cx.ln2_g[l, :].partition_broadcast(128), writes=[Bg2])
        P.dma("sync", b2[:], cx.ln2_b[l, :].partition_broadcast(128), writes=[Bb2])
        stats, Bst = P.tile("ln2stats", [128, 2, 6], F32)
        mv, Bmv = P.tile("ln2mv", [128, 4], F32)
        outs = Rot([P.tile(f"x2o{i}", [128, D], F32) for i in range(2)])
        for i in range(NT):
            o, Bo = outs.next()
            layer_norm_tile(P, cx, acc[:, i, :], Bacc, g2[:], Bg2, b2[:], Bb2, stats, Bst, mv, Bmv, o[:], Bo)
            P.dma("sync", out_dram[i * 128:(i + 1) * 128, :], o[:], reads=[Bo], writes=[Bout])


CH = 64
NCH = T // CH
DBG = {"groups": 8, "stop": 99}


def project_shift(P, cx, rc, W, BW, wcol, ch, dst, Bdst, pst):
    raw, Braw = rc.raw, rc.Braw

    def ev(tb, pt, Bp):
        act(P, raw[:, tb * 512:(tb + 1) * 512], pt[:, :], AF.Copy, [Bp], [Braw])
    proj_fm(P, cx, None, W, BW, wcol, 128, ev, pst=pst)
    ts(P, dst[:, :], raw[:, :], rc.cmix[:, ch:ch + 1], ALU.mult, [Braw, rc.Bcmix], [Bdst])
    stt(P, dst[:, 1:T], raw[:, 0:T - 1], rc.colp[:, ch:ch + 1], dst[:, 1:T], ALU.mult, ALU.add, [Braw, rc.Bcolp, Bdst], [Bdst])
    stt(P, dst[:, 0:T - 1], raw[:, 1:T], rc.colp[:, 26 + ch:27 + ch], dst[:, 0:T - 1], ALU.mult, ALU.add,
        [Braw, rc.Bcolp, Bdst], [Bdst])


def phase_rwkv(P, cx, l):
    w_in = cx.w_in[l]
    rc = Ctx()
    with P.scope():
        rc.colp, rc.Bcolp = P.tile("colp", [128, 160], F32)
        P.dma("sync", rc.colp[:], cx.colp[l], writes=[rc.Bcolp])
        colp = rc.colp
        Bcolp = rc.Bcolp
        rc.cmix, rc.Bcmix = P.tile("cmix", [128, 26], F32)
        tt(P, rc.cmix[:], colp[:, 0:26], colp[:, 26:52], ALU.add, [Bcolp], [rc.Bcmix])
        ts(P, rc.cmix[:], rc.cmix[:], -1.0, ALU.mult, [rc.Bcmix], [rc.Bcmix], s2=1.0, op1=ALU.add)
        omka, Bomka = P.tile("omka", [128, 8], F32)
        ts(P, omka[:], colp[:, 92:100], -1.0, ALU.mult, [Bcolp], [Bomka], s2=1.0, op1=ALU.add)
        d64i, Bd64i = P.tile("d64i", [128, 64], I32)
        P.op("gpsimd", lambda e: e.iota(d64i[0:64, :], pattern=[[1, 64]], base=0, channel_multiplier=-1), writes=[Bd64i])
        P.op("gpsimd", lambda e: e.iota(d64i[64:128, :], pattern=[[1, 64]], base=0, channel_multiplier=-1), writes=[Bd64i])
        d64, Bd64 = P.tile("d64f", [128, 64], F32)
        vcopy(P, d64[:], d64i[:], [Bd64i], [Bd64])
        mk, Bmk = P.tile("mk4", [128, 4, 64], F32)
        ts(P, mk[:, 0, :], d64[:], 0.0, ALU.is_gt, [Bd64], [Bmk])
        ts(P, mk[:, 1, :], d64[:], 0.0, ALU.is_ge, [Bd64], [Bmk])
        ts(P, mk[:, 2, :], d64[:], 0.0, ALU.is_lt, [Bd64], [Bmk])
        ts(P, mk[:, 3, :], d64[:], 0.0, ALU.is_le, [Bd64], [Bmk])
        identblk, Bidb = P.tile("identblk", [128, 64], F32)
        ts(P, identblk[:], d64[:], 0.0, ALU.is_equal, [Bd64], [Bidb])
        maskA, BmaskA = P.tile("maskA", [128, 2, 2, 4, 64], F32)
        maskB, BmaskB = P.tile("maskB", [128, 2, 2, 64], F32)
        for z in range(2):
            st_i, in_i = (0, 1) if z == 0 else (2, 3)
            ot_i = 2 if z == 0 else 0
            for hh in range(2):
                ts(P, maskA[:, z, hh, 0, :], mk[:, st_i, :], -1.0, ALU.mult, [Bmk], [BmaskA])
                ts(P, maskA[:, z, hh, 1, :], mk[:, in_i, :], -1.0, ALU.mult, [Bmk], [BmaskA])
                vcopy(P, maskA[:, z, hh, 2, :], mk[:, st_i, :], [Bmk], [BmaskA])
                vcopy(P, maskA[:, z, hh, 3, :], mk[:, in_i, :], [Bmk], [BmaskA])
                ts(P, maskB[:, z, hh, :], mk[:, ot_i, :], -1.0, ALU.mult, [Bmk], [BmaskB])
        rst, Brst = P.tile("rst", [128, 512], F32)
        memset(P, rst[:], 1.0, [Brst])
        memset(P, rst[:].rearrange("p (n c) -> p n c", c=CH)[:, :, 0:1], 0.0, [Brst])
        wa_up, Bwa = P.tile("wa_up", [128, 2, 1024], BF16)
        for z in range(2):
            P.dma("gpsimd", wa_up[0:64, z, :], cx.rwkv_w_up[l, z], writes=[Bwa])
            P.dma("gpsimd", wa_up[64:128, z, :], cx.rwkv_a_up[l, z], writes=[Bwa])
        g_up, Bgup = P.tile("g_up", [128, 1024], BF16)
        P.dma("gpsimd", g_up[:], cx.rwkv_g_up[l], writes=[Bgup])
        rc.raw, rc.Braw = P.tile("raw", [128, T], F32)
        lin, Blin = P.tile("lin", [128, T], BF16)
        sdg, Bsdg = P.tile("sdg", [128, T], BF16)
        with P.scope():
            Wl, BWl = P.tile("Wl", [128, DC, 256], BF16)
            load_w(P, Wl[:], BWl, w_in, 3072, 256)
            pst = [P.ptile(f"lps{i}", [128, 512], F32) for i in range(2)]
            sh, Bsh = P.tile("lsh", [128, T], F32)
            project_shift(P, cx, rc, Wl, BWl, 0, 24, sh, Bsh, pst)
            act(P, lin[0:64, :], sh[0:64, :], AF.Tanh, [Bsh], [Blin])
            vcopy(P, lin[64:128, :], sh[64:128, :], [Bsh], [Blin])
            project_shift(P, cx, rc, Wl, BWl, 128, 25, sh, Bsh, pst)
            act(P, sdg[:], sh[:], AF.Sigmoid, [Bsh], [Bsdg])
        for g in range(DBG["groups"]):
            if DBG["stop"] < 1:
                break
            rwkv_group(P, cx, rc, l, g, maskA, BmaskA, maskB, BmaskB, identblk, Bidb, rst, Brst, omka, Bomka,
                       wa_up, Bwa, g_up, Bgup, lin, Blin, sdg, Bsdg)


def rwkv_group(P, cx, rc, l, g, maskA, BmaskA, maskB, BmaskB, identblk, Bidb, rst, Brst, omka, Bomka,
               wa_up, Bwa, g_up, Bgup, lin, Blin, sdg, Bsdg):
    w_in = cx.w_in[l]
    colp, Bcolp = rc.colp, rc.Bcolp
    gc = slice(g * 128, (g + 1) * 128)
    with P.scope():
        AR = [P.tile(f"AR{z}", [128, NCH, 2, CH], BF16) for z in range(2)]
        KT = [P.tile(f"KT{z}", [128, T], BF16) for z in range(2)]
        BT = [P.tile(f"BT{z}", [128, T], BF16) for z in range(2)]
        Ktok = [P.tile(f"Ktok{z}", [128, NCH, CH], BF16) for z in range(2)]
        nBtok = [P.tile(f"nBtok{z}", [128, NCH, CH], BF16) for z in range(2)]
        Vtok, BVtok = P.tile("Vtok", [128, NCH, CH], BF16)
        gamC = [P.tile(f"gamC{z}", [128, NCH], F32) for z in range(2)]
        bonus, Bbonus = P.tile("bonus", [128, T], F32)
        gate, Bgate = P.tile("gate_g", [128, T], BF16)
        with P.scope():
            Wr, BWr = P.tile("Wr", [128, DC, 3, 128], BF16)
            for j in range(3):
                load_w(P, Wr[:, :, j, :], BWr, w_in, j * 1024 + g * 128, 128)
            pst = [P.ptile(f"gps{i}", [128, 512], F32) for i in range(2)]
            psA = Rot([P.ptile(f"gpa{i}", [128, 512], F32) for i in range(3)])
            psTb = Rot([P.ptile(f"gpt{i}", [128, 512], F32) for i in range(2)])
            r_s, Br = P.tile("r_s", [128, T], F32)
            k_s, Bk = P.tile("k_s", [128, T], F32)
            v_s, Bv = P.tile("v_s", [128, T], F32)
            Wr2 = Wr[:].rearrange("p c j n -> p c (j n)")
            project_shift(P, cx, rc, Wr2, BWr, 0, g, r_s, Br, pst)
            project_shift(P, cx, rc, Wr2, BWr, 128, 8 + g, k_s, Bk, pst)
            project_shift(P, cx, rc, Wr2, BWr, 256, 16 + g, v_s, Bv, pst)
            kkc = colp[:, 84 + g:85 + g]
            kac = colp[:, 92 + g:93 + g]
            rkc = colp[:, 100 + g:101 + g]
            tmp = {}
            for nm in ("sq", "rn", "kk", "sg", "az", "cs", "incl", "e1", "e2", "t1", "kd", "kd0", "kka"):
                tmp[nm] = P.tile("g_" + nm, [128, 512], F32)
            tmp["u"] = tmp["sq"]
            tb16 = Rot([P.tile(f"g_tb16_{i}", [128, 512], BF16) for i in range(2)])
            for tb in range(4):
                sl = slice(tb * 512, (tb + 1) * 512)
                cs8 = slice(tb * 8, (tb + 1) * 8)
                sq, Bsq = tmp["sq"]
                act(P, sq[:], k_s[:, sl], AF.Square, [Bk, Bcolp], [Bsq], scale=kkc)
                pa, Bpa = psA.next()
                mm(P, pa[:], cx.blk[:], sq[:], True, True, [cx.Bblk, Bsq], [Bpa])
                rn, Brn = tmp["rn"]
                ts(P, rn[:], pa[:], 1e-12, ALU.max, [Bpa], [Brn])
                act(P, rn[:], rn[:], AF.Sqrt, [Brn], [Brn])
                P.op("vector", lambda e, rn=rn: e.reciprocal(out=rn[:], in_=rn[:]), reads=[Brn], writes=[Brn])
                kk, Bkk = tmp["kk"]
                stt(P, kk[:], k_s[:, sl], kkc, rn[:], ALU.mult, ALU.mult, [Bk, Bcolp, Brn], [Bkk])
                kd0, Bkd0 = tmp["kd0"]
                for z in range(2):
                    ARt, BAR = AR[z]
                    sg, Bsg = tmp["sg"]
                    az, Baz = tmp["az"]
                    pa, Bpa = psA.next()
                    mm(P, pa[:], wa_up[0:64, z, gc], lin[0:64, sl], True, True, [Bwa, Blin], [Bpa])
                    act(P, sg[:], pa[:], AF.Sigmoid, [Bpa, Bcolp], [Bsg], bias=colp[:, 52 + z * 8 + g:53 + z * 8 + g], scale=1.0)
                    pa, Bpa = psA.next()
                    mm(P, pa[:], wa_up[64:128, z, gc], lin[64:128, sl], True, True, [Bwa, Blin], [Bpa])
                    act(P, az[:], pa[:], AF.Sigmoid, [Bpa, Bcolp], [Baz], bias=colp[:, 68 + z * 8 + g:69 + z * 8 + g], scale=1.0)
                    cs, Bcs = tmp["cs"]
                    P.op("vector", lambda e, cs=cs, sg=sg: e.tensor_tensor_scan(out=cs[:], data0=rst[:], data1=sg[:], initial=0.0,
                                                                          op0=ALU.mult, op1=ALU.add),
                         reads=[Brst, Bsg], writes=[Bcs])
                    cs3 = cs[:].rearrange("p (n c) -> p n c", c=CH)
                    totb = cs3[:, :, CH - 1:CH].to_broadcast([128, 8, CH])
                    gC, BgC = gamC[z]
                    act(P, gC[:, cs8], cs3[:, :, CH - 1], AF.Exp, [Bcs], [BgC], scale=-C_DECAY)
                    if z == 0:
                        incl, Bincl = cs, Bcs
                    else:
                        incl, Bincl = tmp["incl"]
                        i3 = incl[:].rearrange("p (n c) -> p n c", c=CH)
                        tt(P, i3, totb, cs3, ALU.subtract, [Bcs], [Bincl])
                        tt(P, incl[:], incl[:], sg[:], ALU.add, [Bincl, Bsg], [Bincl], eng="gpsimd")
                    i3 = incl[:].rearrange("p (n c) -> p n c", c=CH)
                    e1, Be1 = tmp["e1"]
                    e2, Be2 = tmp["e2"]
                    t1, Bt1 = tmp["t1"]
                    kd, Bkd = (kd0, Bkd0) if z == 0 else tmp["kd"]
                    kka, Bkka = tmp["kka"]
                    ts(P, t1[:], az[:], kac, ALU.mult, [Baz, Bcolp, Bomka], [Bt1], s2=omka[:, g:g + 1], op1=ALU.add)
                    tt(P, kd[:], t1[:], k_s[:, sl], ALU.mult, [Bt1, Bk], [Bkd])
                    tt(P, kka[:], az[:], kk[:], ALU.mult, [Baz, Bkk], [Bkka], eng="gpsimd")
                    act(P, e1[:], incl[:], AF.Exp, [Bincl], [Be1], scale=-C_DECAY)
                    tt(P, ARt[:, cs8, 1, :], r_s[:, sl].rearrange("p (n c) -> p n c", c=CH), e1[:].rearrange("p (n c) -> p n c", c=CH),
                       ALU.mult, [Br, Be1], [BAR])
                    act(P, e2[:], incl[:], AF.Exp, [Bincl], [Be2], scale=C_DECAY)
                    tt(P, KT[z][0][:, sl], kd[:], e2[:], ALU.mult, [Bkd, Be2], [KT[z][1]])
                    tt(P, BT[z][0][:, sl], kka[:], e2[:], ALU.mult, [Bkka, Be2], [BT[z][1]], eng="gpsimd")
                    tt(P, t1[:], incl[:], sg[:], ALU.subtract, [Bincl, Bsg], [Bt1])
                    act(P, e1[:], t1[:], AF.Exp, [Bt1], [Be1], scale=-C_DECAY)
                    tt(P, ARt[:, cs8, 0, :], kk[:].rearrange("p (n c) -> p n c", c=CH), e1[:].rearrange("p (n c) -> p n c", c=CH),
                       ALU.mult, [Bkk, Be1], [BAR])
                    t13 = t1[:].rearrange("p (n c) -> p n c", c=CH)
                    tt(P, t13, totb, i3, ALU.subtract, [Bcs, Bincl], [Bt1])
                    act(P, e2[:], t1[:], AF.Exp, [Bt1], [Be2], scale=-C_DECAY)
                    for which in range(2):
                        hb, Bhb = tb16.next()
                        if which == 0:
                            tt(P, hb[:], kd[:], e2[:], ALU.mult, [Bkd, Be2], [Bhb])
                            dstt, Bdst = Ktok[z]
                        else:
                            stt(P, hb[:], kka[:], -1.0, e2[:], ALU.mult, ALU.mult, [Bkka, Be2], [Bhb])
                            dstt, Bdst = nBtok[z]
                        pt, Bp = psTb.next()
                        ptb = pt[:].bitcast(BF16)
                        for c in range(8):
                            for hh in range(2):
                                hs = slice(hh * 64, hh * 64 + 64)
                                tr(P, ptb[hs, c * CH:(c + 1) * CH], hb[hs, c * CH:(c + 1) * CH], cx.ident_b[hs, hs], [Bhb, cx.Bident_b], [Bp])
                        act(P, dstt[:, tb * 8:(tb + 1) * 8, :], ptb[:, 0:512].rearrange("p (j c) -> p j c", j=8), AF.Copy, [Bp], [Bdst])
                    if z == 1:
                        tt(P, kd[:], kd[:], kd0[:], ALU.add, [Bkd, Bkd0], [Bkd], eng="gpsimd")
                        u, Bu = tmp["u"]
                        stt(P, u[:], kd[:], rkc, r_s[:, sl], ALU.mult, ALU.mult, [Bkd, Bcolp, Br], [Bu])
                        pa, Bpa = psA.next()
                        mm(P, pa[:], cx.blk[:], u[:], True, True, [cx.Bblk, Bu], [Bpa])
                        tt(P, bonus[:, sl], pa[:], v_s[:, sl], ALU.mult, [Bpa, Bv], [Bbonus])
                hb, Bhb = tb16.next()
                vcopy(P, hb[:], v_s[:, sl], [Bv], [Bhb])
                pt, Bp = psTb.next()
                ptb = pt[:].bitcast(BF16)
                for c in range(8):
                    for hh in range(2):
                        hs = slice(hh * 64, hh * 64 + 64)
                        tr(P, ptb[hs, c * CH:(c + 1) * CH], hb[hs, c * CH:(c + 1) * CH], cx.ident_b[hs, hs], [Bhb, cx.Bident_b], [Bp])
                vcopy(P, Vtok[:, tb * 8:(tb + 1) * 8, :], ptb[:, 0:512].rearrange("p (j c) -> p j c", j=8), [Bp], [BVtok])
                pa, Bpa = psA.next()
                mm(P, pa[:], g_up[:, gc], sdg[:, sl], True, True, [Bgup, Bsdg], [Bpa])
                act(P, gate[:, sl], pa[:], AF.Copy, [Bpa], [Bgate])
        if DBG["stop"] < 2:
            return
        X, BX = P.tile("Xm", [128, NCH, 2, 4, CH], BF16)
        with P.scope():
            NT0, BNT0 = P.tile("NT0", [128, NCH, 2, CH], BF16)
            with P.scope():
                pcA = Rot([P.ptile(f"pcA{i}", [128, 2, 256], F32) for i in range(3)])
                pcB = Rot([P.ptile(f"pcB{i}", [128, 2, CH], F32) for i in range(3)])
                for i in range(NCH // 2):
                    for z in range(2):
                        pa, Bpa = pcA.next()
                        pb_, Bpb = pcB.next()
                        ARt, BAR = AR[z]
                        for j in range(2):
                            n = 2 * i + j
                            ns = slice(n * CH, (n + 1) * CH)
                            for hh in range(2):
                                hs = slice(hh * 64, hh * 64 + 64)
                                arr = ARt[hs, n, :, :].rearrange("p a c -> p (a c)")
                                mm(P, pa[hs, j, 0:128], BT[z][0][hs, ns], arr, True, True, [BT[z][1], BAR], [Bpa])
                                mm(P, pa[hs, j, 128:256], KT[z][0][hs, ns], arr, True, True, [KT[z][1], BAR], [Bpa])
                                mm(P, pb_[hs, j, :], ARt[hs, n, 0, :], BT[z][0][hs, ns], True, True, [BAR, BT[z][1]], [Bpb])
                        tt(P, X[:, 2 * i:2 * i + 2, z, :, :], pa[:].rearrange("p j (w c) -> p j w c", w=4), maskA[:, z, :, :, :], ALU.mult,
                           [Bpa, BmaskA], [BX])
                        tt(P, NT0[:, 2 * i:2 * i + 2, z, :], pb_[:], maskB[:, z, :, :], ALU.mult, [Bpb, BmaskB], [BNT0])
            if DBG["stop"] < 3:
                return
            with P.scope():
                NB = 16
                psN = P.ptile("psN", [128, NB, CH], F32)
                psNT = P.ptile("psNT", [128, NB, CH], F32)
                psI = P.ptile("psI", [128, NB, CH], F32)
                Ns = Rot([P.tile(f"Ncur{i}", [128, NB, CH], BF16) for i in range(2)])
                NTs = Rot([P.tile(f"NTcur{i}", [128, NB, CH], BF16) for i in range(2)])
                Invs = Rot([P.tile(f"Inv{i}", [128, NB, CH], BF16) for i in range(2)])
                Xm = X[:].rearrange("p n z w c -> p (n z) w c")
                NT0m = NT0[:].rearrange("p n z c -> p (n z) c")
                for bi in range(64 // NB):
                    ms = slice(bi * NB, (bi + 1) * NB)
                    Nprev = lambda m, hs, bi=bi: Xm[hs, bi * NB + m, 0, :]
                    NTprev = lambda m, hs, bi=bi: NT0m[hs, bi * NB + m, :]
                    BNprev, BNTprev = BX, BNT0
                    Inv, BInv = Invs.next()
                    tt(P, Inv[:], Xm[:, ms, 0, :], identblk[:].unsqueeze(1).to_broadcast([128, NB, CH]), ALU.add, [BX, Bidb], [BInv])
                    for lev in range(1, 6):
                        Nn, BNn = Ns.next()
                        NTn, BNTn = NTs.next()
                        for m in range(NB):
                            for hh in range(2):
                                hs = slice(hh * 64, hh * 64 + 64)
                                if lev < 5:
                                    mm(P, psN[0][hs, m, :], NTprev(m, hs), Nprev(m, hs), True, True, [BNprev, BNTprev], [psN[1]])
                                mm(P, psNT[0][hs, m, :], Nprev(m, hs), NTprev(m, hs), True, True, [BNprev, BNTprev], [psNT[1]])
                        if lev < 5:
                            act(P, Nn[:], psN[0][:], AF.Copy, [psN[1]], [BNn])
                        vcopy(P, NTn[:], psNT[0][:], [psNT[1]], [BNTn])
                        for m in range(NB):
                            for hh in range(2):
                                hs = slice(hh * 64, hh * 64 + 64)
                                mm(P, psI[0][hs, m, :], NTn[hs, m, :], Inv[hs, m, :], True, True, [BNTn, BInv], [psI[1]])
                        if lev < 5:
                            Inv2, BInv2 = Invs.next()
                            tt(P, Inv2[:], psI[0][:], Inv[:], ALU.add, [psI[1], BInv], [BInv2])
                            Inv, BInv = Inv2, BInv2
                        else:
                            tt(P, Xm[:, ms, 0, :], psI[0][:], Inv[:], ALU.add, [psI[1], BInv], [BX])
                        Nprev = lambda m, hs, Nn=Nn: Nn[hs, m, :]
                        NTprev = lambda m, hs, NTn=NTn: NTn[hs, m, :]
                        BNprev, BNTprev = BNn, BNTn
        if DBG["stop"] < 4:
            return
        Yz, BYz = P.tile("Yz", [128, 2, T], F32)
        with P.scope():
            ST, BST = P.tile("ST", [128, 2, CH], F32)
            STb, BSTb = P.tile("STb", [128, 2, CH], BF16)
            memset(P, ST[:], 0.0, [BST])
            memset(P, STb[:], 0.0, [BSTb])
            psWs = Rot([P.ptile(f"psW{i}", [128, 2, CH], F32) for i in range(2)])
            psPs = Rot([P.ptile(f"psP{i}", [128, 2, CH], F32) for i in range(2)])
            psYs = Rot([P.ptile(f"psYs{i}", [128, 2, CH], F32) for i in range(2)])
            psSs = Rot([P.ptile(f"psS{i}", [128, 2, CH], F32) for i in range(2)])
            Wsbs = Rot([P.tile(f"Wsb{i}", [128, 2, CH], BF16) for i in range(2)])
            Psbs = Rot([P.tile(f"Psb{i}", [128, 2, CH], BF16) for i in range(2)])
            H = [slice(0, 64), slice(64, 128)]
            for n in range(NCH):
                czs = [n, NCH - 1 - n]
                pW, BpW = psWs.next()
                pP, BpP = psPs.next()
                pY, BpY = psYs.next()
                pS, BpS = psSs.next()
                Wsb, BWsb = Wsbs.next()
                Psb, BPsb = Psbs.next()
                for z in range(2):
                    cz = czs[z]
                    for hs in H:
                        mm(P, pW[hs, z, :], AR[z][0][hs, cz, 0, :], STb[hs, z, :], True, False, [AR[z][1], BSTb], [BpW])
                        mm(P, pW[hs, z, :], X[hs, cz, z, 2, :], Vtok[hs, cz, :], False, True, [BX, BVtok], [BpW])
                vcopy(P, Wsb[:], pW[:], [BpW], [BWsb])
                for z in range(2):
                    cz = czs[z]
                    for hs in H:
                        mm(P, pP[hs, z, :], X[hs, cz, z, 0, :], Wsb[hs, z, :], True, True, [BX, BWsb], [BpP])
                act(P, Psb[:], pP[:], AF.Copy, [BpP], [BPsb])
                for z in range(2):
                    cz = czs[z]
                    for hs in H:
                        mm(P, pY[hs, z, :], STb[hs, z, :], AR[z][0][hs, cz, 1, :], True, False, [BSTb, AR[z][1]], [BpY])
                        mm(P, pY[hs, z, :], Vtok[hs, cz, :], X[hs, cz, z, 3, :], False, False, [BVtok, BX], [BpY])
                        mm(P, pY[hs, z, :], Psb[hs, z, :], X[hs, cz, z, 1, :], False, True, [BPsb, BX], [BpY])
                for z in range(2):
                    cz = czs[z]
                    for hs in H:
                        mm(P, pS[hs, z, :], Ktok[z][0][hs, cz, :], Vtok[hs, cz, :], True, False, [Ktok[z][1], BVtok], [BpS])
                        mm(P, pS[hs, z, :], nBtok[z][0][hs, cz, :], Psb[hs, z, :], False, True, [nBtok[z][1], BPsb], [BpS])
                for z in range(2):
                    cz = czs[z]
                    stt(P, ST[:, z, :], ST[:, z, :], gamC[z][0][:, cz:cz + 1], pS[:, z, :], ALU.mult, ALU.add,
                        [BST, gamC[z][1], BpS], [BST])
                act(P, STb[:], ST[:], AF.Copy, [BST], [BSTb])
                for z in range(2):
                    cz = czs[z]
                    if z == 0:
                        act(P, Yz[:, z, cz * CH:(cz + 1) * CH], pY[:, z, :], AF.Copy, [BpY], [BYz])
                    else:
                        vcopy(P, Yz[:, z, cz * CH:(cz + 1) * CH], pY[:, z, :], [BpY], [BYz])
        if DBG["stop"] < 5:
            return
        with P.scope():
            psA = Rot([P.ptile(f"opa{i}", [128, 512], F32) for i in range(2)])
            ysum, Bys = P.tile("ysum", [128, 512], F32)
            yc, Byc = P.tile("yc", [128, 512], F32)
            sq, Bsq = P.tile("osq", [128, 512], F32)
            rs, Brs = P.tile("ors", [128, 512], F32)
            yT, ByT = P.tile("ryT", [128, T], BF16)
            for tb in range(4):
                sl = slice(tb * 512, (tb + 1) * 512)
                tt(P, ysum[:], Yz[:, 0, sl], Yz[:, 1, sl], ALU.add, [BYz], [Bys])
                pa, Bpa = psA.next()
                mm(P, pa[:], cx.blk64[:], ysum[:], True, True, [cx.Bblk64, Bys], [Bpa])
                tt(P, yc[:], ysum[:], pa[:], ALU.subtract, [Bys, Bpa], [Byc])
                act(P, sq[:], yc[:], AF.Square, [Byc], [Bsq])
                pa, Bpa = psA.next()
                mm(P, pa[:], cx.blk64[:], sq[:], True, True, [cx.Bblk64, Bsq], [Bpa])
                ts(P, rs[:], pa[:], 64e-5, ALU.add, [Bpa], [Brs])
                act(P, rs[:], rs[:], AF.Sqrt, [Brs], [Brs])
                P.op("vector", lambda e: e.reciprocal(out=rs[:], in_=rs[:]), reads=[Brs], writes=[Brs])
                tt(P, yc[:], yc[:], rs[:], ALU.mult, [Byc, Brs], [Byc])
                ts(P, yc[:], yc[:], colp[:, 108 + g:109 + g], ALU.mult, [Byc, Bcolp], [Byc], s2=colp[:, 116 + g:117 + g], op1=ALU.add)
                tt(P, yc[:], yc[:], bonus[:, sl], ALU.add, [Byc, Bbonus], [Byc], eng="gpsimd")
                tt(P, yT[:, sl], yc[:], gate[:, sl], ALU.mult, [Byc, Bgate], [ByT])
            P.dma("sync", cx.ybrT[g * 128:(g + 1) * 128, :], yT[:], reads=[ByT], writes=[cx.BybrT[g]])


def host_colp(inputs, nl):
    out = np.zeros((nl, 128, 160), np.float32)
    for l in range(nl):
        mu = inputs["rwkv_mu"][l]
        out[l, :, 0:26] = mu[0].reshape(26, 128).T
        out[l, :, 26:52] = mu[1].reshape(26, 128).T
        for z in range(2):
            out[l, :, 52 + z * 8:60 + z * 8] = inputs["rwkv_w0"][l, z].reshape(8, 128).T
            out[l, :, 68 + z * 8:76 + z * 8] = inputs["rwkv_a0"][l, z].reshape(8, 128).T
        out[l, :, 84:92] = inputs["rwkv_k_k"][l].reshape(8, 128).T
        out[l, :, 92:100] = inputs["rwkv_k_a"][l].reshape(8, 128).T
        out[l, :, 100:108] = inputs["rwkv_r_k"][l].reshape(8, 128).T
        out[l, :, 108:116] = inputs["rwkv_gn_g"][l].reshape(8, 128).T
        out[l, :, 116:124] = inputs["rwkv_gn_b"][l].reshape(8, 128).T
    return out


def build_program(nl=NL, phases=("rwkv", "win", "diff", "mem", "merge", "moe"), dbg=False):
    nc = bass.Bass("TRN2", target_bir_lowering=False)
    cx = Ctx()
    shapes = [(nm, ((nl,) + shp[1:]) if (shp[0] == NL and nm not in ("x",)) else shp) for nm, shp in INPUT_SHAPES]
    declare_inputs(nc, cx, shapes)
    cx.y = nc.dram_tensor("y", [T, D], F32, kind="ExternalOutput").ap()
    kind = "ExternalOutput" if dbg else "Internal"
    cx.ybrT = nc.dram_tensor("ybrT", [26 * 128, T], BF16, kind=kind).ap()
    cx.x1res = nc.dram_tensor("x1res", [T, D], F32, kind=kind).ap()
    cx.xres = nc.dram_tensor("xres", [T, D], F32, kind=kind).ap()
    P = Prog(nc)
    cx.BybrT = [P.buf(f"ybr{i}") for i in range(26)]
    cx.Bx1res = P.buf("x1res")
    cx.Bxres = P.buf("xres")
    cx.By = P.buf("y")
    setup_consts(P, cx)
    cx.memT, cx.BmemT = P.tile("memT", [128, DC, 256], BF16)
    cx.affT, cx.BaffT = P.tile("affT", [16, T], F32)
    cx.aff_tok, cx.Baff_tok = P.tile("aff_tok", [128, NT, 16], F32)
    build_memT(P, cx)
    for l in range(nl):
        src = cx.x if l == 0 else cx.xres
        with P.scope():
            cx.xT, cx.BxT = P.tile("xT", [128, DC, T], BF16)
            build_xT(P, cx, src)
            if "rwkv" in phases:
                phase_rwkv(P, cx, l)
            if "win" in phases:
                phase_window(P, cx, l)
            if "diff" in phases:
                phase_diff(P, cx, l)
            if "mem" in phases:
                phase_mem(P, cx, l)
            if "merge" in phases:
                phase_merge(P, cx, l, src)
        if "moe" in phases:
            last = (l == nl - 1)
            phase_moe(P, cx, l, cx.y if last else cx.xres, cx.By if last else cx.Bxres)
    P.barrier()
    P.emit()
    return nc, P


def make_in_maps(inputs, nl=NL, cores=NCORES):
    G, farb = host_tables(np.asarray(inputs["rel_bias"], np.float32))
    colp = host_colp(inputs, nl)
    shared = {"tblG": G, "farb": farb, "colp": colp}
    for nm, shp in INPUT_SHAPES:
        if nm in ("x", "mem", "tblG", "farb", "colp"):
            continue
        a = np.asarray(inputs[nm], np.float32)
        shared[nm] = np.ascontiguousarray(a[:nl]) if shp[0] == NL else a
    maps = []
    for c in range(cores):
        m = dict(shared)
        m["x"] = np.ascontiguousarray(np.asarray(inputs["x"][c], np.float32))
        m["mem"] = np.ascontiguousarray(np.asarray(inputs["mem"][c], np.float32))
        maps.append(m)
    return maps


_CACHE = {}


def kernel(**inputs):
    if "nc" not in _CACHE:
        _CACHE["nc"] = build_program()[0]
    nc = _CACHE["nc"]
    maps = make_in_maps(inputs)
    res = run_bass_kernel_spmd(nc, maps, core_ids=list(range(NCORES)))
    out = np.stack([np.asarray(r["y"], np.float32) for r in res.results], axis=0)
    return out
```

```python
import math
from contextlib import ExitStack, contextmanager

import numpy as np
import concourse.bass as bass
import concourse.mybir as mybir
from concourse.bass_utils import run_bass_kernel_spmd

F32 = mybir.dt.float32
BF16 = mybir.dt.bfloat16
I32 = mybir.dt.int32
AF = mybir.ActivationFunctionType
ALU = mybir.AluOpType
AX = mybir.AxisListType

T = 2048
D = 1024
NT = 16
DC = 8
NL = 4
NCORES = 8
SEM_LIMIT = 30000
DN_ALPHA = (2 * NL) ** 0.25
C_DECAY = math.exp(-0.5)

O_RWKV = 0
O_WINQ = 3328
O_WINK = 4352
O_WINV = 4608
O_DQ = 4864
O_DK = 5888
O_DV = 6912
O_MEMQ = 7936
O_GATE = 8192


class Buf:
    __slots__ = ("name", "w", "r", "dsem", "dcnt")

    def __init__(self, name):
        self.name = name
        self.w = {}
        self.r = {}
        self.dsem = None
        self.dcnt = 0


class Prog:
    ENGS = ("tensor", "vector", "scalar", "gpsimd", "sync")

    def __init__(self, nc):
        self.nc = nc
        self.es = ExitStack()
        self.q = {n: [] for n in self.ENGS}
        self.esem = {}
        self.ecnt = {}
        self.waited = {n: {} for n in self.ENGS}
        self.nsem = 0
        self.pe_sems = set()
        self.semobj = {}
        self.free_dsems = []
        self.scopes = [[]]
        self.stacks = [self.es]
        self.n_inst = 0
        self.nname = 0
        self.pstate = {}
        for n in self.ENGS:
            self._new_esem(n)

    def sem(self, name):
        self.nsem += 1
        s = self.es.enter_context(self.nc.semaphore(f"{name}_{self.nsem}"))
        self.semobj[id(s)] = s
        return s

    def _new_esem(self, n):
        self.esem[n] = self.sem("e" + n)
        self.ecnt[n] = 0
        if n == "tensor":
            self.pe_sems.add(id(self.esem[n]))

    def sb(self, name, shape, dt):
        self.nname += 1
        return self.stacks[-1].enter_context(self.nc.sbuf_tensor(f"{name}_s{self.nname}", shape, dt))

    def ps(self, name, shape, dt=F32):
        self.nname += 1
        return self.stacks[-1].enter_context(self.nc.psum_tensor(f"{name}_p{self.nname}", shape, dt))

    def buf(self, name):
        b = Buf(name)
        self.scopes[-1].append(b)
        return b

    def tile(self, name, shape, dt):
        return self.sb(name, shape, dt), self.buf(name)

    def ptile(self, name, shape, dt=F32):
        return self.ps(name, shape, dt), self.buf(name)

    @contextmanager
    def scope(self):
        st = ExitStack()
        self.stacks.append(st)
        self.scopes.append([])
        try:
            yield
        finally:
            self.barrier()
            for b in self.scopes.pop():
                if b.dsem is not None:
                    self.free_dsems.append((b.dsem, b.dcnt))
                    b.dsem = None
            self.stacks.pop()
            st.close()

    def _deps(self, reads, writes, skip=()):
        deps = {}

        def add(d):
            for k, v in d.items():
                if k in skip:
                    continue
                if deps.get(k, 0) < v:
                    deps[k] = v

        for b in reads:
            add(b.w)
        for b in writes:
            add(b.w)
            add(b.r)
        return deps

    def _waits(self, eng, deps):
        out = []
        wd = self.waited[eng]
        for k, v in deps.items():
            if wd.get(k, 0) >= v:
                continue
            wd[k] = v
            out.append((self.semobj[k], v))
        return out

    def op(self, eng, fn, reads=(), writes=()):
        if self.ecnt[eng] >= SEM_LIMIT:
            self._new_esem(eng)
        skip = self.pe_sems if eng == "tensor" else ()
        waits = self._waits(eng, self._deps(reads, writes, skip))
        self.ecnt[eng] += 1
        s = self.esem[eng]
        v = self.ecnt[eng]
        k = id(s)
        self.q[eng].append((waits, fn, s, 1))
        self.n_inst += 1
        for b in reads:
            if b.r.get(k, 0) < v:
                b.r[k] = v
        for b in writes:
            b.w = {k: v}
            b.r = {}

    def _get_dsem(self, dst):
        if dst.dsem is None or dst.dcnt >= SEM_LIMIT:
            if self.free_dsems and dst.dsem is None:
                dst.dsem, dst.dcnt = self.free_dsems.pop()
                if dst.dcnt >= SEM_LIMIT:
                    dst.dsem = self.sem("d")
                    dst.dcnt = 0
            else:
                dst.dsem = self.sem("d")
                dst.dcnt = 0

    def dma(self, eng, out, in_, reads=(), writes=(), **kw):
        dst = writes[0]
        old = dst.dsem
        self._get_dsem(dst)
        skip = (id(dst.dsem),) if old is dst.dsem else ()
        waits = self._waits(eng, self._deps(reads, writes, skip))
        dst.dcnt += 16
        s, v = dst.dsem, dst.dcnt
        k = id(s)
        self.q[eng].append((waits, (lambda e, o=out, i=in_, kw=kw: e.dma_start(out=o, in_=i, **kw)), s, 16))
        self.n_inst += 1
        for b in reads:
            if b.r.get(k, 0) < v:
                b.r[k] = v
        for b in writes:
            b.w = {k: v}
            b.r = {}

    def pstart(self, B, bank, pk):
        d = self.pstate.setdefault(id(B), set())
        keys = [(bank, "l"), (bank, "h")] if pk == "f" else [(bank, pk)]
        st = not all(k in d for k in keys)
        d.update(keys)
        return st

    def preset(self, B, pk=None):
        d = self.pstate.get(id(B))
        if d is None:
            return
        if pk is None:
            d.clear()
        else:
            for k in [k for k in d if k[1] == pk]:
                d.discard(k)

    def barrier(self):
        deps = {}
        for n in self.ENGS:
            if self.ecnt[n] > 0:
                deps[id(self.esem[n])] = self.ecnt[n]
        for sc in self.scopes:
            for b in sc:
                if b.dsem is not None and b.dcnt > 0:
                    k = id(b.dsem)
                    if deps.get(k, 0) < b.dcnt:
                        deps[k] = b.dcnt
        for n in self.ENGS:
            own = id(self.esem[n])
            d = {k: v for k, v in deps.items() if k != own}
            waits = self._waits(n, d)
            if waits:
                self.q[n].append((waits, None, None, 0))

    def emit(self):
        nc = self.nc
        with nc.Block() as block:
            def mk(name):
                def body(e):
                    for waits, fn, s, n in self.q[name]:
                        for ws, wv in waits:
                            e.wait_ge(ws, wv)
                        if fn is not None:
                            fn(e).then_inc(s, n)
                return body
            block.sync(mk("sync"))
            block.tensor(mk("tensor"))
            block.vector(mk("vector"))
            block.scalar(mk("scalar"))
            block.gpsimd(mk("gpsimd"))
        self.es.close()


class Ctx:
    pass


def mm(P, out, lhsT, rhs, start, stop, reads, writes):
    P.op("tensor", lambda e: e.matmul(out, lhsT, rhs, start=start, stop=stop), reads=reads, writes=writes)


def mma(P, out, lhsT, rhs, Bps, bank, pk, reads, stop=False):
    mm(P, out, lhsT, rhs, P.pstart(Bps, bank, pk), stop, reads, [Bps])


def tr(P, out, in_, ident, reads, writes):
    P.op("tensor", lambda e: e.transpose(out, in_, ident), reads=reads, writes=writes)


def act(P, out, in_, func, reads, writes, bias=None, scale=None, accum_out=None):
    kw = {}
    if bias is not None:
        kw["bias"] = bias
    if scale is not None:
        kw["scale"] = scale
    if accum_out is not None:
        kw["accum_out"] = accum_out
    P.op("scalar", lambda e: e.activation(out=out, in_=in_, func=func, **kw), reads=reads, writes=writes)


def vcopy(P, out, in_, reads, writes, eng="vector"):
    P.op(eng, lambda e: e.tensor_copy(out=out, in_=in_), reads=reads, writes=writes)


def tt(P, out, in0, in1, op, reads, writes, eng="vector"):
    P.op(eng, lambda e: e.tensor_tensor(out=out, in0=in0, in1=in1, op=op), reads=reads, writes=writes)


def ts(P, out, in0, s1, op0, reads, writes, s2=None, op1=None, eng="vector", accum_out=None):
    kw = {}
    if op1 is not None:
        kw["op1"] = op1
    if accum_out is not None:
        kw["accum_out"] = accum_out
    P.op(eng, lambda e: e.tensor_scalar(out=out, in0=in0, scalar1=s1, scalar2=s2, op0=op0, **kw), reads=reads, writes=writes)


def stt(P, out, in0, scalar, in1, op0, op1, reads, writes):
    P.op("vector", lambda e: e.scalar_tensor_tensor(out=out, in0=in0, scalar=scalar, in1=in1, op0=op0, op1=op1),
         reads=reads, writes=writes)


def rsqrt(P, cx, out, in_, B, scale, eps):
    ts(P, out, in_, scale, ALU.mult, [B], [B], s2=eps, op1=ALU.add)
    act(P, out, out, AF.Sqrt, [B], [B])
    P.op("vector", lambda e: e.reciprocal(out=out, in_=out), reads=[B], writes=[B])


def memset(P, ap, val, writes, eng="vector"):
    P.op(eng, lambda e: e.memset(ap, val), writes=writes)


def setup_consts(P, cx):
    nc = P.nc
    dif_i, Bd = P.tile("dif_i", [128, 128], I32)
    P.op("gpsimd", lambda e: e.iota(dif_i[:], pattern=[[1, 128]], base=0, channel_multiplier=-1), writes=[Bd])
    dif, Bdf = P.tile("dif_f", [128, 128], F32)
    vcopy(P, dif[:], dif_i[:], [Bd], [Bdf])
    cx.ident_f, cx.Bident_f = P.tile("ident_f", [128, 128], F32)
    ts(P, cx.ident_f[:], dif[:], 0.0, ALU.is_equal, [Bdf], [cx.Bident_f])
    cx.ident_b, cx.Bident_b = P.tile("ident_b", [128, 128], BF16)
    vcopy(P, cx.ident_b[:], cx.ident_f[:], [cx.Bident_f], [cx.Bident_b])
    cx.dif = dif
    cx.Bdif = Bdf
    ci_i, Bci = P.tile("ci_i", [128, 256], I32)
    P.op("gpsimd", lambda e: e.iota(ci_i[:], pattern=[[1, 256]], base=0, channel_multiplier=0), writes=[Bci])
    cx.colidx, cx.Bcolidx = P.tile("colidx", [128, 256], F32)
    vcopy(P, cx.colidx[:], ci_i[:], [Bci], [cx.Bcolidx])
    pi_i, Bpi = P.tile("pi_i", [128, 1], I32)
    P.op("gpsimd", lambda e: e.iota(pi_i[:], pattern=[[0, 1]], base=0, channel_multiplier=1), writes=[Bpi])
    cx.pidx, cx.Bpidx = P.tile("pidx", [128, 1], F32)
    vcopy(P, cx.pidx[:], pi_i[:], [Bpi], [cx.Bpidx])
    cx.blk, cx.Bblk = P.tile("blk", [128, 128], F32)
    memset(P, cx.blk[:], 0.0, [cx.Bblk])
    memset(P, cx.blk[0:64, 0:64], 1.0, [cx.Bblk])
    memset(P, cx.blk[64:128, 64:128], 1.0, [cx.Bblk])
    cx.blk64, cx.Bblk64 = P.tile("blk64", [128, 128], F32)
    ts(P, cx.blk64[:], cx.blk[:], 1.0 / 64.0, ALU.mult, [cx.Bblk], [cx.Bblk64])
    cx.ones_b, cx.Bones_b = P.tile("ones_b", [128, 128], BF16)
    memset(P, cx.ones_b[:], 1.0, [cx.Bones_b])


def build_xT(P, cx, src_dram):
    with P.scope():
        xin = [P.tile(f"xin{i}", [128, D], F32) for i in range(2)]
        pst = [P.ptile(f"xtp{i}", [128, 512], F32) for i in range(2)]
        k = 0
        for i in range(NT):
            xt_, Bx = xin[i % 2]
            P.dma("sync", xt_[:], src_dram[i * 128:(i + 1) * 128, :], writes=[Bx])
            for half in range(2):
                pt, Bp = pst[k % 2]
                k += 1
                for j in range(4):
                    c = half * 4 + j
                    tr(P, pt[:, j * 128:(j + 1) * 128], xt_[:, c * 128:(c + 1) * 128], cx.ident_f[:],
                       [Bx, cx.Bident_f], [Bp])
                dst = cx.xT[:, half * 4:half * 4 + 4, i * 128:(i + 1) * 128]
                src = pt[:].rearrange("p (j t) -> p j t", j=4)
                if half == 0:
                    vcopy(P, dst, src, [Bp], [cx.BxT])
                else:
                    act(P, dst, src, AF.Copy, [Bp], [cx.BxT])


def load_w(P, dst_tile, Bdst, w2d, c0, ncols, r0=0, nchunks=DC, eng="gpsimd"):
    src = w2d[r0:r0 + nchunks * 128, c0:c0 + ncols].rearrange("(c p) n -> p c n", p=128)
    P.dma(eng, dst_tile, src, writes=[Bdst])


def proj_fm(P, cx, dst_fn, W, BW, wcol0, M, evac, nchunks=DC, rhs=None, Brhs=None, tlen=T, pst=None, extra_reads=()):
    rhs = cx.xT if rhs is None else rhs
    Brhs = cx.BxT if Brhs is None else Brhs
    nb = tlen // 512
    for tb in range(nb):
        pt, Bp = pst[tb % len(pst)]
        for c in range(nchunks):
            mm(P, pt[0:M, :], W[:, c, wcol0:wcol0 + M], rhs[:, c, tb * 512:(tb + 1) * 512], c == 0, c == nchunks - 1,
               [BW, Brhs], [Bp])
        evac(tb, pt, Bp)


class Rot:
    def __init__(self, items):
        self.items = items
        self.i = 0

    def next(self):
        it = self.items[self.i % len(self.items)]
        self.i += 1
        return it


def attn_block(P, cx, acc, Bacc, dvp, q_ap, Bq, k_ap, Bk, v_ap, Bv, plan, pss, tmps, ets, depth=2):
    started = set()
    slot_w = acc.shape[2]
    staged = []

    def stage_a(item):
        (j, qlo, qhi, kind, arg, firsts, lasts) = item
        ncol = (qhi - qlo + 1) * 128
        ps_s, Bps = pss.next()
        mm(P, ps_s[:, 0:ncol], k_ap(j), q_ap(qlo, ncol), True, True, [Bk, Bq], [Bps])
        eT, Be = ets.next()
        if kind == "tbl":
            tmp, Bt = tmps.next()
            tab, Btab = arg
            stt(P, tmp[:, 0:ncol], ps_s[:, 0:ncol], 0.125, tab, ALU.mult, ALU.add, [Bps, Btab], [Bt])
            act(P, eT[:, 0:ncol], tmp[:, 0:ncol], AF.Exp, [Bt], [Be])
        elif kind == "far":
            col, Bcol = arg
            act(P, eT[:, 0:ncol], ps_s[:, 0:ncol], AF.Exp, [Bps, Bcol], [Be], bias=col, scale=0.125)
        else:
            act(P, eT[:, 0:ncol], ps_s[:, 0:ncol], AF.Exp, [Bps], [Be], scale=0.125)
        staged.append((item, eT, Be))

    def stage_b():
        (j, qlo, qhi, kind, arg, firsts, lasts), eT, Be = staged.pop(0)
        for s in range(qlo, qhi + 1):
            bank = (s * slot_w) // 512
            st = bank not in started
            started.add(bank)
            mm(P, acc[:, s, 0:dvp], eT[:, (s - qlo) * 128:(s - qlo + 1) * 128], v_ap(j), st, s in lasts,
               [Be, Bv], [Bacc])

    n = len(plan)
    for i in range(min(depth, n)):
        stage_a(plan[i])
    for i in range(n):
        if i + depth < n:
            stage_a(plan[i + depth])
        stage_b()


def ytok_to_dram(P, cx, ytok, Bytok, yT, ByT, pst, chunk):
    for q4 in range(4):
        pt, Bp = pst.next()
        ptb = pt[:].bitcast(BF16)
        for j in range(4):
            i = q4 * 4 + j
            tr(P, ptb[:, j * 128:(j + 1) * 128], ytok[:, i, :], cx.ident_b[:], [Bytok, cx.Bident_b], [Bp])
        if q4 % 2 == 0:
            vcopy(P, yT[:, q4 * 512:(q4 + 1) * 512], ptb[:, 0:512], [Bp], [ByT])
        else:
            act(P, yT[:, q4 * 512:(q4 + 1) * 512], ptb[:, 0:512], AF.Copy, [Bp], [ByT])
    P.dma("sync", cx.ybrT[chunk * 128:(chunk + 1) * 128, :], yT[:], reads=[ByT], writes=[cx.BybrT[chunk]])


def phase_window(P, cx, l):
    w_in = cx.w_in[l]
    with P.scope():
        W, BW = P.tile("winW", [128, DC, 1536], BF16)
        for k in range(3):
            load_w(P, W[:, :, k * 512:(k + 1) * 512], BW, w_in, O_WINQ + k * 512, 512)
        pst = Rot([P.ptile(f"wps{i}", [128, 512], F32) for i in range(2)])
        pss = Rot([P.ptile(f"wss{i}", [128, 512], F32) for i in range(3)])
        accs = Rot([P.ptile(f"wacc{i}", [128, 4, 128], F32) for i in range(2)])
        tmps = Rot([P.tile(f"wtmp{i}", [128, 512], F32) for i in range(3)])
        ets = Rot([P.tile(f"wet{i}", [128, 512], BF16) for i in range(4)])
        kT, BkT = P.tile("winkT", [64, 4, T], BF16)
        for h in range(4):
            def ev(tb, pt, Bp, h=h):
                vcopy(P, kT[:, h, tb * 512:(tb + 1) * 512], pt[0:64, :], [Bp], [BkT])
            proj_fm(P, cx, None, W, BW, 1024 + h * 64, 64, ev, pst=pst.items)
        vaug, Bva = P.tile("winv", [128, NT, 4, 65], BF16)
        memset(P, vaug[:, :, :, 64:65], 1.0, [Bva], eng="gpsimd")
        for i in range(NT):
            pt, Bp = pst.next()
            for c in range(DC):
                mm(P, pt[:, 0:256], cx.xT[:, c, i * 128:(i + 1) * 128], W[:, c, 1280:1536], c == 0, c == DC - 1,
                   [cx.BxT, BW], [Bp])
            act(P, vaug[:, i, :, 0:64], pt[:, 0:256].rearrange("p (h d) -> p h d", h=4), AF.Copy, [Bp], [Bva])
        sk, Bsk = P.tile("sinkb", [128, 16], F32)
        P.dma("sync", sk[:], cx.win_sink[l, :].partition_broadcast(128), writes=[Bsk])
        esk, Besk = P.tile("esink", [128, 16], F32)
        act(P, esk[:], sk[:], AF.Exp, [Bsk], [Besk])
        qTs = Rot([P.tile(f"winq{i}", [64, T], BF16) for i in range(2)])
        Gs = Rot([P.tile(f"winG{i}", [128, 1152], F32) for i in range(2)])
        ytoks = Rot([P.tile(f"wytok{i}", [128, NT, 128], BF16) for i in range(2)])
        yTs = Rot([P.tile(f"wyT{i}", [128, T], BF16) for i in range(2)])
        den, Bden = P.tile("wden", [128, 4], F32)
        for g in range(8):
            ytok, Byt = ytoks.next()
            for hh in range(2):
                hq = 2 * g + hh
                hkv = hq // 4
                qT, BqT = qTs.next()

                def ev(tb, pt, Bp, qT=qT, BqT=BqT):
                    act(P, qT[:, tb * 512:(tb + 1) * 512], pt[0:64, :], AF.Copy, [Bp], [BqT])
                proj_fm(P, cx, None, W, BW, hq * 64, 64, ev, pst=pst.items)
                G, BG = Gs.next()
                P.dma("sync", G[:], cx.tblG[hq], writes=[BG])
                for Q in range(4):
                    acc, Bacc = accs.next()
                    plan = []
                    for delta in range(-1, 5):
                        j = 4 * Q + delta
                        if j < 0 or j > 15:
                            continue
                        qlo = max(0, delta - 1)
                        qhi = min(3, delta + 1)
                        firsts = [s for s in range(qlo, qhi + 1) if j == max(4 * Q + s - 1, 0)]
                        lasts = [s for s in range(qlo, qhi + 1) if j == min(4 * Q + s + 1, 15)]
                        c0 = 128 * (4 - delta) + qlo * 128
                        ncol = (qhi - qlo + 1) * 128
                        plan.append((j, qlo, qhi, "tbl", (G[:, c0:c0 + ncol], BG), firsts, lasts))
                    attn_block(P, cx, acc, Bacc, 65,
                               lambda qlo, ncol, qT=qT, Q=Q: qT[:, Q * 512 + qlo * 128:Q * 512 + qlo * 128 + ncol], BqT,
                               lambda j, hkv=hkv: kT[:, hkv, j * 128:(j + 1) * 128], BkT,
                               lambda j, hkv=hkv: vaug[:, j, hkv, :], Bva, plan, pss, tmps, ets)
                    ts(P, den[:], acc[:, :, 64], esk[:, hq:hq + 1], ALU.add, [Bacc, Besk], [Bden])
                    P.op("vector", lambda e: e.reciprocal(out=den[:], in_=den[:]), reads=[Bden], writes=[Bden])
                    tt(P, ytok[:, Q * 4:(Q + 1) * 4, hh * 64:(hh + 1) * 64], acc[:, :, 0:64],
                       den[:].unsqueeze(2).to_broadcast([128, 4, 64]), ALU.mult, [Bacc, Bden], [Byt])
            yT, ByT = yTs.next()
            ytok_to_dram(P, cx, ytok, Byt, yT, ByT, pst, 8 + g)


def phase_diff(P, cx, l):
    w_in = cx.w_in[l]
    lam_init = 0.8 - 0.6 * math.exp(-0.3 * l)
    with P.scope():
        lamb, Blam = P.tile("lamb", [128, 4, 64], F32)
        P.dma("sync", lamb[:], cx.diff_lambda[l].partition_broadcast(128), writes=[Blam])
        lprod, Blp = P.tile("lprod", [128, 2, 64], F32)
        tt(P, lprod[:, 0, :], lamb[:, 0, :], lamb[:, 1, :], ALU.mult, [Blam], [Blp])
        tt(P, lprod[:, 1, :], lamb[:, 2, :], lamb[:, 3, :], ALU.mult, [Blam], [Blp])
        lsum, Bls = P.tile("lsum", [128, 2], F32)
        P.op("vector", lambda e: e.tensor_reduce(out=lsum[:], in_=lprod[:], axis=AX.X, op=ALU.add), reads=[Blp], writes=[Bls])
        act(P, lsum[:], lsum[:], AF.Exp, [Bls], [Bls])
        neglam, Bnl = P.tile("neglam", [128, 1], F32)
        tt(P, neglam[:], lsum[:, 1:2], lsum[:, 0:1], ALU.subtract, [Bls], [Bnl])
        ts(P, neglam[:], neglam[:], -lam_init, ALU.add, [Bnl], [Bnl])
        gsub, Bgs = P.tile("gsub", [128, 128], F32)
        P.dma("sync", gsub[:], cx.diff_subln_g[l, :].partition_broadcast(128), writes=[Bgs])
        ts(P, gsub[:], gsub[:], 1.0 - lam_init, ALU.mult, [Bgs], [Bgs])
        farb, Bfar = P.tile("farb_sb", [128, 16], F32)
        P.dma("sync", farb[:], cx.farb, writes=[Bfar])

        pst = Rot([P.ptile(f"dps{i}", [128, 512], F32) for i in range(1)])
        pss = Rot([P.ptile(f"dss{i}", [128, 512], F32) for i in range(3)])
        acc0, Bacc0 = P.ptile("dacc0", [128, 4, 256], F32)
        acc1, Bacc1 = P.ptile("dacc1", [128, 4, 256], F32)
        tmps = Rot([P.tile(f"dtmp{i}", [128, 512], F32) for i in range(3)])
        ets = Rot([P.tile(f"det{i}", [128, 512], BF16) for i in range(4)])
        Ws = Rot([P.tile(f"dW{i}", [128, DC, 384], BF16) for i in range(2)])
        qTs = Rot([P.tile(f"dq{i}", [64, 2, T], BF16) for i in range(2)])
        kTs = Rot([P.tile(f"dk{i}", [64, 2, T], BF16) for i in range(2)])
        vas = Rot([P.tile(f"dv{i}", [128, NT, 129], BF16) for i in range(2)])
        Gs = Rot([P.tile(f"dG{i}", [128, 1152], F32) for i in range(2)])
        ytoks = Rot([P.tile(f"dytok{i}", [128, NT, 128], BF16) for i in range(2)])
        yTs = Rot([P.tile(f"dyT{i}", [128, T], BF16) for i in range(2)])
        r0, Br0 = P.tile("dr0", [128, 4], F32)
        r1, Br1 = P.tile("dr1", [128, 4], F32)
        o0, Bo0 = P.tile("do0", [128, 4, 128], F32)
        o1, Bo1 = P.tile("do1", [128, 4, 128], F32)
        sq, Bsq = P.tile("dsq", [128, 4, 128], F32)
        ss, Bss = P.tile("dss_", [128, 4], F32)
        for h in range(8):
            W, BW = Ws.next()
            load_w(P, W[:, :, 0:128], BW, w_in, O_DQ + h * 128, 128)
            load_w(P, W[:, :, 128:256], BW, w_in, O_DK + h * 128, 128)
            load_w(P, W[:, :, 256:384], BW, w_in, O_DV + h * 128, 128)
            qT, BqT = qTs.next()
            kT, BkT = kTs.next()
            for (dst, Bdst, wc) in ((qT, BqT, 0), (kT, BkT, 128)):
                def ev(tb, pt, Bp, dst=dst, Bdst=Bdst):
                    vcopy(P, dst[:, 0, tb * 512:(tb + 1) * 512], pt[0:64, :], [Bp], [Bdst])
                    act(P, dst[:, 1, tb * 512:(tb + 1) * 512], pt[64:128, :], AF.Copy, [Bp], [Bdst])
                proj_fm(P, cx, None, W, BW, wc, 128, ev, pst=pst.items)
            vaug, Bva = vas.next()
            memset(P, vaug[:, :, 128:129], 1.0, [Bva], eng="gpsimd")
            for i in range(NT):
                pt, Bp = pst.next()
                for c in range(DC):
                    mm(P, pt[:, 0:128], cx.xT[:, c, i * 128:(i + 1) * 128], W[:, c, 256:384], c == 0, c == DC - 1,
                       [cx.BxT, BW], [Bp])
                vcopy(P, vaug[:, i, 0:128], pt[:, 0:128], [Bp], [Bva])
            G, BG = Gs.next()
            P.dma("sync", G[:], cx.tblG[16 + h], writes=[BG])
            ytok, Byt = ytoks.next()
            for Q in range(4):
                for comp, (acc, Bacc) in enumerate(((acc0, Bacc0), (acc1, Bacc1))):
                    plan = []
                    for j in range(16):
                        delta = j - 4 * Q
                        fl = [0, 1, 2, 3] if j == 0 else []
                        ll = [0, 1, 2, 3] if j == 15 else []
                        if -1 <= delta <= 4:
                            c0 = 128 * (4 - delta)
                            plan.append((j, 0, 3, "tbl", (G[:, c0:c0 + 512], BG), fl, ll))
                        else:
                            ci = 2 * h + (1 if delta > 0 else 0)
                            plan.append((j, 0, 3, "far", (farb[:, ci:ci + 1], Bfar), fl, ll))
                    attn_block(P, cx, acc, Bacc, 129,
                               lambda qlo, ncol, qT=qT, Q=Q, comp=comp: qT[:, comp, Q * 512:(Q + 1) * 512], BqT,
                               lambda j, kT=kT, comp=comp: kT[:, comp, j * 128:(j + 1) * 128], BkT,
                               lambda j, vaug=vaug: vaug[:, j, :], Bva, plan, pss, tmps, ets)
                P.op("vector", lambda e: e.reciprocal(out=r0[:], in_=acc0[:, :, 128]), reads=[Bacc0], writes=[Br0])
                P.op("vector", lambda e: e.reciprocal(out=r1[:], in_=acc1[:, :, 128]), reads=[Bacc1], writes=[Br1])
                ts(P, r1[:], r1[:], neglam[:, 0:1], ALU.mult, [Br1, Bnl], [Br1])
                tt(P, o0[:], acc0[:, :, 0:128], r0[:].unsqueeze(2).to_broadcast([128, 4, 128]), ALU.mult, [Bacc0, Br0], [Bo0])
                tt(P, o1[:], acc1[:, :, 0:128], r1[:].unsqueeze(2).to_broadcast([128, 4, 128]), ALU.mult, [Bacc1, Br1], [Bo1])
                tt(P, o0[:], o0[:], o1[:], ALU.add, [Bo0, Bo1], [Bo0], eng="gpsimd")
                act(P, sq[:], o0[:], AF.Square, [Bo0], [Bsq])
                P.op("vector", lambda e: e.tensor_reduce(out=ss[:], in_=sq[:], axis=AX.X, op=ALU.add), reads=[Bsq], writes=[Bss])
                rsqrt(P, cx, ss[:], ss[:], Bss, 1.0 / 128.0, 1e-5)
                tt(P, o0[:], o0[:], ss[:].unsqueeze(2).to_broadcast([128, 4, 128]), ALU.mult, [Bo0, Bss], [Bo0])
                tt(P, ytok[:, Q * 4:(Q + 1) * 4, :], o0[:], gsub[:].unsqueeze(1).to_broadcast([128, 4, 128]), ALU.mult,
                   [Bo0, Bgs], [Byt], eng="gpsimd")
            yT, ByT = yTs.next()
            ytok_to_dram(P, cx, ytok, Byt, yT, ByT, pst, 16 + h)


def phase_mem(P, cx, l):
    w_in = cx.w_in[l]
    with P.scope():
        Wkv, BWkv = P.tile("mWkv", [128, DC, 512], BF16)
        load_w(P, Wkv[:], BWkv, cx.mem_w_kv[l], 0, 512)
        Wq, BWq = P.tile("mWq", [128, DC, 256], BF16)
        load_w(P, Wq[:], BWq, w_in, O_MEMQ, 256)
        pst = Rot([P.ptile(f"mps{i}", [128, 512], F32) for i in range(2)])
        pss = Rot([P.ptile(f"mss{i}", [128, 512], F32) for i in range(3)])
        accs = Rot([P.ptile(f"macc{i}", [128, 4, 128], F32) for i in range(2)])
        ets = Rot([P.tile(f"met{i}", [128, 512], BF16) for i in range(4)])
        kmT, Bkm = P.tile("kmT", [64, 4, 256], BF16)
        for h in range(4):
            pt, Bp = pst.next()
            for c in range(DC):
                mm(P, pt[0:64, 0:256], Wkv[:, c, h * 64:(h + 1) * 64], cx.memT[:, c, :], c == 0, c == DC - 1,
                   [BWkv, cx.BmemT], [Bp])
            vcopy(P, kmT[:, h, :], pt[0:64, 0:256], [Bp], [Bkm])
        vm, Bvm = P.tile("vmaug", [128, 2, 4, 65], BF16)
        memset(P, vm[:, :, :, 64:65], 1.0, [Bvm], eng="gpsimd")
        for mt in range(2):
            pt, Bp = pst.next()
            for c in range(DC):
                mm(P, pt[:, 0:256], cx.memT[:, c, mt * 128:(mt + 1) * 128], Wkv[:, c, 256:512], c == 0, c == DC - 1,
                   [cx.BmemT, BWkv], [Bp])
            vcopy(P, vm[:, mt, :, 0:64], pt[:, 0:256].rearrange("p (h d) -> p h d", h=4), [Bp], [Bvm])
        qTs = Rot([P.tile(f"memq{i}", [64, T], BF16) for i in range(2)])
        ytoks = Rot([P.tile(f"mytok{i}", [128, NT, 128], BF16) for i in range(2)])
        yTs = Rot([P.tile(f"myT{i}", [128, T], BF16) for i in range(2)])
        den, Bden = P.tile("mden", [128, 4], F32)
        for g in range(2):
            ytok, Byt = ytoks.next()
            for hh in range(2):
                h = 2 * g + hh
                qT, BqT = qTs.next()

                def ev(tb, pt, Bp, qT=qT, BqT=BqT):
                    act(P, qT[:, tb * 512:(tb + 1) * 512], pt[0:64, :], AF.Copy, [Bp], [BqT])
                proj_fm(P, cx, None, Wq, BWq, h * 64, 64, ev, pst=pst.items)
                for Q in range(4):
                    acc, Bacc = accs.next()
                    plan = [(mt, 0, 3, "none", None, [0, 1, 2, 3] if mt == 0 else [], [0, 1, 2, 3] if mt == 1 else [])
                            for mt in range(2)]
                    attn_block(P, cx, acc, Bacc, 65,
                               lambda qlo, ncol, qT=qT, Q=Q: qT[:, Q * 512:(Q + 1) * 512], BqT,
                               lambda j, h=h: kmT[:, h, j * 128:(j + 1) * 128], Bkm,
                               lambda j, h=h: vm[:, j, h, :], Bvm, plan, pss, None, ets)
                    P.op("vector", lambda e, acc=acc: e.reciprocal(out=den[:], in_=acc[:, :, 64]), reads=[Bacc], writes=[Bden])
                    tt(P, ytok[:, Q * 4:(Q + 1) * 4, hh * 64:(hh + 1) * 64], acc[:, :, 0:64],
                       den[:].unsqueeze(2).to_broadcast([128, 4, 64]), ALU.mult, [Bacc, Bden], [Byt])
            yT, ByT = yTs.next()
            ytok_to_dram(P, cx, ytok, Byt, yT, ByT, pst, 24 + g)


def _t5_bucket_np(rel):
    half, exact = 16, 8
    n = np.abs(rel)
    nf = np.maximum(n, 1).astype(np.float32)
    large = exact + (np.log(nf / exact) / math.log(128 / exact) * (half - exact)).astype(np.int32)
    large = np.minimum(large, half - 1)
    return np.where(rel > 0, half, 0) + np.where(n < exact, n, large)


def host_tables(rel_bias):
    p = np.arange(128)[:, None]
    c = np.arange(1152)[None, :]
    rel = p - c + 512
    idx = _t5_bucket_np(rel)
    G = np.ascontiguousarray(np.transpose(rel_bias[idx], (2, 0, 1))).astype(np.float32)
    mask = (np.abs(rel) > 128)
    G[:16][:, mask] = -30000.0
    farb = np.empty((128, 16), np.float32)
    for h in range(8):
        farb[:, 2 * h] = rel_bias[15, 16 + h]
        farb[:, 2 * h + 1] = rel_bias[31, 16 + h]
    return G, farb


def declare_inputs(nc, cx, names_shapes):
    for nm, shp in names_shapes:
        setattr(cx, nm, nc.dram_tensor(nm, list(shp), F32, kind="ExternalInput").ap())


INPUT_SHAPES = [
    ("x", (T, D)), ("mem", (256, D)), ("w_in", (NL, D, 12288)), ("rwkv_w_up", (NL, 2, 64, 1024)),
    ("rwkv_a_up", (NL, 2, 64, 1024)), ("rwkv_g_up", (NL, 128, 1024)), ("win_sink", (NL, 16)),
    ("diff_lambda", (NL, 4, 64)), ("diff_subln_g", (NL, 128)), ("mem_w_kv", (NL, D, 512)),
    ("w_branch", (NL, 3328, D)), ("w_out", (NL, D, D)), ("ln1_g", (NL, D)), ("ln1_b", (NL, D)),
    ("router", (NL, D, 16)), ("exp_w_gate", (NL, 16, D, 2048)), ("exp_w_up", (NL, 16, D, 2048)),
    ("exp_w_down", (NL, 16, 2048, D)), ("ln2_g", (NL, D)), ("ln2_b", (NL, D)),
    ("tblG", (24, 128, 1152)), ("farb", (128, 16)), ("colp", (NL, 128, 160)),
]


def build_memT(P, cx):
    with P.scope():
        xin = [P.tile(f"min{i}", [128, D], F32) for i in range(2)]
        pst = [P.ptile(f"mtp{i}", [128, 512], F32) for i in range(2)]
        k = 0
        for i in range(2):
            xt_, Bx = xin[i]
            P.dma("sync", xt_[:], cx.mem[i * 128:(i + 1) * 128, :], writes=[Bx])
            for half in range(2):
                pt, Bp = pst[k % 2]
                k += 1
                for j in range(4):
                    c = half * 4 + j
                    tr(P, pt[:, j * 128:(j + 1) * 128], xt_[:, c * 128:(c + 1) * 128], cx.ident_f[:],
                       [Bx, cx.Bident_f], [Bp])
                vcopy(P, cx.memT[:, half * 4:half * 4 + 4, i * 128:(i + 1) * 128],
                      pt[:].rearrange("p (j t) -> p j t", j=4), [Bp], [cx.BmemT])


BR_CHUNKS = [(0, 8), (8, 16), (16, 24), (24, 26)]


def layer_norm_tile(P, cx, pre, Bpre, gb, Bgb, bb, Bbb, stats, Bst, mv, Bmv, out, Bout):
    for k in range(2):
        P.op("vector", lambda e, k=k: e.bn_stats(out=stats[:, k, :], in_=pre[:, k * 512:(k + 1) * 512]),
             reads=[Bpre], writes=[Bst])
    P.op("vector", lambda e: e.bn_aggr(out=mv[:, 0:2], in_=stats[:].rearrange("p a b -> p (a b)")), reads=[Bst], writes=[Bmv])
    ts(P, mv[:, 2:3], mv[:, 1:2], 1.0, ALU.mult, [Bmv], [Bmv], s2=1e-5, op1=ALU.add)
    act(P, mv[:, 2:3], mv[:, 2:3], AF.Sqrt, [Bmv], [Bmv])
    P.op("vector", lambda e: e.reciprocal(out=mv[:, 2:3], in_=mv[:, 2:3]), reads=[Bmv], writes=[Bmv])
    ts(P, pre, pre, mv[:, 0:1], ALU.subtract, [Bpre, Bmv], [Bpre], s2=mv[:, 2:3], op1=ALU.mult)
    tt(P, pre, pre, gb, ALU.mult, [Bpre, Bgb], [Bpre], eng="gpsimd")
    tt(P, out, pre, bb, ALU.add, [Bpre, Bbb], [Bout])


def phase_merge(P, cx, l, xres_dram):
    w_in = cx.w_in[l]
    with P.scope():
        Wo, BWo = P.tile("Wo", [128, DC, 1024], BF16)
        load_w(P, Wo[:, :, 0:512], BWo, cx.w_out[l], 0, 512)
        load_w(P, Wo[:, :, 512:1024], BWo, cx.w_out[l], 512, 512)
        Rb, BRb = P.tile("Rb", [128, DC, 16], BF16)
        load_w(P, Rb[:], BRb, cx.router[l], 0, 16)
        g1, Bg1 = P.tile("g1b", [128, D], F32)
        b1, Bb1 = P.tile("b1b", [128, D], F32)
        P.dma("sync", g1[:], cx.ln1_g[l, :].partition_broadcast(128), writes=[Bg1])
        P.dma("sync", b1[:], cx.ln1_b[l, :].partition_broadcast(128), writes=[Bb1])
        ybr, Bybr = P.tile("ybr", [128, 26, 1024], BF16)
        mT, BmT = P.tile("mergedT", [128, DC, 1024], BF16)
        Wbs = Rot([P.tile(f"Wb{i}", [128, 26, 128], BF16) for i in range(2)])
        Wgs = Rot([P.tile(f"Wg{i}", [128, DC, 4, 128], BF16) for i in range(2)])
        psZ = Rot([P.ptile(f"psZ{i}", [128, 512], F32) for i in range(2)])
        psG = Rot([P.ptile(f"psG{i}", [128, 512], F32) for i in range(2)])
        psO, BpsO = P.ptile("psO", [128, 1024], F32)
        psT, BpsT = P.ptile("psT", [128, 512], F32)
        psR, BpsR = P.ptile("psR", [128, 512], F32)
        sgs = Rot([P.tile(f"sg{i}", [128, 512], F32) for i in range(2)])
        macc, Bmacc = P.tile("macc", [128, 512], F32)
        mtmp, Bmtmp = P.tile("mtmp", [128, 512], F32)
        xrs = Rot([P.tile(f"xr{i}", [128, D], F32) for i in range(2)])
        pres = Rot([P.tile(f"pre{i}", [128, D], F32) for i in range(2)])
        x1s = Rot([P.tile(f"x1o{i}", [128, D], F32) for i in range(2)])
        x1b, Bx1b = P.tile("x1b_t", [128, D], BF16)
        x1T, Bx1T = P.tile("x1T_t", [128, DC, 128], BF16)
        stats, Bst = P.tile("lnstats", [128, 2, 6], F32)
        mv, Bmv = P.tile("lnmv", [128, 4], F32)
        sm, Bsm = P.tile("rsm", [128, 4], F32)
        ex, Bex = P.tile("rex", [128, 16], F32)
        ybr_src = cx.ybrT.rearrange("(c p) t -> p c t", p=128)
        for hf in range(2):
            t0 = hf * 1024
            for c0 in range(0, 26, 13):
                P.dma("sync", ybr[:, c0:c0 + 13, :], ybr_src[:, c0:c0 + 13, t0:t0 + 1024], reads=cx.BybrT, writes=[Bybr])
            for dc in range(DC):
                Wb, BWb = Wbs.next()
                load_w(P, Wb[:], BWb, cx.w_branch[l], dc * 128, 128, nchunks=26)
                Wg, BWg = Wgs.next()
                for b in range(4):
                    load_w(P, Wg[:, :, b, :], BWg, w_in, O_GATE + b * 1024 + dc * 128, 128)
                for tb in range(2):
                    for b in range(4):
                        ca, cb = BR_CHUNKS[b]
                        pz, Bpz = psZ.next()
                        for c in range(ca, cb):
                            mm(P, pz[:], Wb[:, c, :], ybr[:, c, tb * 512:(tb + 1) * 512], c == ca, c == cb - 1, [BWb, Bybr], [Bpz])
                        pg, Bpg = psG.next()
                        for c in range(DC):
                            mm(P, pg[:], Wg[:, c, b, :], cx.xT[:, c, t0 + tb * 512:t0 + (tb + 1) * 512], c == 0, c == DC - 1,
                               [BWg, cx.BxT], [Bpg])
                        sg, Bsg = sgs.next()
                        act(P, sg[:], pg[:], AF.Sigmoid, [Bpg], [Bsg])
                        if b == 0:
                            tt(P, macc[:], pz[:], sg[:], ALU.mult, [Bpz, Bsg], [Bmacc])
                        else:
                            tt(P, mtmp[:], pz[:], sg[:], ALU.mult, [Bpz, Bsg], [Bmtmp])
                            if b < 3:
                                tt(P, macc[:], macc[:], mtmp[:], ALU.add, [Bmacc, Bmtmp], [Bmacc], eng="gpsimd")
                            else:
                                tt(P, mT[:, dc, tb * 512:(tb + 1) * 512], macc[:], mtmp[:], ALU.add, [Bmacc, Bmtmp], [BmT],
                                   eng="gpsimd")
            for i in range(8):
                ti = hf * 8 + i
                for dh in range(2):
                    for dc in range(DC):
                        mm(P, psO[:, dh * 512:(dh + 1) * 512], mT[:, dc, i * 128:(i + 1) * 128], Wo[:, dc, dh * 512:(dh + 1) * 512],
                           dc == 0, dc == DC - 1, [BmT, BWo], [BpsO])
                xr, Bxr = xrs.next()
                P.dma("sync", xr[:], xres_dram[ti * 128:(ti + 1) * 128, :], writes=[Bxr])
                pre, Bpre = pres.next()
                stt(P, pre[:], xr[:], DN_ALPHA, psO[:], ALU.mult, ALU.add, [Bxr, BpsO], [Bpre])
                x1, Bx1 = x1s.next()
                layer_norm_tile(P, cx, pre[:], Bpre, g1[:], Bg1, b1[:], Bb1, stats, Bst, mv, Bmv, x1[:], Bx1)
                P.dma("sync", cx.x1res[ti * 128:(ti + 1) * 128, :], x1[:], reads=[Bx1], writes=[cx.Bx1res])
                act(P, x1b[:], x1[:], AF.Copy, [Bx1], [Bx1b])
                ptb = psT[:].bitcast(BF16)
                for c in range(DC):
                    tr(P, ptb[:, c * 128:(c + 1) * 128], x1b[:, c * 128:(c + 1) * 128], cx.ident_b[:], [Bx1b, cx.Bident_b], [BpsT])
                vcopy(P, x1T[:], ptb[:].rearrange("p (c t) -> p c t", c=DC), [BpsT], [Bx1T])
                for c in range(DC):
                    mm(P, psR[:, 0:16], x1T[:, c, :], Rb[:, c, :], c == 0, c == DC - 1, [Bx1T, BRb], [BpsR])
                P.op("vector", lambda e: e.tensor_reduce(out=sm[:, 0:1], in_=psR[:, 0:16], axis=AX.X, op=ALU.max),
                     reads=[BpsR], writes=[Bsm])
                ts(P, sm[:, 1:2], sm[:, 0:1], -1.0, ALU.mult, [Bsm], [Bsm])
                act(P, ex[:], psR[:, 0:16], AF.Exp, [BpsR, Bsm], [Bex], bias=sm[:, 1:2], scale=1.0)
                P.op("vector", lambda e: e.tensor_reduce(out=sm[:, 2:3], in_=ex[:], axis=AX.X, op=ALU.add), reads=[Bex], writes=[Bsm])
                P.op("vector", lambda e: e.reciprocal(out=sm[:, 2:3], in_=sm[:, 2:3]), reads=[Bsm], writes=[Bsm])
                ts(P, cx.aff_tok[:, ti, :], ex[:], sm[:, 2:3], ALU.mult, [Bex, Bsm], [cx.Baff_tok])
                tr(P, psR[0:16, 128:256], cx.aff_tok[:, ti, :], cx.ident_f[:], [cx.Baff_tok, cx.Bident_f], [BpsR])
                vcopy(P, cx.affT[0:16, ti * 128:(ti + 1) * 128], psR[0:16, 128:256], [BpsR], [cx.BaffT])


def phase_moe(P, cx, l, out_dram, Bout):
    with P.scope():
        acc, Bacc = P.tile("moe_acc", [128, NT, D], F32)
        x1b, Bx1b = P.tile("moe_x1b", [128, NT, D], BF16)
        posb, Bposb = P.tile("tk_posb", [16, T], BF16)
        pgen = Rot([P.ptile(f"pgen{i}", [128, 512], F32) for i in range(2)])
        ptok, Bptok = P.tile("tk_ptok", [128, NT, 16], F32)
        Eall, BEall = P.tile("tk_E", [16, 16, 128], BF16)
        jidx, Bjidx = P.tile("tk_jidx", [128, 2], F32)
        with P.scope():
            xrs = Rot([P.tile(f"mxr{i}", [128, D], F32) for i in range(2)])
            for i in range(NT):
                xr, Bxr = xrs.next()
                P.dma("sync", xr[:], cx.x1res[i * 128:(i + 1) * 128, :], reads=[cx.Bx1res], writes=[Bxr])
                act(P, acc[:, i, :], xr[:], AF.Copy, [Bxr], [Bacc], scale=DN_ALPHA)
                vcopy(P, x1b[:, i, :], xr[:], [Bxr], [Bx1b])
            affT = cx.affT
            lo, Blo = P.tile("tk_lo", [16, 1], F32)
            hi, Bhi = P.tile("tk_hi", [16, 1], F32)
            mid, Bmid = P.tile("tk_mid", [16, 1], F32)
            cnt, Bcnt = P.tile("tk_cnt", [16, 1], F32)
            ge, Bge = P.tile("tk_ge", [16, 1], F32)
            dd, Bdd = P.tile("tk_d", [16, 1], F32)
            junk, Bjunk = P.tile("tk_junk", [16, T], F32)
            memset(P, lo[:], 0.0, [Blo])
            memset(P, hi[:], 1.0, [Bhi])
            for it in range(34):
                tt(P, mid[:], lo[:], hi[:], ALU.add, [Blo, Bhi], [Bmid])
                ts(P, mid[:], mid[:], 0.5, ALU.mult, [Bmid], [Bmid])
                ts(P, junk[:], affT[0:16, :], mid[:, 0:1], ALU.is_ge, [cx.BaffT, Bmid], [Bjunk, Bcnt], s2=0.0, op1=ALU.add, accum_out=cnt[:])
                ts(P, ge[:], cnt[:], 255.5, ALU.is_ge, [Bcnt], [Bge])
                tt(P, dd[:], mid[:], lo[:], ALU.subtract, [Bmid, Blo], [Bdd])
                stt(P, lo[:], dd[:], ge[:, 0:1], lo[:], ALU.mult, ALU.add, [Bdd, Bge, Blo], [Blo])
                tt(P, dd[:], hi[:], mid[:], ALU.subtract, [Bhi, Bmid], [Bdd])
                stt(P, hi[:], dd[:], ge[:, 0:1], mid[:], ALU.mult, ALU.add, [Bdd, Bge, Bmid], [Bhi])
            mask, Bmask = P.tile("tk_mask", [16, T], F32)
            ts(P, mask[:], affT[0:16, :], lo[:, 0:1], ALU.is_ge, [cx.BaffT, Blo], [Bmask])
            ones16, Bo16 = P.tile("tk_ones", [16, T], F32)
            memset(P, ones16[:], 1.0, [Bo16])
            posm, Bposm = P.tile("tk_posm", [16, T], F32)
            P.op("vector", lambda e: e.tensor_tensor_scan(out=posm[:], data0=ones16[:], data1=mask[:], initial=0.0, op0=ALU.mult, op1=ALU.add),
                 reads=[Bo16, Bmask], writes=[Bposm])
            tt(P, posm[:], posm[:], mask[:], ALU.mult, [Bposm, Bmask], [Bposm])
            ts(P, posm[:], posm[:], -1.0, ALU.add, [Bposm], [Bposm])
            vcopy(P, posb[:], posm[:], [Bposm], [Bposb])
            for half in range(2):
                pt, Bp = pgen.next()
                for j in range(8):
                    i = half * 8 + j
                    tr(P, pt[:, j * 16:(j + 1) * 16], posm[0:16, i * 128:(i + 1) * 128], cx.ident_f[0:16, 0:16], [Bposm, cx.Bident_f], [Bp])
                vcopy(P, ptok[:, half * 8:(half + 1) * 8, :], pt[:, 0:128].rearrange("p (j e) -> p j e", j=8), [Bp], [Bptok])
            vcopy(P, Eall[:], cx.ident_b[0:16, 0:16].unsqueeze(2).to_broadcast([16, 16, 128]), [cx.Bident_b], [BEall])
            vcopy(P, jidx[:, 0:1], cx.pidx[:], [cx.Bpidx], [Bjidx])
            ts(P, jidx[:, 1:2], cx.pidx[:], 128.0, ALU.add, [cx.Bpidx], [Bjidx])

        with P.scope():
            Sel, BSel = P.tile("Sel", [128, NT, 256], BF16)
            SelT, BSelT = P.tile("SelT", [128, 2, T], BF16)
            xeT, BxeT = P.tile("xeT", [128, DC, 256], BF16)
            hT, BhT = P.tile("hT", [128, 16, 256], BF16)
            ye, Bye = P.tile("ye_sb", [128, 2, D], BF16)
            Wgs = Rot([P.tile(f"eWg{i}", [128, DC, 512], BF16) for i in range(2)])
            Wus = Rot([P.tile(f"eWu{i}", [128, DC, 512], BF16) for i in range(2)])
            Wds = Rot([P.tile(f"eWd{i}", [128, 4, D], BF16) for i in range(2)])
            pgu = Rot([P.ptile(f"pgu{i}", [128, 512], F32) for i in range(2)])
            psY = [[P.ptile(f"psY{i}_{j}", [128, 512], F32) for j in range(2)] for i in range(2)]
            pscat = Rot(pgen.items + [psY[0][0], psY[0][1], psY[1][0], psY[1][1]])
            sgs = Rot([P.tile(f"esg{i}", [128, 256], F32) for i in range(2)])
            for e in range(16):
                tt(P, Sel[:], cx.colidx[:, 0:256].unsqueeze(1).to_broadcast([128, NT, 256]),
                   ptok[:, :, e:e + 1].to_broadcast([128, NT, 256]), ALU.is_equal, [cx.Bcolidx, Bptok], [BSel])
                for c in range(DC):
                    pt, Bp = pgen.next()
                    for i in range(NT):
                        mm(P, pt[:, 0:256], x1b[:, i, c * 128:(c + 1) * 128], Sel[:, i, :], i == 0, i == NT - 1, [Bx1b, BSel], [Bp])
                    if c % 2 == 0:
                        vcopy(P, xeT[:, c, :], pt[:, 0:256], [Bp], [BxeT])
                    else:
                        act(P, xeT[:, c, :], pt[:, 0:256], AF.Copy, [Bp], [BxeT])
                for tb in range(4):
                    pt, Bp = pgen.next()
                    mm(P, pt[:], Eall[:, e, :], posb[0:16, tb * 512:(tb + 1) * 512], True, True, [BEall, Bposb], [Bp])
                    for jt in range(2):
                        ts(P, SelT[:, jt, tb * 512:(tb + 1) * 512], pt[:], jidx[:, jt:jt + 1], ALU.is_equal, [Bp, Bjidx], [BSelT])
                for fq in range(4):
                    Wg, BWg = Wgs.next()
                    Wu, BWu = Wus.next()
                    Wd, BWd = Wds.next()
                    load_w(P, Wg[:], BWg, cx.exp_w_gate[l, e], fq * 512, 512)
                    load_w(P, Wu[:], BWu, cx.exp_w_up[l, e], fq * 512, 512)
                    load_w(P, Wd[:], BWd, cx.exp_w_down[l, e], 0, 1024, r0=fq * 512, nchunks=4)
                    for fc in range(4):
                        F = fq * 4 + fc
                        pG, BpG = pgu.next()
                        for c in range(DC):
                            mm(P, pG[:, 0:256], Wg[:, c, fc * 128:(fc + 1) * 128], xeT[:, c, :], c == 0, c == DC - 1, [BWg, BxeT], [BpG])
                        pU, BpU = pgu.next()
                        for c in range(DC):
                            mm(P, pU[:, 0:256], Wu[:, c, fc * 128:(fc + 1) * 128], xeT[:, c, :], c == 0, c == DC - 1, [BWu, BxeT], [BpU])
                        sg, Bsg = sgs.next()
                        act(P, sg[:], pG[:, 0:256], AF.Silu, [BpG], [Bsg])
                        tt(P, hT[:, F, :], pU[:, 0:256], sg[:], ALU.mult, [BpU, Bsg], [BhT])
                    for jt in range(2):
                        for dh in range(2):
                            pY, BpY = psY[jt][dh]
                            for fc in range(4):
                                mm(P, pY[:], hT[:, fq * 4 + fc, jt * 128:(jt + 1) * 128],
                                   Wd[:, fc, dh * 512:(dh + 1) * 512], fq == 0 and fc == 0, fq == 3 and fc == 3, [BhT, BWd], [BpY])
                for jt in range(2):
                    for dh in range(2):
                        pY, BpY = psY[jt][dh]
                        if dh == 0:
                            vcopy(P, ye[:, jt, dh * 512:(dh + 1) * 512], pY[:], [BpY], [Bye])
                        else:
                            act(P, ye[:, jt, dh * 512:(dh + 1) * 512], pY[:], AF.Copy, [BpY], [Bye])
                for i in range(NT):
                    for dh in range(2):
                        pt, Bp = pscat.next()
                        for jt in range(2):
                            mm(P, pt[:], SelT[:, jt, i * 128:(i + 1) * 128], ye[:, jt, dh * 512:(dh + 1) * 512], jt == 0, jt == 1,
                               [BSelT, Bye], [Bp])
                        stt(P, acc[:, i, dh * 512:(dh + 1) * 512], pt[:], cx.aff_tok[:, i, e:e + 1], acc[:, i, dh * 512:(dh + 1) * 512],
                            ALU.mult, ALU.add, [Bp, cx.Baff_tok, Bacc], [Bacc])
        g2, Bg2 = P.tile("g2b", [128, D], F32)
        b2, Bb2 = P.tile("b2b", [128, D], F32)
        P.dma("sync", g2[:], cx.ln2_g[l, :].partition_broadcast(128), writes=[Bg2])
        P.dma("sync", b2[:], cx.ln2_b[l, :].partition_broadcast(128), writes=[Bb2])
        stats, Bst = P.tile("ln2stats", [128, 2, 6], F32)
        mv, Bmv = P.tile("ln2mv", [128, 4], F32)
        outs = Rot([P.tile(f"x2o{i}", [128, D], F32) for i in range(2)])
        for i in range(NT):
            o, Bo = outs.next()
            layer_norm_tile(P, cx, acc[:, i, :], Bacc, g2[:], Bg2, b2[:], Bb2, stats, Bst, mv, Bmv, o[:], Bo)
            P.dma("sync", out_dram[i * 128:(i + 1) * 128, :], o[:], reads=[Bo], writes=[Bout])


CH = 64
NCH = T // CH
DBG = {"groups": 8, "stop": 99}


def project_shift(P, cx, rc, W, BW, wcol, ch, dst, Bdst, pst):
    raw, Braw = rc.raw, rc.Braw

    def ev(tb, pt, Bp):
        act(P, raw[:, tb * 512:(tb + 1) * 512], pt[:, :], AF.Copy, [Bp], [Braw])
    proj_fm(P, cx, None, W, BW, wcol, 128, ev, pst=pst)
    ts(P, dst[:, :], raw[:, :], rc.cmix[:, ch:ch + 1], ALU.mult, [Braw, rc.Bcmix], [Bdst])
    stt(P, dst[:, 1:T], raw[:, 0:T - 1], rc.colp[:, ch:ch + 1], dst[:, 1:T], ALU.mult, ALU.add, [Braw, rc.Bcolp, Bdst], [Bdst])
    stt(P, dst[:, 0:T - 1], raw[:, 1:T], rc.colp[:, 26 + ch:27 + ch], dst[:, 0:T - 1], ALU.mult, ALU.add,
        [Braw, rc.Bcolp, Bdst], [Bdst])


def phase_rwkv(P, cx, l):
    w_in = cx.w_in[l]
    rc = Ctx()
    with P.scope():
        rc.colp, rc.Bcolp = P.tile("colp", [128, 160], F32)
        P.dma("sync", rc.colp[:], cx.colp[l], writes=[rc.Bcolp])
        colp = rc.colp
        Bcolp = rc.Bcolp
        rc.cmix, rc.Bcmix = P.tile("cmix", [128, 26], F32)
        tt(P, rc.cmix[:], colp[:, 0:26], colp[:, 26:52], ALU.add, [Bcolp], [rc.Bcmix])
        ts(P, rc.cmix[:], rc.cmix[:], -1.0, ALU.mult, [rc.Bcmix], [rc.Bcmix], s2=1.0, op1=ALU.add)
        omka, Bomka = P.tile("omka", [128, 8], F32)
        ts(P, omka[:], colp[:, 92:100], -1.0, ALU.mult, [Bcolp], [Bomka], s2=1.0, op1=ALU.add)
        d64i, Bd64i = P.tile("d64i", [128, 64], I32)
        P.op("gpsimd", lambda e: e.iota(d64i[0:64, :], pattern=[[1, 64]], base=0, channel_multiplier=-1), writes=[Bd64i])
        P.op("gpsimd", lambda e: e.iota(d64i[64:128, :], pattern=[[1, 64]], base=0, channel_multiplier=-1), writes=[Bd64i])
        d64, Bd64 = P.tile("d64f", [128, 64], F32)
        vcopy(P, d64[:], d64i[:], [Bd64i], [Bd64])
        mk, Bmk = P.tile("mk4", [128, 4, 64], F32)
        ts(P, mk[:, 0, :], d64[:], 0.0, ALU.is_gt, [Bd64], [Bmk])
        ts(P, mk[:, 1, :], d64[:], 0.0, ALU.is_ge, [Bd64], [Bmk])
        ts(P, mk[:, 2, :], d64[:], 0.0, ALU.is_lt, [Bd64], [Bmk])
        ts(P, mk[:, 3, :], d64[:], 0.0, ALU.is_le, [Bd64], [Bmk])
        identblk, Bidb = P.tile("identblk", [128, 64], F32)
        ts(P, identblk[:], d64[:], 0.0, ALU.is_equal, [Bd64], [Bidb])
        maskA, BmaskA = P.tile("maskA", [128, 2, 2, 4, 64], F32)
        maskB, BmaskB = P.tile("maskB", [128, 2, 2, 64], F32)
        for z in range(2):
            st_i, in_i = (0, 1) if z == 0 else (2, 3)
            ot_i = 2 if z == 0 else 0
            for hh in range(2):
                ts(P, maskA[:, z, hh, 0, :], mk[:, st_i, :], -1.0, ALU.mult, [Bmk], [BmaskA])
                ts(P, maskA[:, z, hh, 1, :], mk[:, in_i, :], -1.0, ALU.mult, [Bmk], [BmaskA])
                vcopy(P, maskA[:, z, hh, 2, :], mk[:, st_i, :], [Bmk], [BmaskA])
                vcopy(P, maskA[:, z, hh, 3, :], mk[:, in_i, :], [Bmk], [BmaskA])
                ts(P, maskB[:, z, hh, :], mk[:, ot_i, :], -1.0, ALU.mult, [Bmk], [BmaskB])
        rst, Brst = P.tile("rst", [128, 512], F32)
        memset(P, rst[:], 1.0, [Brst])
        memset(P, rst[:].rearrange("p (n c) -> p n c", c=CH)[:, :, 0:1], 0.0, [Brst])
        wa_up, Bwa = P.tile("wa_up", [128, 2, 1024], BF16)
        for z in range(2):
            P.dma("gpsimd", wa_up[0:64, z, :], cx.rwkv_w_up[l, z], writes=[Bwa])
            P.dma("gpsimd", wa_up[64:128, z, :], cx.rwkv_a_up[l, z], writes=[Bwa])
        g_up, Bgup = P.tile("g_up", [128, 1024], BF16)
        P.dma("gpsimd", g_up[:], cx.rwkv_g_up[l], writes=[Bgup])
        rc.raw, rc.Braw = P.tile("raw", [128, T], F32)
        lin, Blin = P.tile("lin", [128, T], BF16)
        sdg, Bsdg = P.tile("sdg", [128, T], BF16)
        with P.scope():
            Wl, BWl = P.tile("Wl", [128, DC, 256], BF16)
            load_w(P, Wl[:], BWl, w_in, 3072, 256)
            pst = [P.ptile(f"lps{i}", [128, 512], F32) for i in range(2)]
            sh, Bsh = P.tile("lsh", [128, T], F32)
            project_shift(P, cx, rc, Wl, BWl, 0, 24, sh, Bsh, pst)
            act(P, lin[0:64, :], sh[0:64, :], AF.Tanh, [Bsh], [Blin])
            vcopy(P, lin[64:128, :], sh[64:128, :], [Bsh], [Blin])
            project_shift(P, cx, rc, Wl, BWl, 128, 25, sh, Bsh, pst)
            act(P, sdg[:], sh[:], AF.Sigmoid, [Bsh], [Bsdg])
        for g in range(DBG["groups"]):
            if DBG["stop"] < 1:
                break
            rwkv_group(P, cx, rc, l, g, maskA, BmaskA, maskB, BmaskB, identblk, Bidb, rst, Brst, omka, Bomka,
                       wa_up, Bwa, g_up, Bgup, lin, Blin, sdg, Bsdg)


def rwkv_group(P, cx, rc, l, g, maskA, BmaskA, maskB, BmaskB, identblk, Bidb, rst, Brst, omka, Bomka,
               wa_up, Bwa, g_up, Bgup, lin, Blin, sdg, Bsdg):
    w_in = cx.w_in[l]
    colp, Bcolp = rc.colp, rc.Bcolp
    gc = slice(g * 128, (g + 1) * 128)
    with P.scope():
        AR = [P.tile(f"AR{z}", [128, NCH, 2, CH], BF16) for z in range(2)]
        KT = [P.tile(f"KT{z}", [128, T], BF16) for z in range(2)]
        BT = [P.tile(f"BT{z}", [128, T], BF16) for z in range(2)]
        Ktok = [P.tile(f"Ktok{z}", [128, NCH, CH], BF16) for z in range(2)]
        nBtok = [P.tile(f"nBtok{z}", [128, NCH, CH], BF16) for z in range(2)]
        Vtok, BVtok = P.tile("Vtok", [128, NCH, CH], BF16)
        gamC = [P.tile(f"gamC{z}", [128, NCH], F32) for z in range(2)]
        bonus, Bbonus = P.tile("bonus", [128, T], F32)
        gate, Bgate = P.tile("gate_g", [128, T], BF16)
        with P.scope():
            Wr, BWr = P.tile("Wr", [128, DC, 3, 128], BF16)
            for j in range(3):
                load_w(P, Wr[:, :, j, :], BWr, w_in, j * 1024 + g * 128, 128)
            pst = [P.ptile(f"gps{i}", [128, 512], F32) for i in range(2)]
            psA = Rot([P.ptile(f"gpa{i}", [128, 512], F32) for i in range(3)])
            psTb = Rot([P.ptile(f"gpt{i}", [128, 512], F32) for i in range(2)])
            r_s, Br = P.tile("r_s", [128, T], F32)
            k_s, Bk = P.tile("k_s", [128, T], F32)
            v_s, Bv = P.tile("v_s", [128, T], F32)
            Wr2 = Wr[:].rearrange("p c j n -> p c (j n)")
            project_shift(P, cx, rc, Wr2, BWr, 0, g, r_s, Br, pst)
            project_shift(P, cx, rc, Wr2, BWr, 128, 8 + g, k_s, Bk, pst)
            project_shift(P, cx, rc, Wr2, BWr, 256, 16 + g, v_s, Bv, pst)
            kkc = colp[:, 84 + g:85 + g]
            kac = colp[:, 92 + g:93 + g]
            rkc = colp[:, 100 + g:101 + g]
            tmp = {}
            for nm in ("sq", "rn", "kk", "sg", "az", "cs", "incl", "e1", "e2", "t1", "kd", "kd0", "kka"):
                tmp[nm] = P.tile("g_" + nm, [128, 512], F32)
            tmp["u"] = tmp["sq"]
            tb16 = Rot([P.tile(f"g_tb16_{i}", [128, 512], BF16) for i in range(2)])
            for tb in range(4):
                sl = slice(tb * 512, (tb + 1) * 512)
                cs8 = slice(tb * 8, (tb + 1) * 8)
                sq, Bsq = tmp["sq"]
                act(P, sq[:], k_s[:, sl], AF.Square, [Bk, Bcolp], [Bsq], scale=kkc)
                pa, Bpa = psA.next()
                mm(P, pa[:], cx.blk[:], sq[:], True, True, [cx.Bblk, Bsq], [Bpa])
                rn, Brn = tmp["rn"]
                ts(P, rn[:], pa[:], 1e-12, ALU.max, [Bpa], [Brn])
                act(P, rn[:], rn[:], AF.Sqrt, [Brn], [Brn])
                P.op("vector", lambda e, rn=rn: e.reciprocal(out=rn[:], in_=rn[:]), reads=[Brn], writes=[Brn])
                kk, Bkk = tmp["kk"]
                stt(P, kk[:], k_s[:, sl], kkc, rn[:], ALU.mult, ALU.mult, [Bk, Bcolp, Brn], [Bkk])
                kd0, Bkd0 = tmp["kd0"]
                for z in range(2):
                    ARt, BAR = AR[z]
                    sg, Bsg = tmp["sg"]
                    az, Baz = tmp["az"]
                    pa, Bpa = psA.next()
                    mm(P, pa[:], wa_up[0:64, z, gc], lin[0:64, sl], True, True, [Bwa, Blin], [Bpa])
                    act(P, sg[:], pa[:], AF.Sigmoid, [Bpa, Bcolp], [Bsg], bias=colp[:, 52 + z * 8 + g:53 + z * 8 + g], scale=1.0)
                    pa, Bpa = psA.next()
                    mm(P, pa[:], wa_up[64:128, z, gc], lin[64:128, sl], True, True, [Bwa, Blin], [Bpa])
                    act(P, az[:], pa[:], AF.Sigmoid, [Bpa, Bcolp], [Baz], bias=colp[:, 68 + z * 8 + g:69 + z * 8 + g], scale=1.0)
                    cs, Bcs = tmp["cs"]
                    P.op("vector", lambda e, cs=cs, sg=sg: e.tensor_tensor_scan(out=cs[:], data0=rst[:], data1=sg[:], initial=0.0,
                                                                          op0=ALU.mult, op1=ALU.add),
                         reads=[Brst, Bsg], writes=[Bcs])
                    cs3 = cs[:].rearrange("p (n c) -> p n c", c=CH)
                    totb = cs3[:, :, CH - 1:CH].to_broadcast([128, 8, CH])
                    gC, BgC = gamC[z]
                    act(P, gC[:, cs8], cs3[:, :, CH - 1], AF.Exp, [Bcs], [BgC], scale=-C_DECAY)
                    if z == 0:
                        incl, Bincl = cs, Bcs
                    else:
                        incl, Bincl = tmp["incl"]
                        i3 = incl[:].rearrange("p (n c) -> p n c", c=CH)
                        tt(P, i3, totb, cs3, ALU.subtract, [Bcs], [Bincl])
                        tt(P, incl[:], incl[:], sg[:], ALU.add, [Bincl, Bsg], [Bincl], eng="gpsimd")
                    i3 = incl[:].rearrange("p (n c) -> p n c", c=CH)
                    e1, Be1 = tmp["e1"]
                    e2, Be2 = tmp["e2"]
                    t1, Bt1 = tmp["t1"]
                    kd, Bkd = (kd0, Bkd0) if z == 0 else tmp["kd"]
                    kka, Bkka = tmp["kka"]
                    ts(P, t1[:], az[:], kac, ALU.mult, [Baz, Bcolp, Bomka], [Bt1], s2=omka[:, g:g + 1], op1=ALU.add)
                    tt(P, kd[:], t1[:], k_s[:, sl], ALU.mult, [Bt1, Bk], [Bkd])
                    tt(P, kka[:], az[:], kk[:], ALU.mult, [Baz, Bkk], [Bkka], eng="gpsimd")
                    act(P, e1[:], incl[:], AF.Exp, [Bincl], [Be1], scale=-C_DECAY)
                    tt(P, ARt[:, cs8, 1, :], r_s[:, sl].rearrange("p (n c) -> p n c", c=CH), e1[:].rearrange("p (n c) -> p n c", c=CH),
                       ALU.mult, [Br, Be1], [BAR])
                    act(P, e2[:], incl[:], AF.Exp, [Bincl], [Be2], scale=C_DECAY)
                    tt(P, KT[z][0][:, sl], kd[:], e2[:], ALU.mult, [Bkd, Be2], [KT[z][1]])
                    tt(P, BT[z][0][:, sl], kka[:], e2[:], ALU.mult, [Bkka, Be2], [BT[z][1]], eng="gpsimd")
                    tt(P, t1[:], incl[:], sg[:], ALU.subtract, [Bincl, Bsg], [Bt1])
                    act(P, e1[:], t1[:], AF.Exp, [Bt1], [Be1], scale=-C_DECAY)
                    tt(P, ARt[:, cs8, 0, :], kk[:].rearrange("p (n c) -> p n c", c=CH), e1[:].rearrange("p (n c) -> p n c", c=CH),
                       ALU.mult, [Bkk, Be1], [BAR])
                    t13 = t1[:].rearrange("p (n c) -> p n c", c=CH)
                    tt(P, t13, totb, i3, ALU.subtract, [Bcs, Bincl], [Bt1])
                    act(P, e2[:], t1[:], AF.Exp, [Bt1], [Be2], scale=-C_DECAY)
                    for which in range(2):
                        hb, Bhb = tb16.next()
                        if which == 0:
                            tt(P, hb[:], kd[:], e2[:], ALU.mult, [Bkd, Be2], [Bhb])
                            dstt, Bdst = Ktok[z]
                        else:
                            stt(P, hb[:], kka[:], -1.0, e2[:], ALU.mult, ALU.mult, [Bkka, Be2], [Bhb])
                            dstt, Bdst = nBtok[z]
                        pt, Bp = psTb.next()
                        ptb = pt[:].bitcast(BF16)
                        for c in range(8):
                            for hh in range(2):
                                hs = slice(hh * 64, hh * 64 + 64)
                                tr(P, ptb[hs, c * CH:(c + 1) * CH], hb[hs, c * CH:(c + 1) * CH], cx.ident_b[hs, hs], [Bhb, cx.Bident_b], [Bp])
                        act(P, dstt[:, tb * 8:(tb + 1) * 8, :], ptb[:, 0:512].rearrange("p (j c) -> p j c", j=8), AF.Copy, [Bp], [Bdst])
                    if z == 1:
                        tt(P, kd[:], kd[:], kd0[:], ALU.add, [Bkd, Bkd0], [Bkd], eng="gpsimd")
                        u, Bu = tmp["u"]
                        stt(P, u[:], kd[:], rkc, r_s[:, sl], ALU.mult, ALU.mult, [Bkd, Bcolp, Br], [Bu])
                        pa, Bpa = psA.next()
                        mm(P, pa[:], cx.blk[:], u[:], True, True, [cx.Bblk, Bu], [Bpa])
                        tt(P, bonus[:, sl], pa[:], v_s[:, sl], ALU.mult, [Bpa, Bv], [Bbonus])
                hb, Bhb = tb16.next()
                vcopy(P, hb[:], v_s[:, sl], [Bv], [Bhb])
                pt, Bp = psTb.next()
                ptb = pt[:].bitcast(BF16)
                for c in range(8):
                    for hh in range(2):
                        hs = slice(hh * 64, hh * 64 + 64)
                        tr(P, ptb[hs, c * CH:(c + 1) * CH], hb[hs, c * CH:(c + 1) * CH], cx.ident_b[hs, hs], [Bhb, cx.Bident_b], [Bp])
                vcopy(P, Vtok[:, tb * 8:(tb + 1) * 8, :], ptb[:, 0:512].rearrange("p (j c) -> p j c", j=8), [Bp], [BVtok])
                pa, Bpa = psA.next()
                mm(P, pa[:], g_up[:, gc], sdg[:, sl], True, True, [Bgup, Bsdg], [Bpa])
                act(P, gate[:, sl], pa[:], AF.Copy, [Bpa], [Bgate])
        if DBG["stop"] < 2:
            return
        X, BX = P.tile("Xm", [128, NCH, 2, 4, CH], BF16)
        with P.scope():
            NT0, BNT0 = P.tile("NT0", [128, NCH, 2, CH], BF16)
            with P.scope():
                pcA = Rot([P.ptile(f"pcA{i}", [128, 2, 256], F32) for i in range(3)])
                pcB = Rot([P.ptile(f"pcB{i}", [128, 2, CH], F32) for i in range(3)])
                for i in range(NCH // 2):
                    for z in range(2):
                        pa, Bpa = pcA.next()
                        pb_, Bpb = pcB.next()
                        ARt, BAR = AR[z]
                        for j in range(2):
                            n = 2 * i + j
                            ns = slice(n * CH, (n + 1) * CH)
                            for hh in range(2):
                                hs = slice(hh * 64, hh * 64 + 64)
                                arr = ARt[hs, n, :, :].rearrange("p a c -> p (a c)")
                                mm(P, pa[hs, j, 0:128], BT[z][0][hs, ns], arr, True, True, [BT[z][1], BAR], [Bpa])
                                mm(P, pa[hs, j, 128:256], KT[z][0][hs, ns], arr, True, True, [KT[z][1], BAR], [Bpa])
                                mm(P, pb_[hs, j, :], ARt[hs, n, 0, :], BT[z][0][hs, ns], True, True, [BAR, BT[z][1]], [Bpb])
                        tt(P, X[:, 2 * i:2 * i + 2, z, :, :], pa[:].rearrange("p j (w c) -> p j w c", w=4), maskA[:, z, :, :, :], ALU.mult,
                           [Bpa, BmaskA], [BX])
                        tt(P, NT0[:, 2 * i:2 * i + 2, z, :], pb_[:], maskB[:, z, :, :], ALU.mult, [Bpb, BmaskB], [BNT0])
            if DBG["stop"] < 3:
                return
            with P.scope():
                NB = 8
                NSTR = 2
                Xm = X[:].rearrange("p n z w c -> p (n z) w c")
                NT0m = NT0[:].rearrange("p n z c -> p (n z) c")
                streams = []
                for si in range(NSTR):
                    st_ = Ctx()
                    st_.psN = P.ptile(f"psN{si}", [128, NB, CH], F32)
                    st_.psNT = P.ptile(f"psNT{si}", [128, NB, CH], F32)
                    st_.psI = P.ptile(f"psI{si}", [128, NB, CH], F32)
                    st_.Ns = Rot([P.tile(f"Ncur{si}_{i}", [128, NB, CH], BF16) for i in range(2)])
                    st_.NTs = Rot([P.tile(f"NTcur{si}_{i}", [128, NB, CH], BF16) for i in range(2)])
                    st_.Invs = Rot([P.tile(f"Inv{si}_{i}", [128, NB, CH], BF16) for i in range(2)])
                    streams.append(st_)
                nbatch = 64 // NB
                for b0 in range(0, nbatch, NSTR):
                    act_streams = []
                    for si in range(NSTR):
                        bi = b0 + si
                        st_ = streams[si]
                        st_.ms = slice(bi * NB, (bi + 1) * NB)
                        st_.Nprev = (lambda m, hs, bi=bi: Xm[hs, bi * NB + m, 0, :])
                        st_.NTprev = (lambda m, hs, bi=bi: NT0m[hs, bi * NB + m, :])
                        st_.BNprev, st_.BNTprev = BX, BNT0
                        st_.Inv, st_.BInv = st_.Invs.next()
                        tt(P, st_.Inv[:], Xm[:, st_.ms, 0, :], identblk[:].unsqueeze(1).to_broadcast([128, NB, CH]), ALU.add,
                           [BX, Bidb], [st_.BInv])
                        act_streams.append(st_)
                    for lev in range(1, 6):
                        for st_ in act_streams:
                            st_.Nn, st_.BNn = st_.Ns.next()
                            st_.NTn, st_.BNTn = st_.NTs.next()
                            for m in range(NB):
                                for hh in range(2):
                                    hs = slice(hh * 64, hh * 64 + 64)
                                    if lev < 5:
                                        mm(P, st_.psN[0][hs, m, :], st_.NTprev(m, hs), st_.Nprev(m, hs), True, True,
                                           [st_.BNprev, st_.BNTprev], [st_.psN[1]])
                                    mm(P, st_.psNT[0][hs, m, :], st_.Nprev(m, hs), st_.NTprev(m, hs), True, True,
                                       [st_.BNprev, st_.BNTprev], [st_.psNT[1]])
                            if lev < 5:
                                act(P, st_.Nn[:], st_.psN[0][:], AF.Copy, [st_.psN[1]], [st_.BNn])
                            vcopy(P, st_.NTn[:], st_.psNT[0][:], [st_.psNT[1]], [st_.BNTn])
                        for st_ in act_streams:
                            for m in range(NB):
                                for hh in range(2):
                                    hs = slice(hh * 64, hh * 64 + 64)
                                    mm(P, st_.psI[0][hs, m, :], st_.NTn[hs, m, :], st_.Inv[hs, m, :], True, True,
                                       [st_.BNTn, st_.BInv], [st_.psI[1]])
                            if lev < 5:
                                Inv2, BInv2 = st_.Invs.next()
                                tt(P, Inv2[:], st_.psI[0][:], st_.Inv[:], ALU.add, [st_.psI[1], st_.BInv], [BInv2])
                                st_.Inv, st_.BInv = Inv2, BInv2
                            else:
                                tt(P, Xm[:, st_.ms, 0, :], st_.psI[0][:], st_.Inv[:], ALU.add, [st_.psI[1], st_.BInv], [BX])
                            st_.Nprev = (lambda m, hs, Nn=st_.Nn: Nn[hs, m, :])
                            st_.NTprev = (lambda m, hs, NTn=st_.NTn: NTn[hs, m, :])
                            st_.BNprev, st_.BNTprev = st_.BNn, st_.BNTn
        if DBG["stop"] < 4:
            return
        Yz, BYz = P.tile("Yz", [128, 2, T], F32)
        with P.scope():
            ST, BST = P.tile("ST", [128, 2, CH], F32)
            STb, BSTb = P.tile("STb", [128, 2, CH], BF16)
            memset(P, ST[:], 0.0, [BST])
            memset(P, STb[:], 0.0, [BSTb])
            psWs = Rot([P.ptile(f"psW{i}", [128, 2, CH], F32) for i in range(2)])
            psPs = Rot([P.ptile(f"psP{i}", [128, 2, CH], F32) for i in range(2)])
            psYs = Rot([P.ptile(f"psYs{i}", [128, 2, CH], F32) for i in range(2)])
            psSs = Rot([P.ptile(f"psS{i}", [128, 2, CH], F32) for i in range(2)])
            Wsbs = Rot([P.tile(f"Wsb{i}", [128, 2, CH], BF16) for i in range(2)])
            Psbs = Rot([P.tile(f"Psb{i}", [128, 2, CH], BF16) for i in range(2)])
            H = [slice(0, 64), slice(64, 128)]
            for n in range(NCH):
                czs = [n, NCH - 1 - n]
                pW, BpW = psWs.next()
                pP, BpP = psPs.next()
                pY, BpY = psYs.next()
                pS, BpS = psSs.next()
                Wsb, BWsb = Wsbs.next()
                Psb, BPsb = Psbs.next()
                for z in range(2):
                    cz = czs[z]
                    for hs in H:
                        mm(P, pW[hs, z, :], AR[z][0][hs, cz, 0, :], STb[hs, z, :], True, False, [AR[z][1], BSTb], [BpW])
                        mm(P, pW[hs, z, :], X[hs, cz, z, 2, :], Vtok[hs, cz, :], False, True, [BX, BVtok], [BpW])
                vcopy(P, Wsb[:], pW[:], [BpW], [BWsb])
                for z in range(2):
                    cz = czs[z]
                    for hs in H:
                        mm(P, pP[hs, z, :], X[hs, cz, z, 0, :], Wsb[hs, z, :], True, True, [BX, BWsb], [BpP])
                act(P, Psb[:], pP[:], AF.Copy, [BpP], [BPsb])
                for z in range(2):
                    cz = czs[z]
                    for hs in H:
                        mm(P, pS[hs, z, :], Ktok[z][0][hs, cz, :], Vtok[hs, cz, :], True, False, [Ktok[z][1], BVtok], [BpS])
                        mm(P, pS[hs, z, :], nBtok[z][0][hs, cz, :], Psb[hs, z, :], False, True, [nBtok[z][1], BPsb], [BpS])
                for z in range(2):
                    cz = czs[z]
                    for hs in H:
                        mm(P, pY[hs, z, :], STb[hs, z, :], AR[z][0][hs, cz, 1, :], True, False, [BSTb, AR[z][1]], [BpY])
                        mm(P, pY[hs, z, :], Vtok[hs, cz, :], X[hs, cz, z, 3, :], False, False, [BVtok, BX], [BpY])
                        mm(P, pY[hs, z, :], Psb[hs, z, :], X[hs, cz, z, 1, :], False, True, [BPsb, BX], [BpY])
                for z in range(2):
                    cz = czs[z]
                    ts(P, ST[:, z, :], ST[:, z, :], gamC[z][0][:, cz:cz + 1], ALU.mult, [BST, gamC[z][1]], [BST], eng="gpsimd")
                tt(P, ST[:], ST[:], pS[:], ALU.add, [BST, BpS], [BST])
                act(P, STb[:], ST[:], AF.Copy, [BST], [BSTb])
                for z in range(2):
                    cz = czs[z]
                    if z == 0:
                        act(P, Yz[:, z, cz * CH:(cz + 1) * CH], pY[:, z, :], AF.Copy, [BpY], [BYz])
                    else:
                        vcopy(P, Yz[:, z, cz * CH:(cz + 1) * CH], pY[:, z, :], [BpY], [BYz])
        if DBG["stop"] < 5:
            return
        with P.scope():
            psA = Rot([P.ptile(f"opa{i}", [128, 512], F32) for i in range(2)])
            ysum, Bys = P.tile("ysum", [128, 512], F32)
            yc, Byc = P.tile("yc", [128, 512], F32)
            sq, Bsq = P.tile("osq", [128, 512], F32)
            rs, Brs = P.tile("ors", [128, 512], F32)
            yT, ByT = P.tile("ryT", [128, T], BF16)
            for tb in range(4):
                sl = slice(tb * 512, (tb + 1) * 512)
                tt(P, ysum[:], Yz[:, 0, sl], Yz[:, 1, sl], ALU.add, [BYz], [Bys])
                pa, Bpa = psA.next()
                mm(P, pa[:], cx.blk64[:], ysum[:], True, True, [cx.Bblk64, Bys], [Bpa])
                tt(P, yc[:], ysum[:], pa[:], ALU.subtract, [Bys, Bpa], [Byc])
                act(P, sq[:], yc[:], AF.Square, [Byc], [Bsq])
                pa, Bpa = psA.next()
                mm(P, pa[:], cx.blk64[:], sq[:], True, True, [cx.Bblk64, Bsq], [Bpa])
                ts(P, rs[:], pa[:], 64e-5, ALU.add, [Bpa], [Brs])
                act(P, rs[:], rs[:], AF.Sqrt, [Brs], [Brs])
                P.op("vector", lambda e: e.reciprocal(out=rs[:], in_=rs[:]), reads=[Brs], writes=[Brs])
                tt(P, yc[:], yc[:], rs[:], ALU.mult, [Byc, Brs], [Byc])
                ts(P, yc[:], yc[:], colp[:, 108 + g:109 + g], ALU.mult, [Byc, Bcolp], [Byc], s2=colp[:, 116 + g:117 + g], op1=ALU.add)
                tt(P, yc[:], yc[:], bonus[:, sl], ALU.add, [Byc, Bbonus], [Byc], eng="gpsimd")
                tt(P, yT[:, sl], yc[:], gate[:, sl], ALU.mult, [Byc, Bgate], [ByT])
            P.dma("sync", cx.ybrT[g * 128:(g + 1) * 128, :], yT[:], reads=[ByT], writes=[cx.BybrT[g]])


def host_colp(inputs, nl):
    out = np.zeros((nl, 128, 160), np.float32)
    for l in range(nl):
        mu = inputs["rwkv_mu"][l]
        out[l, :, 0:26] = mu[0].reshape(26, 128).T
        out[l, :, 26:52] = mu[1].reshape(26, 128).T
        for z in range(2):
            out[l, :, 52 + z * 8:60 + z * 8] = inputs["rwkv_w0"][l, z].reshape(8, 128).T
            out[l, :, 68 + z * 8:76 + z * 8] = inputs["rwkv_a0"][l, z].reshape(8, 128).T
        out[l, :, 84:92] = inputs["rwkv_k_k"][l].reshape(8, 128).T
        out[l, :, 92:100] = inputs["rwkv_k_a"][l].reshape(8, 128).T
        out[l, :, 100:108] = inputs["rwkv_r_k"][l].reshape(8, 128).T
        out[l, :, 108:116] = inputs["rwkv_gn_g"][l].reshape(8, 128).T
        out[l, :, 116:124] = inputs["rwkv_gn_b"][l].reshape(8, 128).T
    return out


def build_program(nl=NL, phases=("rwkv", "win", "diff", "mem", "merge", "moe"), dbg=False):
    nc = bass.Bass("TRN2", target_bir_lowering=False)
    cx = Ctx()
    shapes = [(nm, ((nl,) + shp[1:]) if (shp[0] == NL and nm not in ("x",)) else shp) for nm, shp in INPUT_SHAPES]
    declare_inputs(nc, cx, shapes)
    cx.y = nc.dram_tensor("y", [T, D], F32, kind="ExternalOutput").ap()
    kind = "ExternalOutput" if dbg else "Internal"
    cx.ybrT = nc.dram_tensor("ybrT", [26 * 128, T], BF16, kind=kind).ap()
    cx.x1res = nc.dram_tensor("x1res", [T, D], F32, kind=kind).ap()
    cx.xres = nc.dram_tensor("xres", [T, D], F32, kind=kind).ap()
    P = Prog(nc)
    cx.BybrT = [P.buf(f"ybr{i}") for i in range(26)]
    cx.Bx1res = P.buf("x1res")
    cx.Bxres = P.buf("xres")
    cx.By = P.buf("y")
    setup_consts(P, cx)
    cx.memT, cx.BmemT = P.tile("memT", [128, DC, 256], BF16)
    cx.affT, cx.BaffT = P.tile("affT", [16, T], F32)
    cx.aff_tok, cx.Baff_tok = P.tile("aff_tok", [128, NT, 16], F32)
    build_memT(P, cx)
    for l in range(nl):
        src = cx.x if l == 0 else cx.xres
        with P.scope():
            cx.xT, cx.BxT = P.tile("xT", [128, DC, T], BF16)
            build_xT(P, cx, src)
            if "rwkv" in phases:
                phase_rwkv(P, cx, l)
            if "win" in phases:
                phase_window(P, cx, l)
            if "diff" in phases:
                phase_diff(P, cx, l)
            if "mem" in phases:
                phase_mem(P, cx, l)
            if "merge" in phases:
                phase_merge(P, cx, l, src)
        if "moe" in phases:
            last = (l == nl - 1)
            phase_moe(P, cx, l, cx.y if last else cx.xres, cx.By if last else cx.Bxres)
    P.barrier()
    P.emit()
    return nc, P


def make_in_maps(inputs, nl=NL, cores=NCORES):
    G, farb = host_tables(np.asarray(inputs["rel_bias"], np.float32))
    colp = host_colp(inputs, nl)
    shared = {"tblG": G, "farb": farb, "colp": colp}
    for nm, shp in INPUT_SHAPES:
        if nm in ("x", "mem", "tblG", "farb", "colp"):
            continue
        a = np.asarray(inputs[nm], np.float32)
        shared[nm] = np.ascontiguousarray(a[:nl]) if shp[0] == NL else a
    maps = []
    for c in range(cores):
        m = dict(shared)
        m["x"] = np.ascontiguousarray(np.asarray(inputs["x"][c], np.float32))
        m["mem"] = np.ascontiguousarray(np.asarray(inputs["mem"][c], np.float32))
        maps.append(m)
    return maps


_CACHE = {}


def kernel(**inputs):
    if "nc" not in _CACHE:
        _CACHE["nc"] = build_program()[0]
    nc = _CACHE["nc"]
    maps = make_in_maps(inputs)
    res = run_bass_kernel_spmd(nc, maps, core_ids=list(range(NCORES)))
    out = np.stack([np.asarray(r["y"], np.float32) for r in res.results], axis=0)
    return out
```

```python
import math
from contextlib import ExitStack, contextmanager

import numpy as np
import concourse.bass as bass
import concourse.mybir as mybir
from concourse.bass_utils import run_bass_kernel_spmd

F32 = mybir.dt.float32
BF16 = mybir.dt.bfloat16
I32 = mybir.dt.int32
AF = mybir.ActivationFunctionType
ALU = mybir.AluOpType
AX = mybir.AxisListType

T = 2048
D = 1024
NT = 16
DC = 8
NL = 4
NCORES = 8
SEM_LIMIT = 30000
DN_ALPHA = (2 * NL) ** 0.25
C_DECAY = math.exp(-0.5)

O_RWKV = 0
O_WINQ = 3328
O_WINK = 4352
O_WINV = 4608
O_DQ = 4864
O_DK = 5888
O_DV = 6912
O_MEMQ = 7936
O_GATE = 8192


class Buf:
    __slots__ = ("name", "w", "r", "dsem", "dcnt")

    def __init__(self, name):
        self.name = name
        self.w = {}
        self.r = {}
        self.dsem = None
        self.dcnt = 0


class Prog:
    ENGS = ("tensor", "vector", "scalar", "gpsimd", "sync")

    def __init__(self, nc):
        self.nc = nc
        self.es = ExitStack()
        self.q = {n: [] for n in self.ENGS}
        self.esem = {}
        self.ecnt = {}
        self.waited = {n: {} for n in self.ENGS}
        self.nsem = 0
        self.pe_sems = set()
        self.semobj = {}
        self.free_dsems = []
        self.scopes = [[]]
        self.stacks = [self.es]
        self.n_inst = 0
        self.nname = 0
        self.pstate = {}
        for n in self.ENGS:
            self._new_esem(n)

    def sem(self, name):
        self.nsem += 1
        s = self.es.enter_context(self.nc.semaphore(f"{name}_{self.nsem}"))
        self.semobj[id(s)] = s
        return s

    def _new_esem(self, n):
        self.esem[n] = self.sem("e" + n)
        self.ecnt[n] = 0
        if n == "tensor":
            self.pe_sems.add(id(self.esem[n]))

    def sb(self, name, shape, dt):
        self.nname += 1
        return self.stacks[-1].enter_context(self.nc.sbuf_tensor(f"{name}_s{self.nname}", shape, dt))

    def ps(self, name, shape, dt=F32):
        self.nname += 1
        return self.stacks[-1].enter_context(self.nc.psum_tensor(f"{name}_p{self.nname}", shape, dt))

    def buf(self, name):
        b = Buf(name)
        self.scopes[-1].append(b)
        return b

    def tile(self, name, shape, dt):
        return self.sb(name, shape, dt), self.buf(name)

    def ptile(self, name, shape, dt=F32):
        return self.ps(name, shape, dt), self.buf(name)

    @contextmanager
    def scope(self):
        st = ExitStack()
        self.stacks.append(st)
        self.scopes.append([])
        try:
            yield
        finally:
            self.barrier()
            for b in self.scopes.pop():
                if b.dsem is not None:
                    self.free_dsems.append((b.dsem, b.dcnt))
                    b.dsem = None
            self.stacks.pop()
            st.close()

    def _deps(self, reads, writes, skip=()):
        deps = {}

        def add(d):
            for k, v in d.items():
                if k in skip:
                    continue
                if deps.get(k, 0) < v:
                    deps[k] = v

        for b in reads:
            add(b.w)
        for b in writes:
            add(b.w)
            add(b.r)
        return deps

    def _waits(self, eng, deps):
        out = []
        wd = self.waited[eng]
        for k, v in deps.items():
            if wd.get(k, 0) >= v:
                continue
            wd[k] = v
            out.append((self.semobj[k], v))
        return out

    def op(self, eng, fn, reads=(), writes=()):
        if self.ecnt[eng] >= SEM_LIMIT:
            self._new_esem(eng)
        skip = self.pe_sems if eng == "tensor" else ()
        waits = self._waits(eng, self._deps(reads, writes, skip))
        self.ecnt[eng] += 1
        s = self.esem[eng]
        v = self.ecnt[eng]
        k = id(s)
        self.q[eng].append((waits, fn, s, 1))
        self.n_inst += 1
        for b in reads:
            if b.r.get(k, 0) < v:
                b.r[k] = v
        for b in writes:
            b.w = {k: v}
            b.r = {}

    def _get_dsem(self, dst):
        if dst.dsem is None or dst.dcnt >= SEM_LIMIT:
            if self.free_dsems and dst.dsem is None:
                dst.dsem, dst.dcnt = self.free_dsems.pop()
                if dst.dcnt >= SEM_LIMIT:
                    dst.dsem = self.sem("d")
                    dst.dcnt = 0
            else:
                dst.dsem = self.sem("d")
                dst.dcnt = 0

    def dma(self, eng, out, in_, reads=(), writes=(), **kw):
        dst = writes[0]
        old = dst.dsem
        self._get_dsem(dst)
        skip = (id(dst.dsem),) if old is dst.dsem else ()
        waits = self._waits(eng, self._deps(reads, writes, skip))
        dst.dcnt += 16
        s, v = dst.dsem, dst.dcnt
        k = id(s)
        self.q[eng].append((waits, (lambda e, o=out, i=in_, kw=kw: e.dma_start(out=o, in_=i, **kw)), s, 16))
        self.n_inst += 1
        for b in reads:
            if b.r.get(k, 0) < v:
                b.r[k] = v
        for b in writes:
            b.w = {k: v}
            b.r = {}

    def pstart(self, B, bank, pk):
        d = self.pstate.setdefault(id(B), set())
        keys = [(bank, "l"), (bank, "h")] if pk == "f" else [(bank, pk)]
        st = not all(k in d for k in keys)
        d.update(keys)
        return st

    def preset(self, B, pk=None):
        d = self.pstate.get(id(B))
        if d is None:
            return
        if pk is None:
            d.clear()
        else:
            for k in [k for k in d if k[1] == pk]:
                d.discard(k)

    def barrier(self):
        deps = {}
        for n in self.ENGS:
            if self.ecnt[n] > 0:
                deps[id(self.esem[n])] = self.ecnt[n]
        for sc in self.scopes:
            for b in sc:
                if b.dsem is not None and b.dcnt > 0:
                    k = id(b.dsem)
                    if deps.get(k, 0) < b.dcnt:
                        deps[k] = b.dcnt
        for n in self.ENGS:
            own = id(self.esem[n])
            d = {k: v for k, v in deps.items() if k != own}
            waits = self._waits(n, d)
            if waits:
                self.q[n].append((waits, None, None, 0))

    def emit(self):
        nc = self.nc
        with nc.Block() as block:
            def mk(name):
                def body(e):
                    for waits, fn, s, n in self.q[name]:
                        for ws, wv in waits:
                            e.wait_ge(ws, wv)
                        if fn is not None:
                            fn(e).then_inc(s, n)
                return body
            block.sync(mk("sync"))
            block.tensor(mk("tensor"))
            block.vector(mk("vector"))
            block.scalar(mk("scalar"))
            block.gpsimd(mk("gpsimd"))
        self.es.close()


class Ctx:
    pass


def mm(P, out, lhsT, rhs, start, stop, reads, writes):
    P.op("tensor", lambda e: e.matmul(out, lhsT, rhs, start=start, stop=stop), reads=reads, writes=writes)


def mma(P, out, lhsT, rhs, Bps, bank, pk, reads, stop=False):
    mm(P, out, lhsT, rhs, P.pstart(Bps, bank, pk), stop, reads, [Bps])


def tr(P, out, in_, ident, reads, writes):
    P.op("tensor", lambda e: e.transpose(out, in_, ident), reads=reads, writes=writes)


def act(P, out, in_, func, reads, writes, bias=None, scale=None, accum_out=None):
    kw = {}
    if bias is not None:
        kw["bias"] = bias
    if scale is not None:
        kw["scale"] = scale
    if accum_out is not None:
        kw["accum_out"] = accum_out
    P.op("scalar", lambda e: e.activation(out=out, in_=in_, func=func, **kw), reads=reads, writes=writes)


def vcopy(P, out, in_, reads, writes, eng="vector"):
    P.op(eng, lambda e: e.tensor_copy(out=out, in_=in_), reads=reads, writes=writes)


def tt(P, out, in0, in1, op, reads, writes, eng="vector"):
    P.op(eng, lambda e: e.tensor_tensor(out=out, in0=in0, in1=in1, op=op), reads=reads, writes=writes)


def ts(P, out, in0, s1, op0, reads, writes, s2=None, op1=None, eng="vector", accum_out=None):
    kw = {}
    if op1 is not None:
        kw["op1"] = op1
    if accum_out is not None:
        kw["accum_out"] = accum_out
    P.op(eng, lambda e: e.tensor_scalar(out=out, in0=in0, scalar1=s1, scalar2=s2, op0=op0, **kw), reads=reads, writes=writes)


def stt(P, out, in0, scalar, in1, op0, op1, reads, writes):
    P.op("vector", lambda e: e.scalar_tensor_tensor(out=out, in0=in0, scalar=scalar, in1=in1, op0=op0, op1=op1),
         reads=reads, writes=writes)


def rsqrt(P, cx, out, in_, B, scale, eps):
    ts(P, out, in_, scale, ALU.mult, [B], [B], s2=eps, op1=ALU.add)
    act(P, out, out, AF.Sqrt, [B], [B])
    P.op("vector", lambda e: e.reciprocal(out=out, in_=out), reads=[B], writes=[B])


def memset(P, ap, val, writes, eng="vector"):
    P.op(eng, lambda e: e.memset(ap, val), writes=writes)


def setup_consts(P, cx):
    nc = P.nc
    dif_i, Bd = P.tile("dif_i", [128, 128], I32)
    P.op("gpsimd", lambda e: e.iota(dif_i[:], pattern=[[1, 128]], base=0, channel_multiplier=-1), writes=[Bd])
    dif, Bdf = P.tile("dif_f", [128, 128], F32)
    vcopy(P, dif[:], dif_i[:], [Bd], [Bdf])
    cx.ident_f, cx.Bident_f = P.tile("ident_f", [128, 128], F32)
    ts(P, cx.ident_f[:], dif[:], 0.0, ALU.is_equal, [Bdf], [cx.Bident_f])
    cx.ident_b, cx.Bident_b = P.tile("ident_b", [128, 128], BF16)
    vcopy(P, cx.ident_b[:], cx.ident_f[:], [cx.Bident_f], [cx.Bident_b])
    cx.dif = dif
    cx.Bdif = Bdf
    ci_i, Bci = P.tile("ci_i", [128, 256], I32)
    P.op("gpsimd", lambda e: e.iota(ci_i[:], pattern=[[1, 256]], base=0, channel_multiplier=0), writes=[Bci])
    cx.colidx, cx.Bcolidx = P.tile("colidx", [128, 256], F32)
    vcopy(P, cx.colidx[:], ci_i[:], [Bci], [cx.Bcolidx])
    pi_i, Bpi = P.tile("pi_i", [128, 1], I32)
    P.op("gpsimd", lambda e: e.iota(pi_i[:], pattern=[[0, 1]], base=0, channel_multiplier=1), writes=[Bpi])
    cx.pidx, cx.Bpidx = P.tile("pidx", [128, 1], F32)
    vcopy(P, cx.pidx[:], pi_i[:], [Bpi], [cx.Bpidx])
    cx.blk, cx.Bblk = P.tile("blk", [128, 128], F32)
    memset(P, cx.blk[:], 0.0, [cx.Bblk])
    memset(P, cx.blk[0:64, 0:64], 1.0, [cx.Bblk])
    memset(P, cx.blk[64:128, 64:128], 1.0, [cx.Bblk])
    cx.blk64, cx.Bblk64 = P.tile("blk64", [128, 128], F32)
    ts(P, cx.blk64[:], cx.blk[:], 1.0 / 64.0, ALU.mult, [cx.Bblk], [cx.Bblk64])
    cx.ones_b, cx.Bones_b = P.tile("ones_b", [128, 128], BF16)
    memset(P, cx.ones_b[:], 1.0, [cx.Bones_b])


def build_xT(P, cx, src_dram):
    with P.scope():
        xin = [P.tile(f"xin{i}", [128, D], F32) for i in range(2)]
        pst = [P.ptile(f"xtp{i}", [128, 512], F32) for i in range(2)]
        k = 0
        for i in range(NT):
            xt_, Bx = xin[i % 2]
            P.dma("sync", xt_[:], src_dram[i * 128:(i + 1) * 128, :], writes=[Bx])
            for half in range(2):
                pt, Bp = pst[k % 2]
                k += 1
                for j in range(4):
                    c = half * 4 + j
                    tr(P, pt[:, j * 128:(j + 1) * 128], xt_[:, c * 128:(c + 1) * 128], cx.ident_f[:],
                       [Bx, cx.Bident_f], [Bp])
                dst = cx.xT[:, half * 4:half * 4 + 4, i * 128:(i + 1) * 128]
                src = pt[:].rearrange("p (j t) -> p j t", j=4)
                if half == 0:
                    vcopy(P, dst, src, [Bp], [cx.BxT])
                else:
                    act(P, dst, src, AF.Copy, [Bp], [cx.BxT])


def load_w(P, dst_tile, Bdst, w2d, c0, ncols, r0=0, nchunks=DC, eng="gpsimd"):
    src = w2d[r0:r0 + nchunks * 128, c0:c0 + ncols].rearrange("(c p) n -> p c n", p=128)
    P.dma(eng, dst_tile, src, writes=[Bdst])


def proj_fm(P, cx, dst_fn, W, BW, wcol0, M, evac, nchunks=DC, rhs=None, Brhs=None, tlen=T, pst=None, extra_reads=()):
    rhs = cx.xT if rhs is None else rhs
    Brhs = cx.BxT if Brhs is None else Brhs
    nb = tlen // 512
    for tb in range(nb):
        pt, Bp = pst[tb % len(pst)]
        for c in range(nchunks):
            mm(P, pt[0:M, :], W[:, c, wcol0:wcol0 + M], rhs[:, c, tb * 512:(tb + 1) * 512], c == 0, c == nchunks - 1,
               [BW, Brhs], [Bp])
        evac(tb, pt, Bp)


class Rot:
    def __init__(self, items):
        self.items = items
        self.i = 0

    def next(self):
        it = self.items[self.i % len(self.items)]
        self.i += 1
        return it


def attn_block(P, cx, acc, Bacc, dvp, q_ap, Bq, k_ap, Bk, v_ap, Bv, plan, pss, tmps, ets, depth=2):
    started = set()
    slot_w = acc.shape[2]
    staged = []

    def stage_a(item):
        (j, qlo, qhi, kind, arg, firsts, lasts) = item
        ncol = (qhi - qlo + 1) * 128
        ps_s, Bps = pss.next()
        mm(P, ps_s[:, 0:ncol], k_ap(j), q_ap(qlo, ncol), True, True, [Bk, Bq], [Bps])
        eT, Be = ets.next()
        if kind == "tbl":
            tmp, Bt = tmps.next()
            tab, Btab = arg
            stt(P, tmp[:, 0:ncol], ps_s[:, 0:ncol], 0.125, tab, ALU.mult, ALU.add, [Bps, Btab], [Bt])
            act(P, eT[:, 0:ncol], tmp[:, 0:ncol], AF.Exp, [Bt], [Be])
        elif kind == "far":
            col, Bcol = arg
            act(P, eT[:, 0:ncol], ps_s[:, 0:ncol], AF.Exp, [Bps, Bcol], [Be], bias=col, scale=0.125)
        else:
            act(P, eT[:, 0:ncol], ps_s[:, 0:ncol], AF.Exp, [Bps], [Be], scale=0.125)
        staged.append((item, eT, Be))

    def stage_b():
        (j, qlo, qhi, kind, arg, firsts, lasts), eT, Be = staged.pop(0)
        for s in range(qlo, qhi + 1):
            bank = (s * slot_w) // 512
            st = bank not in started
            started.add(bank)
            mm(P, acc[:, s, 0:dvp], eT[:, (s - qlo) * 128:(s - qlo + 1) * 128], v_ap(j), st, s in lasts,
               [Be, Bv], [Bacc])

    n = len(plan)
    for i in range(min(depth, n)):
        stage_a(plan[i])
    for i in range(n):
        if i + depth < n:
            stage_a(plan[i + depth])
        stage_b()


def ytok_to_dram(P, cx, ytok, Bytok, yT, ByT, pst, chunk):
    for q4 in range(4):
        pt, Bp = pst.next()
        ptb = pt[:].bitcast(BF16)
        for j in range(4):
            i = q4 * 4 + j
            tr(P, ptb[:, j * 128:(j + 1) * 128], ytok[:, i, :], cx.ident_b[:], [Bytok, cx.Bident_b], [Bp])
        if q4 % 2 == 0:
            vcopy(P, yT[:, q4 * 512:(q4 + 1) * 512], ptb[:, 0:512], [Bp], [ByT])
        else:
            act(P, yT[:, q4 * 512:(q4 + 1) * 512], ptb[:, 0:512], AF.Copy, [Bp], [ByT])
    P.dma("sync", cx.ybrT[chunk * 128:(chunk + 1) * 128, :], yT[:], reads=[ByT], writes=[cx.BybrT[chunk]])


def phase_window(P, cx, l):
    w_in = cx.w_in[l]
    with P.scope():
        W, BW = P.tile("winW", [128, DC, 1536], BF16)
        for k in range(3):
            load_w(P, W[:, :, k * 512:(k + 1) * 512], BW, w_in, O_WINQ + k * 512, 512)
        pst = Rot([P.ptile(f"wps{i}", [128, 512], F32) for i in range(2)])
        pss = Rot([P.ptile(f"wss{i}", [128, 512], F32) for i in range(3)])
        accs = Rot([P.ptile(f"wacc{i}", [128, 4, 128], F32) for i in range(2)])
        tmps = Rot([P.tile(f"wtmp{i}", [128, 512], F32) for i in range(3)])
        ets = Rot([P.tile(f"wet{i}", [128, 512], BF16) for i in range(4)])
        kT, BkT = P.tile("winkT", [64, 4, T], BF16)
        for h in range(4):
            def ev(tb, pt, Bp, h=h):
                vcopy(P, kT[:, h, tb * 512:(tb + 1) * 512], pt[0:64, :], [Bp], [BkT])
            proj_fm(P, cx, None, W, BW, 1024 + h * 64, 64, ev, pst=pst.items)
        vaug, Bva = P.tile("winv", [128, NT, 4, 65], BF16)
        memset(P, vaug[:, :, :, 64:65], 1.0, [Bva], eng="gpsimd")
        for i in range(NT):
            pt, Bp = pst.next()
            for c in range(DC):
                mm(P, pt[:, 0:256], cx.xT[:, c, i * 128:(i + 1) * 128], W[:, c, 1280:1536], c == 0, c == DC - 1,
                   [cx.BxT, BW], [Bp])
            act(P, vaug[:, i, :, 0:64], pt[:, 0:256].rearrange("p (h d) -> p h d", h=4), AF.Copy, [Bp], [Bva])
        sk, Bsk = P.tile("sinkb", [128, 16], F32)
        P.dma("sync", sk[:], cx.win_sink[l, :].partition_broadcast(128), writes=[Bsk])
        esk, Besk = P.tile("esink", [128, 16], F32)
        act(P, esk[:], sk[:], AF.Exp, [Bsk], [Besk])
        qTs = Rot([P.tile(f"winq{i}", [64, T], BF16) for i in range(2)])
        Gs = Rot([P.tile(f"winG{i}", [128, 1152], F32) for i in range(2)])
        ytoks = Rot([P.tile(f"wytok{i}", [128, NT, 128], BF16) for i in range(2)])
        yTs = Rot([P.tile(f"wyT{i}", [128, T], BF16) for i in range(2)])
        den, Bden = P.tile("wden", [128, 4], F32)
        for g in range(8):
            ytok, Byt = ytoks.next()
            for hh in range(2):
                hq = 2 * g + hh
                hkv = hq // 4
                qT, BqT = qTs.next()

                def ev(tb, pt, Bp, qT=qT, BqT=BqT):
                    act(P, qT[:, tb * 512:(tb + 1) * 512], pt[0:64, :], AF.Copy, [Bp], [BqT])
                proj_fm(P, cx, None, W, BW, hq * 64, 64, ev, pst=pst.items)
                G, BG = Gs.next()
                P.dma("sync", G[:], cx.tblG[hq], writes=[BG])
                for Q in range(4):
                    acc, Bacc = accs.next()
                    plan = []
                    for delta in range(-1, 5):
                        j = 4 * Q + delta
                        if j < 0 or j > 15:
                            continue
                        qlo = max(0, delta - 1)
                        qhi = min(3, delta + 1)
                        firsts = [s for s in range(qlo, qhi + 1) if j == max(4 * Q + s - 1, 0)]
                        lasts = [s for s in range(qlo, qhi + 1) if j == min(4 * Q + s + 1, 15)]
                        c0 = 128 * (4 - delta) + qlo * 128
                        ncol = (qhi - qlo + 1) * 128
                        plan.append((j, qlo, qhi, "tbl", (G[:, c0:c0 + ncol], BG), firsts, lasts))
                    attn_block(P, cx, acc, Bacc, 65,
                               lambda qlo, ncol, qT=qT, Q=Q: qT[:, Q * 512 + qlo * 128:Q * 512 + qlo * 128 + ncol], BqT,
                               lambda j, hkv=hkv: kT[:, hkv, j * 128:(j + 1) * 128], BkT,
                               lambda j, hkv=hkv: vaug[:, j, hkv, :], Bva, plan, pss, tmps, ets)
                    ts(P, den[:], acc[:, :, 64], esk[:, hq:hq + 1], ALU.add, [Bacc, Besk], [Bden])
                    P.op("vector", lambda e: e.reciprocal(out=den[:], in_=den[:]), reads=[Bden], writes=[Bden])
                    tt(P, ytok[:, Q * 4:(Q + 1) * 4, hh * 64:(hh + 1) * 64], acc[:, :, 0:64],
                       den[:].unsqueeze(2).to_broadcast([128, 4, 64]), ALU.mult, [Bacc, Bden], [Byt])
            yT, ByT = yTs.next()
            ytok_to_dram(P, cx, ytok, Byt, yT, ByT, pst, 8 + g)


def phase_diff(P, cx, l):
    w_in = cx.w_in[l]
    lam_init = 0.8 - 0.6 * math.exp(-0.3 * l)
    with P.scope():
        lamb, Blam = P.tile("lamb", [128, 4, 64], F32)
        P.dma("sync", lamb[:], cx.diff_lambda[l].partition_broadcast(128), writes=[Blam])
        lprod, Blp = P.tile("lprod", [128, 2, 64], F32)
        tt(P, lprod[:, 0, :], lamb[:, 0, :], lamb[:, 1, :], ALU.mult, [Blam], [Blp])
        tt(P, lprod[:, 1, :], lamb[:, 2, :], lamb[:, 3, :], ALU.mult, [Blam], [Blp])
        lsum, Bls = P.tile("lsum", [128, 2], F32)
        P.op("vector", lambda e: e.tensor_reduce(out=lsum[:], in_=lprod[:], axis=AX.X, op=ALU.add), reads=[Blp], writes=[Bls])
        act(P, lsum[:], lsum[:], AF.Exp, [Bls], [Bls])
        neglam, Bnl = P.tile("neglam", [128, 1], F32)
        tt(P, neglam[:], lsum[:, 1:2], lsum[:, 0:1], ALU.subtract, [Bls], [Bnl])
        ts(P, neglam[:], neglam[:], -lam_init, ALU.add, [Bnl], [Bnl])
        gsub, Bgs = P.tile("gsub", [128, 128], F32)
        P.dma("sync", gsub[:], cx.diff_subln_g[l, :].partition_broadcast(128), writes=[Bgs])
        ts(P, gsub[:], gsub[:], 1.0 - lam_init, ALU.mult, [Bgs], [Bgs])
        farb, Bfar = P.tile("farb_sb", [128, 16], F32)
        P.dma("sync", farb[:], cx.farb, writes=[Bfar])

        pst = Rot([P.ptile(f"dps{i}", [128, 512], F32) for i in range(1)])
        pss = Rot([P.ptile(f"dss{i}", [128, 512], F32) for i in range(3)])
        acc0, Bacc0 = P.ptile("dacc0", [128, 4, 256], F32)
        acc1, Bacc1 = P.ptile("dacc1", [128, 4, 256], F32)
        tmps = Rot([P.tile(f"dtmp{i}", [128, 512], F32) for i in range(3)])
        ets = Rot([P.tile(f"det{i}", [128, 512], BF16) for i in range(4)])
        Ws = Rot([P.tile(f"dW{i}", [128, DC, 384], BF16) for i in range(2)])
        qTs = Rot([P.tile(f"dq{i}", [64, 2, T], BF16) for i in range(2)])
        kTs = Rot([P.tile(f"dk{i}", [64, 2, T], BF16) for i in range(2)])
        vas = Rot([P.tile(f"dv{i}", [128, NT, 129], BF16) for i in range(2)])
        Gs = Rot([P.tile(f"dG{i}", [128, 1152], F32) for i in range(2)])
        ytoks = Rot([P.tile(f"dytok{i}", [128, NT, 128], BF16) for i in range(2)])
        yTs = Rot([P.tile(f"dyT{i}", [128, T], BF16) for i in range(2)])
        r0, Br0 = P.tile("dr0", [128, 4], F32)
        r1, Br1 = P.tile("dr1", [128, 4], F32)
        o0, Bo0 = P.tile("do0", [128, 4, 128], F32)
        o1, Bo1 = P.tile("do1", [128, 4, 128], F32)
        sq, Bsq = P.tile("dsq", [128, 4, 128], F32)
        ss, Bss = P.tile("dss_", [128, 4], F32)
        for h in range(8):
            W, BW = Ws.next()
            load_w(P, W[:, :, 0:128], BW, w_in, O_DQ + h * 128, 128)
            load_w(P, W[:, :, 128:256], BW, w_in, O_DK + h * 128, 128)
            load_w(P, W[:, :, 256:384], BW, w_in, O_DV + h * 128, 128)
            qT, BqT = qTs.next()
            kT, BkT = kTs.next()
            for (dst, Bdst, wc) in ((qT, BqT, 0), (kT, BkT, 128)):
                def ev(tb, pt, Bp, dst=dst, Bdst=Bdst):
                    vcopy(P, dst[:, 0, tb * 512:(tb + 1) * 512], pt[0:64, :], [Bp], [Bdst])
                    act(P, dst[:, 1, tb * 512:(tb + 1) * 512], pt[64:128, :], AF.Copy, [Bp], [Bdst])
                proj_fm(P, cx, None, W, BW, wc, 128, ev, pst=pst.items)
            vaug, Bva = vas.next()
            memset(P, vaug[:, :, 128:129], 1.0, [Bva], eng="gpsimd")
            for i in range(NT):
                pt, Bp = pst.next()
                for c in range(DC):
                    mm(P, pt[:, 0:128], cx.xT[:, c, i * 128:(i + 1) * 128], W[:, c, 256:384], c == 0, c == DC - 1,
                       [cx.BxT, BW], [Bp])
                vcopy(P, vaug[:, i, 0:128], pt[:, 0:128], [Bp], [Bva])
            G, BG = Gs.next()
            P.dma("sync", G[:], cx.tblG[16 + h], writes=[BG])
            ytok, Byt = ytoks.next()
            for Q in range(4):
                for comp, (acc, Bacc) in enumerate(((acc0, Bacc0), (acc1, Bacc1))):
                    plan = []
                    for j in range(16):
                        delta = j - 4 * Q
                        fl = [0, 1, 2, 3] if j == 0 else []
                        ll = [0, 1, 2, 3] if j == 15 else []
                        if -1 <= delta <= 4:
                            c0 = 128 * (4 - delta)
                            plan.append((j, 0, 3, "tbl", (G[:, c0:c0 + 512], BG), fl, ll))
                        else:
                            ci = 2 * h + (1 if delta > 0 else 0)
                            plan.append((j, 0, 3, "far", (farb[:, ci:ci + 1], Bfar), fl, ll))
                    attn_block(P, cx, acc, Bacc, 129,
                               lambda qlo, ncol, qT=qT, Q=Q, comp=comp: qT[:, comp, Q * 512:(Q + 1) * 512], BqT,
                               lambda j, kT=kT, comp=comp: kT[:, comp, j * 128:(j + 1) * 128], BkT,
                               lambda j, vaug=vaug: vaug[:, j, :], Bva, plan, pss, tmps, ets)
                P.op("vector", lambda e: e.reciprocal(out=r0[:], in_=acc0[:, :, 128]), reads=[Bacc0], writes=[Br0])
                P.op("vector", lambda e: e.reciprocal(out=r1[:], in_=acc1[:, :, 128]), reads=[Bacc1], writes=[Br1])
                ts(P, r1[:], r1[:], neglam[:, 0:1], ALU.mult, [Br1, Bnl], [Br1])
                tt(P, o0[:], acc0[:, :, 0:128], r0[:].unsqueeze(2).to_broadcast([128, 4, 128]), ALU.mult, [Bacc0, Br0], [Bo0])
                tt(P, o1[:], acc1[:, :, 0:128], r1[:].unsqueeze(2).to_broadcast([128, 4, 128]), ALU.mult, [Bacc1, Br1], [Bo1])
                tt(P, o0[:], o0[:], o1[:], ALU.add, [Bo0, Bo1], [Bo0], eng="gpsimd")
                act(P, sq[:], o0[:], AF.Square, [Bo0], [Bsq])
                P.op("vector", lambda e: e.tensor_reduce(out=ss[:], in_=sq[:], axis=AX.X, op=ALU.add), reads=[Bsq], writes=[Bss])
                rsqrt(P, cx, ss[:], ss[:], Bss, 1.0 / 128.0, 1e-5)
                tt(P, o0[:], o0[:], ss[:].unsqueeze(2).to_broadcast([128, 4, 128]), ALU.mult, [Bo0, Bss], [Bo0])
                tt(P, ytok[:, Q * 4:(Q + 1) * 4, :], o0[:], gsub[:].unsqueeze(1).to_broadcast([128, 4, 128]), ALU.mult,
                   [Bo0, Bgs], [Byt], eng="gpsimd")
            yT, ByT = yTs.next()
            ytok_to_dram(P, cx, ytok, Byt, yT, ByT, pst, 16 + h)


def phase_mem(P, cx, l):
    w_in = cx.w_in[l]
    with P.scope():
        Wkv, BWkv = P.tile("mWkv", [128, DC, 512], BF16)
        load_w(P, Wkv[:], BWkv, cx.mem_w_kv[l], 0, 512)
        Wq, BWq = P.tile("mWq", [128, DC, 256], BF16)
        load_w(P, Wq[:], BWq, w_in, O_MEMQ, 256)
        pst = Rot([P.ptile(f"mps{i}", [128, 512], F32) for i in range(2)])
        pss = Rot([P.ptile(f"mss{i}", [128, 512], F32) for i in range(3)])
        accs = Rot([P.ptile(f"macc{i}", [128, 4, 128], F32) for i in range(2)])
        ets = Rot([P.tile(f"met{i}", [128, 512], BF16) for i in range(4)])
        kmT, Bkm = P.tile("kmT", [64, 4, 256], BF16)
        for h in range(4):
            pt, Bp = pst.next()
            for c in range(DC):
                mm(P, pt[0:64, 0:256], Wkv[:, c, h * 64:(h + 1) * 64], cx.memT[:, c, :], c == 0, c == DC - 1,
                   [BWkv, cx.BmemT], [Bp])
            vcopy(P, kmT[:, h, :], pt[0:64, 0:256], [Bp], [Bkm])
        vm, Bvm = P.tile("vmaug", [128, 2, 4, 65], BF16)
        memset(P, vm[:, :, :, 64:65], 1.0, [Bvm], eng="gpsimd")
        for mt in range(2):
            pt, Bp = pst.next()
            for c in range(DC):
                mm(P, pt[:, 0:256], cx.memT[:, c, mt * 128:(mt + 1) * 128], Wkv[:, c, 256:512], c == 0, c == DC - 1,
                   [cx.BmemT, BWkv], [Bp])
            vcopy(P, vm[:, mt, :, 0:64], pt[:, 0:256].rearrange("p (h d) -> p h d", h=4), [Bp], [Bvm])
        qTs = Rot([P.tile(f"memq{i}", [64, T], BF16) for i in range(2)])
        ytoks = Rot([P.tile(f"mytok{i}", [128, NT, 128], BF16) for i in range(2)])
        yTs = Rot([P.tile(f"myT{i}", [128, T], BF16) for i in range(2)])
        den, Bden = P.tile("mden", [128, 4], F32)
        for g in range(2):
            ytok, Byt = ytoks.next()
            for hh in range(2):
                h = 2 * g + hh
                qT, BqT = qTs.next()

                def ev(tb, pt, Bp, qT=qT, BqT=BqT):
                    act(P, qT[:, tb * 512:(tb + 1) * 512], pt[0:64, :], AF.Copy, [Bp], [BqT])
                proj_fm(P, cx, None, Wq, BWq, h * 64, 64, ev, pst=pst.items)
                for Q in range(4):
                    acc, Bacc = accs.next()
                    plan = [(mt, 0, 3, "none", None, [0, 1, 2, 3] if mt == 0 else [], [0, 1, 2, 3] if mt == 1 else [])
                            for mt in range(2)]
                    attn_block(P, cx, acc, Bacc, 65,
                               lambda qlo, ncol, qT=qT, Q=Q: qT[:, Q * 512:(Q + 1) * 512], BqT,
                               lambda j, h=h: kmT[:, h, j * 128:(j + 1) * 128], Bkm,
                               lambda j, h=h: vm[:, j, h, :], Bvm, plan, pss, None, ets)
                    P.op("vector", lambda e, acc=acc: e.reciprocal(out=den[:], in_=acc[:, :, 64]), reads=[Bacc], writes=[Bden])
                    tt(P, ytok[:, Q * 4:(Q + 1) * 4, hh * 64:(hh + 1) * 64], acc[:, :, 0:64],
                       den[:].unsqueeze(2).to_broadcast([128, 4, 64]), ALU.mult, [Bacc, Bden], [Byt])
            yT, ByT = yTs.next()
            ytok_to_dram(P, cx, ytok, Byt, yT, ByT, pst, 24 + g)


def _t5_bucket_np(rel):
    half, exact = 16, 8
    n = np.abs(rel)
    nf = np.maximum(n, 1).astype(np.float32)
    large = exact + (np.log(nf / exact) / math.log(128 / exact) * (half - exact)).astype(np.int32)
    large = np.minimum(large, half - 1)
    return np.where(rel > 0, half, 0) + np.where(n < exact, n, large)


def host_tables(rel_bias):
    p = np.arange(128)[:, None]
    c = np.arange(1152)[None, :]
    rel = p - c + 512
    idx = _t5_bucket_np(rel)
    G = np.ascontiguousarray(np.transpose(rel_bias[idx], (2, 0, 1))).astype(np.float32)
    mask = (np.abs(rel) > 128)
    G[:16][:, mask] = -30000.0
    farb = np.empty((128, 16), np.float32)
    for h in range(8):
        farb[:, 2 * h] = rel_bias[15, 16 + h]
        farb[:, 2 * h + 1] = rel_bias[31, 16 + h]
    return G, farb


def declare_inputs(nc, cx, names_shapes):
    for nm, shp in names_shapes:
        setattr(cx, nm, nc.dram_tensor(nm, list(shp), F32, kind="ExternalInput").ap())


INPUT_SHAPES = [
    ("x", (T, D)), ("mem", (256, D)), ("w_in", (NL, D, 12288)), ("rwkv_w_up", (NL, 2, 64, 1024)),
    ("rwkv_a_up", (NL, 2, 64, 1024)), ("rwkv_g_up", (NL, 128, 1024)), ("win_sink", (NL, 16)),
    ("diff_lambda", (NL, 4, 64)), ("diff_subln_g", (NL, 128)), ("mem_w_kv", (NL, D, 512)),
    ("w_branch", (NL, 3328, D)), ("w_out", (NL, D, D)), ("ln1_g", (NL, D)), ("ln1_b", (NL, D)),
    ("router", (NL, D, 16)), ("exp_w_gate", (NL, 16, D, 2048)), ("exp_w_up", (NL, 16, D, 2048)),
    ("exp_w_down", (NL, 16, 2048, D)), ("ln2_g", (NL, D)), ("ln2_b", (NL, D)),
    ("tblG", (24, 128, 1152)), ("farb", (128, 16)), ("colp", (NL, 128, 160)),
]


def build_memT(P, cx):
    with P.scope():
        xin = [P.tile(f"min{i}", [128, D], F32) for i in range(2)]
        pst = [P.ptile(f"mtp{i}", [128, 512], F32) for i in range(2)]
        k = 0
        for i in range(2):
            xt_, Bx = xin[i]
            P.dma("sync", xt_[:], cx.mem[i * 128:(i + 1) * 128, :], writes=[Bx])
            for half in range(2):
                pt, Bp = pst[k % 2]
                k += 1
                for j in range(4):
                    c = half * 4 + j
                    tr(P, pt[:, j * 128:(j + 1) * 128], xt_[:, c * 128:(c + 1) * 128], cx.ident_f[:],
                       [Bx, cx.Bident_f], [Bp])
                vcopy(P, cx.memT[:, half * 4:half * 4 + 4, i * 128:(i + 1) * 128],
                      pt[:].rearrange("p (j t) -> p j t", j=4), [Bp], [cx.BmemT])


BR_CHUNKS = [(0, 8), (8, 16), (16, 24), (24, 26)]


def layer_norm_tile(P, cx, pre, Bpre, gb, Bgb, bb, Bbb, stats, Bst, mv, Bmv, out, Bout):
    for k in range(2):
        P.op("vector", lambda e, k=k: e.bn_stats(out=stats[:, k, :], in_=pre[:, k * 512:(k + 1) * 512]),
             reads=[Bpre], writes=[Bst])
    P.op("vector", lambda e: e.bn_aggr(out=mv[:, 0:2], in_=stats[:].rearrange("p a b -> p (a b)")), reads=[Bst], writes=[Bmv])
    ts(P, mv[:, 2:3], mv[:, 1:2], 1.0, ALU.mult, [Bmv], [Bmv], s2=1e-5, op1=ALU.add)
    act(P, mv[:, 2:3], mv[:, 2:3], AF.Sqrt, [Bmv], [Bmv])
    P.op("vector", lambda e: e.reciprocal(out=mv[:, 2:3], in_=mv[:, 2:3]), reads=[Bmv], writes=[Bmv])
    ts(P, pre, pre, mv[:, 0:1], ALU.subtract, [Bpre, Bmv], [Bpre], s2=mv[:, 2:3], op1=ALU.mult)
    tt(P, pre, pre, gb, ALU.mult, [Bpre, Bgb], [Bpre], eng="gpsimd")
    tt(P, out, pre, bb, ALU.add, [Bpre, Bbb], [Bout])


def phase_merge(P, cx, l, xres_dram):
    w_in = cx.w_in[l]
    with P.scope():
        Wo, BWo = P.tile("Wo", [128, DC, 1024], BF16)
        load_w(P, Wo[:, :, 0:512], BWo, cx.w_out[l], 0, 512)
        load_w(P, Wo[:, :, 512:1024], BWo, cx.w_out[l], 512, 512)
        Rb, BRb = P.tile("Rb", [128, DC, 16], BF16)
        load_w(P, Rb[:], BRb, cx.router[l], 0, 16)
        g1, Bg1 = P.tile("g1b", [128, D], F32)
        b1, Bb1 = P.tile("b1b", [128, D], F32)
        P.dma("sync", g1[:], cx.ln1_g[l, :].partition_broadcast(128), writes=[Bg1])
        P.dma("sync", b1[:], cx.ln1_b[l, :].partition_broadcast(128), writes=[Bb1])
        ybr, Bybr = P.tile("ybr", [128, 26, 1024], BF16)
        mT, BmT = P.tile("mergedT", [128, DC, 1024], BF16)
        Wbs = Rot([P.tile(f"Wb{i}", [128, 26, 128], BF16) for i in range(2)])
        Wgs = Rot([P.tile(f"Wg{i}", [128, DC, 4, 128], BF16) for i in range(2)])
        psZ = Rot([P.ptile(f"psZ{i}", [128, 512], F32) for i in range(2)])
        psG = Rot([P.ptile(f"psG{i}", [128, 512], F32) for i in range(2)])
        psO, BpsO = P.ptile("psO", [128, 1024], F32)
        psT, BpsT = P.ptile("psT", [128, 512], F32)
        psR, BpsR = P.ptile("psR", [128, 512], F32)
        sgs = Rot([P.tile(f"sg{i}", [128, 512], F32) for i in range(2)])
        macc, Bmacc = P.tile("macc", [128, 512], F32)
        mtmp, Bmtmp = P.tile("mtmp", [128, 512], F32)
        xrs = Rot([P.tile(f"xr{i}", [128, D], F32) for i in range(2)])
        pres = Rot([P.tile(f"pre{i}", [128, D], F32) for i in range(2)])
        x1s = Rot([P.tile(f"x1o{i}", [128, D], F32) for i in range(2)])
        x1b, Bx1b = P.tile("x1b_t", [128, D], BF16)
        x1T, Bx1T = P.tile("x1T_t", [128, DC, 128], BF16)
        stats, Bst = P.tile("lnstats", [128, 2, 6], F32)
        mv, Bmv = P.tile("lnmv", [128, 4], F32)
        sm, Bsm = P.tile("rsm", [128, 4], F32)
        ex, Bex = P.tile("rex", [128, 16], F32)
        ybr_src = cx.ybrT.rearrange("(c p) t -> p c t", p=128)
        for hf in range(2):
            t0 = hf * 1024
            for c0 in range(0, 26, 13):
                P.dma("sync", ybr[:, c0:c0 + 13, :], ybr_src[:, c0:c0 + 13, t0:t0 + 1024], reads=cx.BybrT, writes=[Bybr])
            for dc in range(DC):
                Wb, BWb = Wbs.next()
                load_w(P, Wb[:], BWb, cx.w_branch[l], dc * 128, 128, nchunks=26)
                Wg, BWg = Wgs.next()
                for b in range(4):
                    load_w(P, Wg[:, :, b, :], BWg, w_in, O_GATE + b * 1024 + dc * 128, 128)
                for tb in range(2):
                    for b in range(4):
                        ca, cb = BR_CHUNKS[b]
                        pz, Bpz = psZ.next()
                        for c in range(ca, cb):
                            mm(P, pz[:], Wb[:, c, :], ybr[:, c, tb * 512:(tb + 1) * 512], c == ca, c == cb - 1, [BWb, Bybr], [Bpz])
                        pg, Bpg = psG.next()
                        for c in range(DC):
                            mm(P, pg[:], Wg[:, c, b, :], cx.xT[:, c, t0 + tb * 512:t0 + (tb + 1) * 512], c == 0, c == DC - 1,
                               [BWg, cx.BxT], [Bpg])
                        sg, Bsg = sgs.next()
                        act(P, sg[:], pg[:], AF.Sigmoid, [Bpg], [Bsg])
                        if b == 0:
                            tt(P, macc[:], pz[:], sg[:], ALU.mult, [Bpz, Bsg], [Bmacc])
                        else:
                            tt(P, mtmp[:], pz[:], sg[:], ALU.mult, [Bpz, Bsg], [Bmtmp])
                            if b < 3:
                                tt(P, macc[:], macc[:], mtmp[:], ALU.add, [Bmacc, Bmtmp], [Bmacc], eng="gpsimd")
                            else:
                                tt(P, mT[:, dc, tb * 512:(tb + 1) * 512], macc[:], mtmp[:], ALU.add, [Bmacc, Bmtmp], [BmT],
                                   eng="gpsimd")
            for i in range(8):
                ti = hf * 8 + i
                for dh in range(2):
                    for dc in range(DC):
                        mm(P, psO[:, dh * 512:(dh + 1) * 512], mT[:, dc, i * 128:(i + 1) * 128], Wo[:, dc, dh * 512:(dh + 1) * 512],
                           dc == 0, dc == DC - 1, [BmT, BWo], [BpsO])
                xr, Bxr = xrs.next()
                P.dma("sync", xr[:], xres_dram[ti * 128:(ti + 1) * 128, :], writes=[Bxr])
                pre, Bpre = pres.next()
                stt(P, pre[:], xr[:], DN_ALPHA, psO[:], ALU.mult, ALU.add, [Bxr, BpsO], [Bpre])
                x1, Bx1 = x1s.next()
                layer_norm_tile(P, cx, pre[:], Bpre, g1[:], Bg1, b1[:], Bb1, stats, Bst, mv, Bmv, x1[:], Bx1)
                P.dma("sync", cx.x1res[ti * 128:(ti + 1) * 128, :], x1[:], reads=[Bx1], writes=[cx.Bx1res])
                act(P, x1b[:], x1[:], AF.Copy, [Bx1], [Bx1b])
                ptb = psT[:].bitcast(BF16)
                for c in range(DC):
                    tr(P, ptb[:, c * 128:(c + 1) * 128], x1b[:, c * 128:(c + 1) * 128], cx.ident_b[:], [Bx1b, cx.Bident_b], [BpsT])
                vcopy(P, x1T[:], ptb[:].rearrange("p (c t) -> p c t", c=DC), [BpsT], [Bx1T])
                for c in range(DC):
                    mm(P, psR[:, 0:16], x1T[:, c, :], Rb[:, c, :], c == 0, c == DC - 1, [Bx1T, BRb], [BpsR])
                P.op("vector", lambda e: e.tensor_reduce(out=sm[:, 0:1], in_=psR[:, 0:16], axis=AX.X, op=ALU.max),
                     reads=[BpsR], writes=[Bsm])
                ts(P, sm[:, 1:2], sm[:, 0:1], -1.0, ALU.mult, [Bsm], [Bsm])
                act(P, ex[:], psR[:, 0:16], AF.Exp, [BpsR, Bsm], [Bex], bias=sm[:, 1:2], scale=1.0)
                P.op("vector", lambda e: e.tensor_reduce(out=sm[:, 2:3], in_=ex[:], axis=AX.X, op=ALU.add), reads=[Bex], writes=[Bsm])
                P.op("vector", lambda e: e.reciprocal(out=sm[:, 2:3], in_=sm[:, 2:3]), reads=[Bsm], writes=[Bsm])
                ts(P, cx.aff_tok[:, ti, :], ex[:], sm[:, 2:3], ALU.mult, [Bex, Bsm], [cx.Baff_tok])


def phase_moe(P, cx, l, out_dram, Bout):
    with P.scope():
        acc, Bacc = P.tile("moe_acc", [128, NT, D], F32)
        x1b, Bx1b = P.tile("moe_x1b", [128, NT, D], BF16)
        posb, Bposb = P.tile("tk_posb", [16, T], BF16)
        pgen = Rot([P.ptile(f"pgen{i}", [128, 512], F32) for i in range(2)])
        ptok, Bptok = P.tile("tk_ptok", [128, NT, 16], F32)
        Eall, BEall = P.tile("tk_E", [16, 16, 128], BF16)
        jidx, Bjidx = P.tile("tk_jidx", [128, 2], F32)
        with P.scope():
            xrs = Rot([P.tile(f"mxr{i}", [128, D], F32) for i in range(2)])
            for i in range(NT):
                xr, Bxr = xrs.next()
                P.dma("sync", xr[:], cx.x1res[i * 128:(i + 1) * 128, :], reads=[cx.Bx1res], writes=[Bxr])
                act(P, acc[:, i, :], xr[:], AF.Copy, [Bxr], [Bacc], scale=DN_ALPHA)
                vcopy(P, x1b[:, i, :], xr[:], [Bxr], [Bx1b])
            affT, BaffT = P.tile("affT", [16, T], F32)
            for q4 in range(4):
                pt, Bp = pgen.next()
                for j in range(4):
                    ti = q4 * 4 + j
                    tr(P, pt[0:16, j * 128:(j + 1) * 128], cx.aff_tok[:, ti, :], cx.ident_f[:], [cx.Baff_tok, cx.Bident_f], [Bp])
                vcopy(P, affT[0:16, q4 * 512:(q4 + 1) * 512], pt[0:16, :], [Bp], [BaffT])
            lo, Blo = P.tile("tk_lo", [16, 1], F32)
            hi, Bhi = P.tile("tk_hi", [16, 1], F32)
            mid, Bmid = P.tile("tk_mid", [16, 1], F32)
            cnt, Bcnt = P.tile("tk_cnt", [16, 1], F32)
            ge, Bge = P.tile("tk_ge", [16, 1], F32)
            dd, Bdd = P.tile("tk_d", [16, 1], F32)
            junk, Bjunk = P.tile("tk_junk", [16, T], F32)
            memset(P, lo[:], 0.0, [Blo])
            memset(P, hi[:], 1.0, [Bhi])
            for it in range(34):
                tt(P, mid[:], lo[:], hi[:], ALU.add, [Blo, Bhi], [Bmid])
                ts(P, mid[:], mid[:], 0.5, ALU.mult, [Bmid], [Bmid])
                ts(P, junk[:], affT[0:16, :], mid[:, 0:1], ALU.is_ge, [BaffT, Bmid], [Bjunk, Bcnt], s2=0.0, op1=ALU.add, accum_out=cnt[:])
                ts(P, ge[:], cnt[:], 255.5, ALU.is_ge, [Bcnt], [Bge])
                tt(P, dd[:], mid[:], lo[:], ALU.subtract, [Bmid, Blo], [Bdd])
                stt(P, lo[:], dd[:], ge[:, 0:1], lo[:], ALU.mult, ALU.add, [Bdd, Bge, Blo], [Blo])
                tt(P, dd[:], hi[:], mid[:], ALU.subtract, [Bhi, Bmid], [Bdd])
                stt(P, hi[:], dd[:], ge[:, 0:1], mid[:], ALU.mult, ALU.add, [Bdd, Bge, Bmid], [Bhi])
            mask, Bmask = P.tile("tk_mask", [16, T], F32)
            ts(P, mask[:], affT[0:16, :], lo[:, 0:1], ALU.is_ge, [BaffT, Blo], [Bmask])
            ones16, Bo16 = P.tile("tk_ones", [16, T], F32)
            memset(P, ones16[:], 1.0, [Bo16])
            posm, Bposm = P.tile("tk_posm", [16, T], F32)
            P.op("vector", lambda e: e.tensor_tensor_scan(out=posm[:], data0=ones16[:], data1=mask[:], initial=0.0, op0=ALU.mult, op1=ALU.add),
                 reads=[Bo16, Bmask], writes=[Bposm])
            tt(P, posm[:], posm[:], mask[:], ALU.mult, [Bposm, Bmask], [Bposm])
            ts(P, posm[:], posm[:], -1.0, ALU.add, [Bposm], [Bposm])
            vcopy(P, posb[:], posm[:], [Bposm], [Bposb])
            for half in range(2):
                pt, Bp = pgen.next()
                for j in range(8):
                    i = half * 8 + j
                    tr(P, pt[:, j * 16:(j + 1) * 16], posm[0:16, i * 128:(i + 1) * 128], cx.ident_f[0:16, 0:16], [Bposm, cx.Bident_f], [Bp])
                vcopy(P, ptok[:, half * 8:(half + 1) * 8, :], pt[:, 0:128].rearrange("p (j e) -> p j e", j=8), [Bp], [Bptok])
            vcopy(P, Eall[:], cx.ident_b[0:16, 0:16].unsqueeze(2).to_broadcast([16, 16, 128]), [cx.Bident_b], [BEall])
            vcopy(P, jidx[:, 0:1], cx.pidx[:], [cx.Bpidx], [Bjidx])
            ts(P, jidx[:, 1:2], cx.pidx[:], 128.0, ALU.add, [cx.Bpidx], [Bjidx])

        with P.scope():
            Sel, BSel = P.tile("Sel", [128, NT, 256], BF16)
            SelT, BSelT = P.tile("SelT", [128, 2, T], BF16)
            xeT, BxeT = P.tile("xeT", [128, DC, 256], BF16)
            hT, BhT = P.tile("hT", [128, 16, 256], BF16)
            ye, Bye = P.tile("ye_sb", [128, 2, D], BF16)
            Wgs = Rot([P.tile(f"eWg{i}", [128, DC, 512], BF16) for i in range(2)])
            Wus = Rot([P.tile(f"eWu{i}", [128, DC, 512], BF16) for i in range(2)])
            Wds = Rot([P.tile(f"eWd{i}", [128, 4, D], BF16) for i in range(2)])
            pgu = Rot([P.ptile(f"pgu{i}", [128, 512], F32) for i in range(2)])
            psY = [[P.ptile(f"psY{i}_{j}", [128, 512], F32) for j in range(2)] for i in range(2)]
            pscat = Rot(pgen.items + [psY[0][0], psY[0][1], psY[1][0], psY[1][1]])
            sgs = Rot([P.tile(f"esg{i}", [128, 256], F32) for i in range(2)])
            for e in range(16):
                tt(P, Sel[:], cx.colidx[:, 0:256].unsqueeze(1).to_broadcast([128, NT, 256]),
                   ptok[:, :, e:e + 1].to_broadcast([128, NT, 256]), ALU.is_equal, [cx.Bcolidx, Bptok], [BSel])
                for c in range(DC):
                    pt, Bp = pgen.next()
                    for i in range(NT):
                        mm(P, pt[:, 0:256], x1b[:, i, c * 128:(c + 1) * 128], Sel[:, i, :], i == 0, i == NT - 1, [Bx1b, BSel], [Bp])
                    if c % 2 == 0:
                        vcopy(P, xeT[:, c, :], pt[:, 0:256], [Bp], [BxeT])
                    else:
                        act(P, xeT[:, c, :], pt[:, 0:256], AF.Copy, [Bp], [BxeT])
                for tb in range(4):
                    pt, Bp = pgen.next()
                    mm(P, pt[:], Eall[:, e, :], posb[0:16, tb * 512:(tb + 1) * 512], True, True, [BEall, Bposb], [Bp])
                    for jt in range(2):
                        ts(P, SelT[:, jt, tb * 512:(tb + 1) * 512], pt[:], jidx[:, jt:jt + 1], ALU.is_equal, [Bp, Bjidx], [BSelT])
                for fq in range(4):
                    Wg, BWg = Wgs.next()
                    Wu, BWu = Wus.next()
                    Wd, BWd = Wds.next()
                    load_w(P, Wg[:], BWg, cx.exp_w_gate[l, e], fq * 512, 512)
                    load_w(P, Wu[:], BWu, cx.exp_w_up[l, e], fq * 512, 512)
                    load_w(P, Wd[:], BWd, cx.exp_w_down[l, e], 0, 1024, r0=fq * 512, nchunks=4)
                    for fc in range(4):
                        F = fq * 4 + fc
                        pG, BpG = pgu.next()
                        for c in range(DC):
                            mm(P, pG[:, 0:256], Wg[:, c, fc * 128:(fc + 1) * 128], xeT[:, c, :], c == 0, c == DC - 1, [BWg, BxeT], [BpG])
                        pU, BpU = pgu.next()
                        for c in range(DC):
                            mm(P, pU[:, 0:256], Wu[:, c, fc * 128:(fc + 1) * 128], xeT[:, c, :], c == 0, c == DC - 1, [BWu, BxeT], [BpU])
                        sg, Bsg = sgs.next()
                        act(P, sg[:], pG[:, 0:256], AF.Silu, [BpG], [Bsg])
                        tt(P, hT[:, F, :], pU[:, 0:256], sg[:], ALU.mult, [BpU, Bsg], [BhT])
                    for jt in range(2):
                        for dh in range(2):
                            pY, BpY = psY[jt][dh]
                            for fc in range(4):
                                mm(P, pY[:], hT[:, fq * 4 + fc, jt * 128:(jt + 1) * 128],
                                   Wd[:, fc, dh * 512:(dh + 1) * 512], fq == 0 and fc == 0, fq == 3 and fc == 3, [BhT, BWd], [BpY])
                for jt in range(2):
                    for dh in range(2):
                        pY, BpY = psY[jt][dh]
                        if dh == 0:
                            vcopy(P, ye[:, jt, dh * 512:(dh + 1) * 512], pY[:], [BpY], [Bye])
                        else:
                            act(P, ye[:, jt, dh * 512:(dh + 1) * 512], pY[:], AF.Copy, [BpY], [Bye])
                for i in range(NT):
                    for dh in range(2):
                        pt, Bp = pscat.next()
                        for jt in range(2):
                            mm(P, pt[:], SelT[:, jt, i * 128:(i + 1) * 128], ye[:, jt, dh * 512:(dh + 1) * 512], jt == 0, jt == 1,
                               [BSelT, Bye], [Bp])
                        stt(P, acc[:, i, dh * 512:(dh + 1) * 512], pt[:], cx.aff_tok[:, i, e:e + 1], acc[:, i, dh * 512:(dh + 1) * 512],
                            ALU.mult, ALU.add, [Bp, cx.Baff_tok, Bacc], [Bacc])
        g2, Bg2 = P.tile("g2b", [128, D], F32)
        b2, Bb2 = P.tile("b2b", [128, D], F32)
        P.dma("sync", g2[:], cx.ln2_g[l, :].partition_broadcast(128), writes=[Bg2])
        P.dma("sync", b2[:], cx.ln2_b[l, :].partition_broadcast(128), writes=[Bb2])
        stats, Bst = P.tile("ln2stats", [128, 2, 6], F32)
        mv, Bmv = P.tile("ln2mv", [128, 4], F32)
        outs = Rot([P.tile(f"x2o{i}", [128, D], F32) for i in range(2)])
        for i in range(NT):
            o, Bo = outs.next()
            layer_norm_tile(P, cx, acc[:, i, :], Bacc, g2[:], Bg2, b2[:], Bb2, stats, Bst, mv, Bmv, o[:], Bo)
            P.dma("sync", out_dram[i * 128:(i + 1) * 128, :], o[:], reads=[Bo], writes=[Bout])


CH = 64
NCH = T // CH
DBG = {"groups": 8, "stop": 99}


def project_shift(P, cx, rc, W, BW, wcol, ch, dst, Bdst, pst):
    raw, Braw = rc.raw, rc.Braw

    def ev(tb, pt, Bp):
        act(P, raw[:, tb * 512:(tb + 1) * 512], pt[:, :], AF.Copy, [Bp], [Braw])
    proj_fm(P, cx, None, W, BW, wcol, 128, ev, pst=pst)
    ts(P, dst[:, :], raw[:, :], rc.cmix[:, ch:ch + 1], ALU.mult, [Braw, rc.Bcmix], [Bdst])
    stt(P, dst[:, 1:T], raw[:, 0:T - 1], rc.colp[:, ch:ch + 1], dst[:, 1:T], ALU.mult, ALU.add, [Braw, rc.Bcolp, Bdst], [Bdst])
    stt(P, dst[:, 0:T - 1], raw[:, 1:T], rc.colp[:, 26 + ch:27 + ch], dst[:, 0:T - 1], ALU.mult, ALU.add,
        [Braw, rc.Bcolp, Bdst], [Bdst])


def phase_rwkv(P, cx, l):
    w_in = cx.w_in[l]
    rc = Ctx()
    with P.scope():
        rc.colp, rc.Bcolp = P.tile("colp", [128, 160], F32)
        P.dma("sync", rc.colp[:], cx.colp[l], writes=[rc.Bcolp])
        colp = rc.colp
        Bcolp = rc.Bcolp
        rc.cmix, rc.Bcmix = P.tile("cmix", [128, 26], F32)
        tt(P, rc.cmix[:], colp[:, 0:26], colp[:, 26:52], ALU.add, [Bcolp], [rc.Bcmix])
        ts(P, rc.cmix[:], rc.cmix[:], -1.0, ALU.mult, [rc.Bcmix], [rc.Bcmix], s2=1.0, op1=ALU.add)
        omka, Bomka = P.tile("omka", [128, 8], F32)
        ts(P, omka[:], colp[:, 92:100], -1.0, ALU.mult, [Bcolp], [Bomka], s2=1.0, op1=ALU.add)
        d64i, Bd64i = P.tile("d64i", [128, 64], I32)
        P.op("gpsimd", lambda e: e.iota(d64i[0:64, :], pattern=[[1, 64]], base=0, channel_multiplier=-1), writes=[Bd64i])
        P.op("gpsimd", lambda e: e.iota(d64i[64:128, :], pattern=[[1, 64]], base=0, channel_multiplier=-1), writes=[Bd64i])
        d64, Bd64 = P.tile("d64f", [128, 64], F32)
        vcopy(P, d64[:], d64i[:], [Bd64i], [Bd64])
        mk, Bmk = P.tile("mk4", [128, 4, 64], F32)
        ts(P, mk[:, 0, :], d64[:], 0.0, ALU.is_gt, [Bd64], [Bmk])
        ts(P, mk[:, 1, :], d64[:], 0.0, ALU.is_ge, [Bd64], [Bmk])
        ts(P, mk[:, 2, :], d64[:], 0.0, ALU.is_lt, [Bd64], [Bmk])
        ts(P, mk[:, 3, :], d64[:], 0.0, ALU.is_le, [Bd64], [Bmk])
        identblk, Bidb = P.tile("identblk", [128, 64], F32)
        ts(P, identblk[:], d64[:], 0.0, ALU.is_equal, [Bd64], [Bidb])
        maskA, BmaskA = P.tile("maskA", [128, 2, 2, 4, 64], F32)
        maskB, BmaskB = P.tile("maskB", [128, 2, 2, 64], F32)
        for z in range(2):
            st_i, in_i = (0, 1) if z == 0 else (2, 3)
            ot_i = 2 if z == 0 else 0
            for hh in range(2):
                ts(P, maskA[:, z, hh, 0, :], mk[:, st_i, :], -1.0, ALU.mult, [Bmk], [BmaskA])
                ts(P, maskA[:, z, hh, 1, :], mk[:, in_i, :], -1.0, ALU.mult, [Bmk], [BmaskA])
                vcopy(P, maskA[:, z, hh, 2, :], mk[:, st_i, :], [Bmk], [BmaskA])
                vcopy(P, maskA[:, z, hh, 3, :], mk[:, in_i, :], [Bmk], [BmaskA])
                ts(P, maskB[:, z, hh, :], mk[:, ot_i, :], -1.0, ALU.mult, [Bmk], [BmaskB])
        rst, Brst = P.tile("rst", [128, 512], F32)
        memset(P, rst[:], 1.0, [Brst])
        memset(P, rst[:].rearrange("p (n c) -> p n c", c=CH)[:, :, 0:1], 0.0, [Brst])
        wa_up, Bwa = P.tile("wa_up", [128, 2, 1024], BF16)
        for z in range(2):
            P.dma("gpsimd", wa_up[0:64, z, :], cx.rwkv_w_up[l, z], writes=[Bwa])
            P.dma("gpsimd", wa_up[64:128, z, :], cx.rwkv_a_up[l, z], writes=[Bwa])
        g_up, Bgup = P.tile("g_up", [128, 1024], BF16)
        P.dma("gpsimd", g_up[:], cx.rwkv_g_up[l], writes=[Bgup])
        lin, Blin = P.tile("lin", [128, T], BF16)
        sdg, Bsdg = P.tile("sdg", [128, T], BF16)
        with P.scope():
            rc.raw, rc.Braw = P.tile("raw", [128, T], F32)
            Wl, BWl = P.tile("Wl", [128, DC, 256], BF16)
            load_w(P, Wl[:], BWl, w_in, 3072, 256)
            pst = [P.ptile(f"lps{i}", [128, 512], F32) for i in range(2)]
            sh, Bsh = P.tile("lsh", [128, T], F32)
            project_shift(P, cx, rc, Wl, BWl, 0, 24, sh, Bsh, pst)
            act(P, lin[0:64, :], sh[0:64, :], AF.Tanh, [Bsh], [Blin])
            vcopy(P, lin[64:128, :], sh[64:128, :], [Bsh], [Blin])
            project_shift(P, cx, rc, Wl, BWl, 128, 25, sh, Bsh, pst)
            act(P, sdg[:], sh[:], AF.Sigmoid, [Bsh], [Bsdg])
        for g in range(DBG["groups"]):
            if DBG["stop"] < 1:
                break
            rwkv_group(P, cx, rc, l, g, maskA, BmaskA, maskB, BmaskB, identblk, Bidb, rst, Brst, omka, Bomka,
                       wa_up, Bwa, g_up, Bgup, lin, Blin, sdg, Bsdg)


def rwkv_group(P, cx, rc, l, g, maskA, BmaskA, maskB, BmaskB, identblk, Bidb, rst, Brst, omka, Bomka,
               wa_up, Bwa, g_up, Bgup, lin, Blin, sdg, Bsdg):
    w_in = cx.w_in[l]
    colp, Bcolp = rc.colp, rc.Bcolp
    gc = slice(g * 128, (g + 1) * 128)
    with P.scope():
        AR = [P.tile(f"AR{z}", [128, NCH, 2, CH], BF16) for z in range(2)]
        KT = [P.tile(f"KT{z}", [128, T], BF16) for z in range(2)]
        BT = [P.tile(f"BT{z}", [128, T], BF16) for z in range(2)]
        Ktok = [P.tile(f"Ktok{z}", [128, NCH, CH], BF16) for z in range(2)]
        nBtok = [P.tile(f"nBtok{z}", [128, NCH, CH], BF16) for z in range(2)]
        Vtok, BVtok = P.tile("Vtok", [128, NCH, CH], BF16)
        gamC = [P.tile(f"gamC{z}", [128, NCH], F32) for z in range(2)]
        bonus, Bbonus = P.tile("bonus", [128, T], F32)
        gate, Bgate = P.tile("gate_g", [128, T], BF16)
        with P.scope():
            Wr, BWr = P.tile("Wr", [128, DC, 3, 128], BF16)
            for j in range(3):
                load_w(P, Wr[:, :, j, :], BWr, w_in, j * 1024 + g * 128, 128)
            pst = [P.ptile(f"gps{i}", [128, 512], F32) for i in range(2)]
            psA = Rot([P.ptile(f"gpa{i}", [128, 512], F32) for i in range(3)])
            psTb = Rot([P.ptile(f"gpt{i}", [128, 512], F32) for i in range(2)])
            r_s, Br = P.tile("r_s", [128, T], F32)
            k_s, Bk = P.tile("k_s", [128, T], F32)
            v_s, Bv = P.tile("v_s", [128, T], F32)
            Wr2 = Wr[:].rearrange("p c j n -> p c (j n)")
            with P.scope():
                rc.raw, rc.Braw = P.tile("raw", [128, T], F32)
                project_shift(P, cx, rc, Wr2, BWr, 0, g, r_s, Br, pst)
                project_shift(P, cx, rc, Wr2, BWr, 128, 8 + g, k_s, Bk, pst)
                project_shift(P, cx, rc, Wr2, BWr, 256, 16 + g, v_s, Bv, pst)
            kkc = colp[:, 84 + g:85 + g]
            kac = colp[:, 92 + g:93 + g]
            rkc = colp[:, 100 + g:101 + g]
            tmp = {}
            for nm in ("sq", "rn", "kk", "sg", "az", "cs", "incl", "e1", "e2", "t1", "kd", "kd0", "kka"):
                tmp[nm] = P.tile("g_" + nm, [128, 512], F32)
            tmp["u"] = tmp["sq"]
            tmpr = {nm: Rot([tmp[nm], P.tile("g2_" + nm, [128, 512], F32)]) for nm in ("sg", "az", "cs", "incl", "e1", "e2", "t1", "kka")}
            tb16 = Rot([P.tile(f"g_tb16_{i}", [128, 512], BF16) for i in range(2)])
            for tb in range(4):
                sl = slice(tb * 512, (tb + 1) * 512)
                cs8 = slice(tb * 8, (tb + 1) * 8)
                sq, Bsq = tmp["sq"]
                act(P, sq[:], k_s[:, sl], AF.Square, [Bk, Bcolp], [Bsq], scale=kkc)
                pa, Bpa = psA.next()
                mm(P, pa[:], cx.blk[:], sq[:], True, True, [cx.Bblk, Bsq], [Bpa])
                rn, Brn = tmp["rn"]
                ts(P, rn[:], pa[:], 1e-12, ALU.max, [Bpa], [Brn])
                act(P, rn[:], rn[:], AF.Sqrt, [Brn], [Brn])
                P.op("vector", lambda e, rn=rn: e.reciprocal(out=rn[:], in_=rn[:]), reads=[Brn], writes=[Brn])
                kk, Bkk = tmp["kk"]
                stt(P, kk[:], k_s[:, sl], kkc, rn[:], ALU.mult, ALU.mult, [Bk, Bcolp, Brn], [Bkk])
                kd0, Bkd0 = tmp["kd0"]
                def chain(z, tb=tb, sl=sl, cs8=cs8, kk=kk, Bkk=Bkk, kd0=kd0, Bkd0=Bkd0):
                    ARt, BAR = AR[z]
                    sg, Bsg = tmpr["sg"].next()
                    az, Baz = tmpr["az"].next()
                    pa, Bpa = psA.next()
                    mm(P, pa[:], wa_up[0:64, z, gc], lin[0:64, sl], True, True, [Bwa, Blin], [Bpa])
                    act(P, sg[:], pa[:], AF.Sigmoid, [Bpa, Bcolp], [Bsg], bias=colp[:, 52 + z * 8 + g:53 + z * 8 + g], scale=1.0)
                    pa, Bpa = psA.next()
                    mm(P, pa[:], wa_up[64:128, z, gc], lin[64:128, sl], True, True, [Bwa, Blin], [Bpa])
                    act(P, az[:], pa[:], AF.Sigmoid, [Bpa, Bcolp], [Baz], bias=colp[:, 68 + z * 8 + g:69 + z * 8 + g], scale=1.0)
                    yield
                    cs, Bcs = tmpr["cs"].next()
                    P.op("vector", lambda e, cs=cs, sg=sg: e.tensor_tensor_scan(out=cs[:], data0=rst[:], data1=sg[:], initial=0.0,
                                                                          op0=ALU.mult, op1=ALU.add),
                         reads=[Brst, Bsg], writes=[Bcs])
                    cs3 = cs[:].rearrange("p (n c) -> p n c", c=CH)
                    totb = cs3[:, :, CH - 1:CH].to_broadcast([128, 8, CH])
                    gC, BgC = gamC[z]
                    act(P, gC[:, cs8], cs3[:, :, CH - 1], AF.Exp, [Bcs], [BgC], scale=-C_DECAY)
                    yield
                    if z == 0:
                        incl, Bincl = cs, Bcs
                    else:
                        incl, Bincl = tmpr["incl"].next()
                        i3 = incl[:].rearrange("p (n c) -> p n c", c=CH)
                        tt(P, i3, totb, cs3, ALU.subtract, [Bcs], [Bincl])
                        tt(P, incl[:], incl[:], sg[:], ALU.add, [Bincl, Bsg], [Bincl], eng="gpsimd")
                    i3 = incl[:].rearrange("p (n c) -> p n c", c=CH)
                    e1, Be1 = tmpr["e1"].next()
                    e2, Be2 = tmpr["e2"].next()
                    t1, Bt1 = tmpr["t1"].next()
                    kd, Bkd = (kd0, Bkd0) if z == 0 else tmp["kd"]
                    kka, Bkka = tmpr["kka"].next()
                    ts(P, t1[:], az[:], kac, ALU.mult, [Baz, Bcolp, Bomka], [Bt1], s2=omka[:, g:g + 1], op1=ALU.add)
                    tt(P, kd[:], t1[:], k_s[:, sl], ALU.mult, [Bt1, Bk], [Bkd])
                    tt(P, kka[:], az[:], kk[:], ALU.mult, [Baz, Bkk], [Bkka], eng="gpsimd")
                    yield
                    act(P, e1[:], incl[:], AF.Exp, [Bincl], [Be1], scale=-C_DECAY)
                    tt(P, ARt[:, cs8, 1, :], r_s[:, sl].rearrange("p (n c) -> p n c", c=CH), e1[:].rearrange("p (n c) -> p n c", c=CH),
                       ALU.mult, [Br, Be1], [BAR])
                    act(P, e2[:], incl[:], AF.Exp, [Bincl], [Be2], scale=C_DECAY)
                    tt(P, KT[z][0][:, sl], kd[:], e2[:], ALU.mult, [Bkd, Be2], [KT[z][1]])
                    tt(P, BT[z][0][:, sl], kka[:], e2[:], ALU.mult, [Bkka, Be2], [BT[z][1]], eng="gpsimd")
                    yield
                    tt(P, t1[:], incl[:], sg[:], ALU.subtract, [Bincl, Bsg], [Bt1])
                    act(P, e1[:], t1[:], AF.Exp, [Bt1], [Be1], scale=-C_DECAY)
                    tt(P, ARt[:, cs8, 0, :], kk[:].rearrange("p (n c) -> p n c", c=CH), e1[:].rearrange("p (n c) -> p n c", c=CH),
                       ALU.mult, [Bkk, Be1], [BAR])
                    yield
                    t13 = t1[:].rearrange("p (n c) -> p n c", c=CH)
                    tt(P, t13, totb, i3, ALU.subtract, [Bcs, Bincl], [Bt1])
                    act(P, e2[:], t1[:], AF.Exp, [Bt1], [Be2], scale=-C_DECAY)
                    yield
                    for which in range(2):
                        hb, Bhb = tb16.next()
                        if which == 0:
                            tt(P, hb[:], kd[:], e2[:], ALU.mult, [Bkd, Be2], [Bhb])
                            dstt, Bdst = Ktok[z]
                        else:
                            stt(P, hb[:], kka[:], -1.0, e2[:], ALU.mult, ALU.mult, [Bkka, Be2], [Bhb])
                            dstt, Bdst = nBtok[z]
                        pt, Bp = psTb.next()
                        ptb = pt[:].bitcast(BF16)
                        for c in range(8):
                            for hh in range(2):
                                hs = slice(hh * 64, hh * 64 + 64)
                                tr(P, ptb[hs, c * CH:(c + 1) * CH], hb[hs, c * CH:(c + 1) * CH], cx.ident_b[hs, hs], [Bhb, cx.Bident_b], [Bp])
                        act(P, dstt[:, tb * 8:(tb + 1) * 8, :], ptb[:, 0:512].rearrange("p (j c) -> p j c", j=8), AF.Copy, [Bp], [Bdst])
                    if z == 1:
                        tt(P, kd[:], kd[:], kd0[:], ALU.add, [Bkd, Bkd0], [Bkd], eng="gpsimd")
                        u, Bu = tmp["u"]
                        stt(P, u[:], kd[:], rkc, r_s[:, sl], ALU.mult, ALU.mult, [Bkd, Bcolp, Br], [Bu])
                        pa, Bpa = psA.next()
                        mm(P, pa[:], cx.blk[:], u[:], True, True, [cx.Bblk, Bu], [Bpa])
                        tt(P, bonus[:, sl], pa[:], v_s[:, sl], ALU.mult, [Bpa, Bv], [Bbonus])

                gens = [chain(0), chain(1)]
                while gens:
                    for g_ in list(gens):
                        try:
                            next(g_)
                        except StopIteration:
                            gens.remove(g_)
                hb, Bhb = tb16.next()
                vcopy(P, hb[:], v_s[:, sl], [Bv], [Bhb])
                pt, Bp = psTb.next()
                ptb = pt[:].bitcast(BF16)
                for c in range(8):
                    for hh in range(2):
                        hs = slice(hh * 64, hh * 64 + 64)
                        tr(P, ptb[hs, c * CH:(c + 1) * CH], hb[hs, c * CH:(c + 1) * CH], cx.ident_b[hs, hs], [Bhb, cx.Bident_b], [Bp])
                vcopy(P, Vtok[:, tb * 8:(tb + 1) * 8, :], ptb[:, 0:512].rearrange("p (j c) -> p j c", j=8), [Bp], [BVtok])
                pa, Bpa = psA.next()
                mm(P, pa[:], g_up[:, gc], sdg[:, sl], True, True, [Bgup, Bsdg], [Bpa])
                act(P, gate[:, sl], pa[:], AF.Copy, [Bpa], [Bgate])
        if DBG["stop"] < 2:
            return
        X, BX = P.tile("Xm", [128, NCH, 2, 4, CH], BF16)
        with P.scope():
            NT0, BNT0 = P.tile("NT0", [128, NCH, 2, CH], BF16)
            with P.scope():
                pcA = Rot([P.ptile(f"pcA{i}", [128, 2, 256], F32) for i in range(3)])
                pcB = Rot([P.ptile(f"pcB{i}", [128, 2, CH], F32) for i in range(3)])
                for i in range(NCH // 2):
                    for z in range(2):
                        pa, Bpa = pcA.next()
                        pb_, Bpb = pcB.next()
                        ARt, BAR = AR[z]
                        for j in range(2):
                            n = 2 * i + j
                            ns = slice(n * CH, (n + 1) * CH)
                            for hh in range(2):
                                hs = slice(hh * 64, hh * 64 + 64)
                                arr = ARt[hs, n, :, :].rearrange("p a c -> p (a c)")
                                mm(P, pa[hs, j, 0:128], BT[z][0][hs, ns], arr, True, True, [BT[z][1], BAR], [Bpa])
                                mm(P, pa[hs, j, 128:256], KT[z][0][hs, ns], arr, True, True, [KT[z][1], BAR], [Bpa])
                                mm(P, pb_[hs, j, :], ARt[hs, n, 0, :], BT[z][0][hs, ns], True, True, [BAR, BT[z][1]], [Bpb])
                        tt(P, X[:, 2 * i:2 * i + 2, z, :, :], pa[:].rearrange("p j (w c) -> p j w c", w=4), maskA[:, z, :, :, :], ALU.mult,
                           [Bpa, BmaskA], [BX])
                        tt(P, NT0[:, 2 * i:2 * i + 2, z, :], pb_[:], maskB[:, z, :, :], ALU.mult, [Bpb, BmaskB], [BNT0])
            if DBG["stop"] < 3:
                return
            with P.scope():
                NB = 8
                NSTR = 2
                Xm = X[:].rearrange("p n z w c -> p (n z) w c")
                NT0m = NT0[:].rearrange("p n z c -> p (n z) c")
                streams = []
                for si in range(NSTR):
                    st_ = Ctx()
                    st_.psN = P.ptile(f"psN{si}", [128, NB, CH], F32)
                    st_.psNT = P.ptile(f"psNT{si}", [128, NB, CH], F32)
                    st_.psI = P.ptile(f"psI{si}", [128, NB, CH], F32)
                    st_.Ns = Rot([P.tile(f"Ncur{si}_{i}", [128, NB, CH], BF16) for i in range(2)])
                    st_.NTs = Rot([P.tile(f"NTcur{si}_{i}", [128, NB, CH], BF16) for i in range(2)])
                    st_.Invs = Rot([P.tile(f"Inv{si}_{i}", [128, NB, CH], BF16) for i in range(2)])
                    streams.append(st_)
                nbatch = 64 // NB
                for b0 in range(0, nbatch, NSTR):
                    act_streams = []
                    for si in range(NSTR):
                        bi = b0 + si
                        st_ = streams[si]
                        st_.ms = slice(bi * NB, (bi + 1) * NB)
                        st_.Nprev = (lambda m, hs, bi=bi: Xm[hs, bi * NB + m, 0, :])
                        st_.NTprev = (lambda m, hs, bi=bi: NT0m[hs, bi * NB + m, :])
                        st_.BNprev, st_.BNTprev = BX, BNT0
                        st_.Inv, st_.BInv = st_.Invs.next()
                        tt(P, st_.Inv[:], Xm[:, st_.ms, 0, :], identblk[:].unsqueeze(1).to_broadcast([128, NB, CH]), ALU.add,
                           [BX, Bidb], [st_.BInv])
                        act_streams.append(st_)
                    for lev in range(1, 6):
                        for st_ in act_streams:
                            st_.Nn, st_.BNn = st_.Ns.next()
                            st_.NTn, st_.BNTn = st_.NTs.next()
                            for m in range(NB):
                                for hh in range(2):
                                    hs = slice(hh * 64, hh * 64 + 64)
                                    if lev < 5:
                                        mm(P, st_.psN[0][hs, m, :], st_.NTprev(m, hs), st_.Nprev(m, hs), True, True,
                                           [st_.BNprev, st_.BNTprev], [st_.psN[1]])
                                    mm(P, st_.psNT[0][hs, m, :], st_.Nprev(m, hs), st_.NTprev(m, hs), True, True,
                                       [st_.BNprev, st_.BNTprev], [st_.psNT[1]])
                            if lev < 5:
                                act(P, st_.Nn[:], st_.psN[0][:], AF.Copy, [st_.psN[1]], [st_.BNn])
                            vcopy(P, st_.NTn[:], st_.psNT[0][:], [st_.psNT[1]], [st_.BNTn])
                        for st_ in act_streams:
                            for m in range(NB):
                                for hh in range(2):
                                    hs = slice(hh * 64, hh * 64 + 64)
                                    mm(P, st_.psI[0][hs, m, :], st_.NTn[hs, m, :], st_.Inv[hs, m, :], True, True,
                                       [st_.BNTn, st_.BInv], [st_.psI[1]])
                            if lev < 5:
                                Inv2, BInv2 = st_.Invs.next()
                                tt(P, Inv2[:], st_.psI[0][:], st_.Inv[:], ALU.add, [st_.psI[1], st_.BInv], [BInv2])
                                st_.Inv, st_.BInv = Inv2, BInv2
                            else:
                                tt(P, Xm[:, st_.ms, 0, :], st_.psI[0][:], st_.Inv[:], ALU.add, [st_.psI[1], st_.BInv], [BX])
                            st_.Nprev = (lambda m, hs, Nn=st_.Nn: Nn[hs, m, :])
                            st_.NTprev = (lambda m, hs, NTn=st_.NTn: NTn[hs, m, :])
                            st_.BNprev, st_.BNTprev = st_.BNn, st_.BNTn
        if DBG["stop"] < 4:
            return
        Yz, BYz = P.tile("Yz", [128, 2, T], F32)
        with P.scope():
            ST, BST = P.tile("ST", [128, 2, CH], F32)
            STb, BSTb = P.tile("STb", [128, 2, CH], BF16)
            memset(P, ST[:], 0.0, [BST])
            memset(P, STb[:], 0.0, [BSTb])
            psWs = Rot([P.ptile(f"psW{i}", [128, 2, CH], F32) for i in range(2)])
            psPs = Rot([P.ptile(f"psP{i}", [128, 2, CH], F32) for i in range(2)])
            psYs = Rot([P.ptile(f"psYs{i}", [128, 2, CH], F32) for i in range(2)])
            psSs = Rot([P.ptile(f"psS{i}", [128, 2, CH], F32) for i in range(2)])
            Wsbs = Rot([P.tile(f"Wsb{i}", [128, 2, CH], BF16) for i in range(2)])
            Psbs = Rot([P.tile(f"Psb{i}", [128, 2, CH], BF16) for i in range(2)])
            H = [slice(0, 64), slice(64, 128)]
            for n in range(NCH):
                czs = [n, NCH - 1 - n]
                pW, BpW = psWs.next()
                pP, BpP = psPs.next()
                pY, BpY = psYs.next()
                pS, BpS = psSs.next()
                Wsb, BWsb = Wsbs.next()
                Psb, BPsb = Psbs.next()
                for z in range(2):
                    cz = czs[z]
                    for hs in H:
                        mm(P, pW[hs, z, :], AR[z][0][hs, cz, 0, :], STb[hs, z, :], True, False, [AR[z][1], BSTb], [BpW])
                        mm(P, pW[hs, z, :], X[hs, cz, z, 2, :], Vtok[hs, cz, :], False, True, [BX, BVtok], [BpW])
                vcopy(P, Wsb[:], pW[:], [BpW], [BWsb])
                for z in range(2):
                    cz = czs[z]
                    for hs in H:
                        mm(P, pP[hs, z, :], X[hs, cz, z, 0, :], Wsb[hs, z, :], True, True, [BX, BWsb], [BpP])
                act(P, Psb[:], pP[:], AF.Copy, [BpP], [BPsb])
                for z in range(2):
                    cz = czs[z]
                    for hs in H:
                        mm(P, pS[hs, z, :], Ktok[z][0][hs, cz, :], Vtok[hs, cz, :], True, False, [Ktok[z][1], BVtok], [BpS])
                        mm(P, pS[hs, z, :], nBtok[z][0][hs, cz, :], Psb[hs, z, :], False, True, [nBtok[z][1], BPsb], [BpS])
                for z in range(2):
                    cz = czs[z]
                    for hs in H:
                        mm(P, pY[hs, z, :], STb[hs, z, :], AR[z][0][hs, cz, 1, :], True, False, [BSTb, AR[z][1]], [BpY])
                        mm(P, pY[hs, z, :], Vtok[hs, cz, :], X[hs, cz, z, 3, :], False, False, [BVtok, BX], [BpY])
                        mm(P, pY[hs, z, :], Psb[hs, z, :], X[hs, cz, z, 1, :], False, True, [BPsb, BX], [BpY])
                for z in range(2):
                    cz = czs[z]
                    ts(P, ST[:, z, :], ST[:, z, :], gamC[z][0][:, cz:cz + 1], ALU.mult, [BST, gamC[z][1]], [BST], eng="gpsimd")
                tt(P, ST[:], ST[:], pS[:], ALU.add, [BST, BpS], [BST])
                act(P, STb[:], ST[:], AF.Copy, [BST], [BSTb])
                for z in range(2):
                    cz = czs[z]
                    if z == 0:
                        act(P, Yz[:, z, cz * CH:(cz + 1) * CH], pY[:, z, :], AF.Copy, [BpY], [BYz])
                    else:
                        vcopy(P, Yz[:, z, cz * CH:(cz + 1) * CH], pY[:, z, :], [BpY], [BYz])
        if DBG["stop"] < 5:
            return
        with P.scope():
            psA = Rot([P.ptile(f"opa{i}", [128, 512], F32) for i in range(2)])
            ysum, Bys = P.tile("ysum", [128, 512], F32)
            yc, Byc = P.tile("yc", [128, 512], F32)
            sq, Bsq = P.tile("osq", [128, 512], F32)
            rs, Brs = P.tile("ors", [128, 512], F32)
            yT, ByT = P.tile("ryT", [128, T], BF16)
            for tb in range(4):
                sl = slice(tb * 512, (tb + 1) * 512)
                tt(P, ysum[:], Yz[:, 0, sl], Yz[:, 1, sl], ALU.add, [BYz], [Bys])
                pa, Bpa = psA.next()
                mm(P, pa[:], cx.blk64[:], ysum[:], True, True, [cx.Bblk64, Bys], [Bpa])
                tt(P, yc[:], ysum[:], pa[:], ALU.subtract, [Bys, Bpa], [Byc])
                act(P, sq[:], yc[:], AF.Square, [Byc], [Bsq])
                pa, Bpa = psA.next()
                mm(P, pa[:], cx.blk64[:], sq[:], True, True, [cx.Bblk64, Bsq], [Bpa])
                ts(P, rs[:], pa[:], 64e-5, ALU.add, [Bpa], [Brs])
                act(P, rs[:], rs[:], AF.Sqrt, [Brs], [Brs])
                P.op("vector", lambda e: e.reciprocal(out=rs[:], in_=rs[:]), reads=[Brs], writes=[Brs])
                tt(P, yc[:], yc[:], rs[:], ALU.mult, [Byc, Brs], [Byc])
                ts(P, yc[:], yc[:], colp[:, 108 + g:109 + g], ALU.mult, [Byc, Bcolp], [Byc], s2=colp[:, 116 + g:117 + g], op1=ALU.add)
                tt(P, yc[:], yc[:], bonus[:, sl], ALU.add, [Byc, Bbonus], [Byc], eng="gpsimd")
                tt(P, yT[:, sl], yc[:], gate[:, sl], ALU.mult, [Byc, Bgate], [ByT])
            P.dma("sync", cx.ybrT[g * 128:(g + 1) * 128, :], yT[:], reads=[ByT], writes=[cx.BybrT[g]])


def host_colp(inputs, nl):
    out = np.zeros((nl, 128, 160), np.float32)
    for l in range(nl):
        mu = inputs["rwkv_mu"][l]
        out[l, :, 0:26] = mu[0].reshape(26, 128).T
        out[l, :, 26:52] = mu[1].reshape(26, 128).T
        for z in range(2):
            out[l, :, 52 + z * 8:60 + z * 8] = inputs["rwkv_w0"][l, z].reshape(8, 128).T
            out[l, :, 68 + z * 8:76 + z * 8] = inputs["rwkv_a0"][l, z].reshape(8, 128).T
        out[l, :, 84:92] = inputs["rwkv_k_k"][l].reshape(8, 128).T
        out[l, :, 92:100] = inputs["rwkv_k_a"][l].reshape(8, 128).T
        out[l, :, 100:108] = inputs["rwkv_r_k"][l].reshape(8, 128).T
        out[l, :, 108:116] = inputs["rwkv_gn_g"][l].reshape(8, 128).T
        out[l, :, 116:124] = inputs["rwkv_gn_b"][l].reshape(8, 128).T
    return out


def build_program(nl=NL, phases=("rwkv", "win", "diff", "mem", "merge", "moe"), dbg=False):
    nc = bass.Bass("TRN2", target_bir_lowering=False)
    cx = Ctx()
    shapes = [(nm, ((nl,) + shp[1:]) if (shp[0] == NL and nm not in ("x",)) else shp) for nm, shp in INPUT_SHAPES]
    declare_inputs(nc, cx, shapes)
    cx.y = nc.dram_tensor("y", [T, D], F32, kind="ExternalOutput").ap()
    kind = "ExternalOutput" if dbg else "Internal"
    cx.ybrT = nc.dram_tensor("ybrT", [26 * 128, T], BF16, kind=kind).ap()
    cx.x1res = nc.dram_tensor("x1res", [T, D], F32, kind=kind).ap()
    cx.xres = nc.dram_tensor("xres", [T, D], F32, kind=kind).ap()
    P = Prog(nc)
    cx.BybrT = [P.buf(f"ybr{i}") for i in range(26)]
    cx.Bx1res = P.buf("x1res")
    cx.Bxres = P.buf("xres")
    cx.By = P.buf("y")
    setup_consts(P, cx)
    cx.memT, cx.BmemT = P.tile("memT", [128, DC, 256], BF16)
    cx.aff_tok, cx.Baff_tok = P.tile("aff_tok", [128, NT, 16], F32)
    build_memT(P, cx)
    for l in range(nl):
        src = cx.x if l == 0 else cx.xres
        with P.scope():
            cx.xT, cx.BxT = P.tile("xT", [128, DC, T], BF16)
            build_xT(P, cx, src)
            if "rwkv" in phases:
                phase_rwkv(P, cx, l)
            if "win" in phases:
                phase_window(P, cx, l)
            if "diff" in phases:
                phase_diff(P, cx, l)
            if "mem" in phases:
                phase_mem(P, cx, l)
            if "merge" in phases:
                phase_merge(P, cx, l, src)
        if "moe" in phases:
            last = (l == nl - 1)
            phase_moe(P, cx, l, cx.y if last else cx.xres, cx.By if last else cx.Bxres)
    P.barrier()
    P.emit()
    return nc, P


def make_in_maps(inputs, nl=NL, cores=NCORES):
    G, farb = host_tables(np.asarray(inputs["rel_bias"], np.float32))
    colp = host_colp(inputs, nl)
    shared = {"tblG": G, "farb": farb, "colp": colp}
    for nm, shp in INPUT_SHAPES:
        if nm in ("x", "mem", "tblG", "farb", "colp"):
            continue
        a = np.asarray(inputs[nm], np.float32)
        shared[nm] = np.ascontiguousarray(a[:nl]) if shp[0] == NL else a
    maps = []
    for c in range(cores):
        m = dict(shared)
        m["x"] = np.ascontiguousarray(np.asarray(inputs["x"][c], np.float32))
        m["mem"] = np.ascontiguousarray(np.asarray(inputs["mem"][c], np.float32))
        maps.append(m)
    return maps


_CACHE = {}


def kernel(**inputs):
    if "nc" not in _CACHE:
        _CACHE["nc"] = build_program()[0]
    nc = _CACHE["nc"]
    maps = make_in_maps(inputs)
    res = run_bass_kernel_spmd(nc, maps, core_ids=list(range(NCORES)))
    out = np.stack([np.asarray(r["y"], np.float32) for r in res.results], axis=0)
    return out
```

```python
import math
from contextlib import ExitStack, contextmanager

import numpy as np
import concourse.bass as bass
import concourse.mybir as mybir
from concourse.bass_utils import run_bass_kernel_spmd

F32 = mybir.dt.float32
BF16 = mybir.dt.bfloat16
I32 = mybir.dt.int32
AF = mybir.ActivationFunctionType
ALU = mybir.AluOpType
AX = mybir.AxisListType

T = 2048
D = 1024
NT = 16
DC = 8
NL = 4
NCORES = 8
SEM_LIMIT = 30000
DN_ALPHA = (2 * NL) ** 0.25
C_DECAY = math.exp(-0.5)

O_RWKV = 0
O_WINQ = 3328
O_WINK = 4352
O_WINV = 4608
O_DQ = 4864
O_DK = 5888
O_DV = 6912
O_MEMQ = 7936
O_GATE = 8192


class Buf:
    __slots__ = ("name", "w", "r", "dsem", "dcnt")

    def __init__(self, name):
        self.name = name
        self.w = {}
        self.r = {}
        self.dsem = None
        self.dcnt = 0


class Prog:
    ENGS = ("tensor", "vector", "scalar", "gpsimd", "sync")

    def __init__(self, nc):
        self.nc = nc
        self.es = ExitStack()
        self.q = {n: [] for n in self.ENGS}
        self.esem = {}
        self.ecnt = {}
        self.waited = {n: {} for n in self.ENGS}
        self.nsem = 0
        self.pe_sems = set()
        self.semobj = {}
        self.free_dsems = []
        self.scopes = [[]]
        self.stacks = [self.es]
        self.n_inst = 0
        self.nname = 0
        self.pstate = {}
        for n in self.ENGS:
            self._new_esem(n)

    def sem(self, name):
        self.nsem += 1
        s = self.es.enter_context(self.nc.semaphore(f"{name}_{self.nsem}"))
        self.semobj[id(s)] = s
        return s

    def _new_esem(self, n):
        self.esem[n] = self.sem("e" + n)
        self.ecnt[n] = 0
        if n == "tensor":
            self.pe_sems.add(id(self.esem[n]))

    def sb(self, name, shape, dt):
        self.nname += 1
        return self.stacks[-1].enter_context(self.nc.sbuf_tensor(f"{name}_s{self.nname}", shape, dt))

    def ps(self, name, shape, dt=F32):
        self.nname += 1
        return self.stacks[-1].enter_context(self.nc.psum_tensor(f"{name}_p{self.nname}", shape, dt))

    def buf(self, name):
        b = Buf(name)
        self.scopes[-1].append(b)
        return b

    def tile(self, name, shape, dt):
        return self.sb(name, shape, dt), self.buf(name)

    def ptile(self, name, shape, dt=F32):
        return self.ps(name, shape, dt), self.buf(name)

    @contextmanager
    def scope(self):
        st = ExitStack()
        self.stacks.append(st)
        self.scopes.append([])
        try:
            yield
        finally:
            self.barrier()
            for b in self.scopes.pop():
                if b.dsem is not None:
                    self.free_dsems.append((b.dsem, b.dcnt))
                    b.dsem = None
            self.stacks.pop()
            st.close()

    def _deps(self, reads, writes, skip=()):
        deps = {}

        def add(d):
            for k, v in d.items():
                if k in skip:
                    continue
                if deps.get(k, 0) < v:
                    deps[k] = v

        for b in reads:
            add(b.w)
        for b in writes:
            add(b.w)
            add(b.r)
        return deps

    def _waits(self, eng, deps):
        out = []
        wd = self.waited[eng]
        for k, v in deps.items():
            if wd.get(k, 0) >= v:
                continue
            wd[k] = v
            out.append((self.semobj[k], v))
        return out

    def op(self, eng, fn, reads=(), writes=()):
        if self.ecnt[eng] >= SEM_LIMIT:
            self._new_esem(eng)
        skip = self.pe_sems if eng == "tensor" else ()
        waits = self._waits(eng, self._deps(reads, writes, skip))
        self.ecnt[eng] += 1
        s = self.esem[eng]
        v = self.ecnt[eng]
        k = id(s)
        self.q[eng].append((waits, fn, s, 1))
        self.n_inst += 1
        for b in reads:
            if b.r.get(k, 0) < v:
                b.r[k] = v
        for b in writes:
            b.w = {k: v}
            b.r = {}

    def _get_dsem(self, dst):
        if dst.dsem is None or dst.dcnt >= SEM_LIMIT:
            if self.free_dsems and dst.dsem is None:
                dst.dsem, dst.dcnt = self.free_dsems.pop()
                if dst.dcnt >= SEM_LIMIT:
                    dst.dsem = self.sem("d")
                    dst.dcnt = 0
            else:
                dst.dsem = self.sem("d")
                dst.dcnt = 0

    def dma(self, eng, out, in_, reads=(), writes=(), **kw):
        dst = writes[0]
        old = dst.dsem
        self._get_dsem(dst)
        skip = (id(dst.dsem),) if old is dst.dsem else ()
        waits = self._waits(eng, self._deps(reads, writes, skip))
        dst.dcnt += 16
        s, v = dst.dsem, dst.dcnt
        k = id(s)
        self.q[eng].append((waits, (lambda e, o=out, i=in_, kw=kw: e.dma_start(out=o, in_=i, **kw)), s, 16))
        self.n_inst += 1
        for b in reads:
            if b.r.get(k, 0) < v:
                b.r[k] = v
        for b in writes:
            b.w = {k: v}
            b.r = {}

    def pstart(self, B, bank, pk):
        d = self.pstate.setdefault(id(B), set())
        keys = [(bank, "l"), (bank, "h")] if pk == "f" else [(bank, pk)]
        st = not all(k in d for k in keys)
        d.update(keys)
        return st

    def preset(self, B, pk=None):
        d = self.pstate.get(id(B))
        if d is None:
            return
        if pk is None:
            d.clear()
        else:
            for k in [k for k in d if k[1] == pk]:
                d.discard(k)

    def barrier(self):
        deps = {}
        for n in self.ENGS:
            if self.ecnt[n] > 0:
                deps[id(self.esem[n])] = self.ecnt[n]
        for sc in self.scopes:
            for b in sc:
                if b.dsem is not None and b.dcnt > 0:
                    k = id(b.dsem)
                    if deps.get(k, 0) < b.dcnt:
                        deps[k] = b.dcnt
        for n in self.ENGS:
            own = id(self.esem[n])
            d = {k: v for k, v in deps.items() if k != own}
            waits = self._waits(n, d)
            if waits:
                self.q[n].append((waits, None, None, 0))

    def emit(self):
        nc = self.nc
        with nc.Block() as block:
            def mk(name):
                def body(e):
                    for waits, fn, s, n in self.q[name]:
                        for ws, wv in waits:
                            e.wait_ge(ws, wv)
                        if fn is not None:
                            fn(e).then_inc(s, n)
                return body
            block.sync(mk("sync"))
            block.tensor(mk("tensor"))
            block.vector(mk("vector"))
            block.scalar(mk("scalar"))
            block.gpsimd(mk("gpsimd"))
        self.es.close()


class Ctx:
    pass


def mm(P, out, lhsT, rhs, start, stop, reads, writes):
    P.op("tensor", lambda e: e.matmul(out, lhsT, rhs, start=start, stop=stop), reads=reads, writes=writes)


def mma(P, out, lhsT, rhs, Bps, bank, pk, reads, stop=False):
    mm(P, out, lhsT, rhs, P.pstart(Bps, bank, pk), stop, reads, [Bps])


def tr(P, out, in_, ident, reads, writes):
    P.op("tensor", lambda e: e.transpose(out, in_, ident), reads=reads, writes=writes)


def act(P, out, in_, func, reads, writes, bias=None, scale=None, accum_out=None):
    kw = {}
    if bias is not None:
        kw["bias"] = bias
    if scale is not None:
        kw["scale"] = scale
    if accum_out is not None:
        kw["accum_out"] = accum_out
    P.op("scalar", lambda e: e.activation(out=out, in_=in_, func=func, **kw), reads=reads, writes=writes)


def vcopy(P, out, in_, reads, writes, eng="vector"):
    P.op(eng, lambda e: e.tensor_copy(out=out, in_=in_), reads=reads, writes=writes)


def tt(P, out, in0, in1, op, reads, writes, eng="vector"):
    P.op(eng, lambda e: e.tensor_tensor(out=out, in0=in0, in1=in1, op=op), reads=reads, writes=writes)


def ts(P, out, in0, s1, op0, reads, writes, s2=None, op1=None, eng="vector", accum_out=None):
    kw = {}
    if op1 is not None:
        kw["op1"] = op1
    if accum_out is not None:
        kw["accum_out"] = accum_out
    P.op(eng, lambda e: e.tensor_scalar(out=out, in0=in0, scalar1=s1, scalar2=s2, op0=op0, **kw), reads=reads, writes=writes)


def stt(P, out, in0, scalar, in1, op0, op1, reads, writes):
    P.op("vector", lambda e: e.scalar_tensor_tensor(out=out, in0=in0, scalar=scalar, in1=in1, op0=op0, op1=op1),
         reads=reads, writes=writes)


def rsqrt(P, cx, out, in_, B, scale, eps):
    ts(P, out, in_, scale, ALU.mult, [B], [B], s2=eps, op1=ALU.add)
    act(P, out, out, AF.Sqrt, [B], [B])
    P.op("vector", lambda e: e.reciprocal(out=out, in_=out), reads=[B], writes=[B])


def memset(P, ap, val, writes, eng="vector"):
    P.op(eng, lambda e: e.memset(ap, val), writes=writes)


def setup_consts(P, cx):
    nc = P.nc
    dif_i, Bd = P.tile("dif_i", [128, 128], I32)
    P.op("gpsimd", lambda e: e.iota(dif_i[:], pattern=[[1, 128]], base=0, channel_multiplier=-1), writes=[Bd])
    dif, Bdf = P.tile("dif_f", [128, 128], F32)
    vcopy(P, dif[:], dif_i[:], [Bd], [Bdf])
    cx.ident_f, cx.Bident_f = P.tile("ident_f", [128, 128], F32)
    ts(P, cx.ident_f[:], dif[:], 0.0, ALU.is_equal, [Bdf], [cx.Bident_f])
    cx.ident_b, cx.Bident_b = P.tile("ident_b", [128, 128], BF16)
    vcopy(P, cx.ident_b[:], cx.ident_f[:], [cx.Bident_f], [cx.Bident_b])
    cx.dif = dif
    cx.Bdif = Bdf
    ci_i, Bci = P.tile("ci_i", [128, 256], I32)
    P.op("gpsimd", lambda e: e.iota(ci_i[:], pattern=[[1, 256]], base=0, channel_multiplier=0), writes=[Bci])
    cx.colidx, cx.Bcolidx = P.tile("colidx", [128, 256], F32)
    vcopy(P, cx.colidx[:], ci_i[:], [Bci], [cx.Bcolidx])
    pi_i, Bpi = P.tile("pi_i", [128, 1], I32)
    P.op("gpsimd", lambda e: e.iota(pi_i[:], pattern=[[0, 1]], base=0, channel_multiplier=1), writes=[Bpi])
    cx.pidx, cx.Bpidx = P.tile("pidx", [128, 1], F32)
    vcopy(P, cx.pidx[:], pi_i[:], [Bpi], [cx.Bpidx])
    cx.blk, cx.Bblk = P.tile("blk", [128, 128], F32)
    memset(P, cx.blk[:], 0.0, [cx.Bblk])
    memset(P, cx.blk[0:64, 0:64], 1.0, [cx.Bblk])
    memset(P, cx.blk[64:128, 64:128], 1.0, [cx.Bblk])
    cx.blk64, cx.Bblk64 = P.tile("blk64", [128, 128], F32)
    ts(P, cx.blk64[:], cx.blk[:], 1.0 / 64.0, ALU.mult, [cx.Bblk], [cx.Bblk64])
    cx.ones_b, cx.Bones_b = P.tile("ones_b", [128, 128], BF16)
    memset(P, cx.ones_b[:], 1.0, [cx.Bones_b])


def build_xT(P, cx, src_dram):
    with P.scope():
        xin = [P.tile(f"xin{i}", [128, D], F32) for i in range(2)]
        pst = [P.ptile(f"xtp{i}", [128, 512], F32) for i in range(2)]
        k = 0
        for i in range(NT):
            xt_, Bx = xin[i % 2]
            P.dma("sync", xt_[:], src_dram[i * 128:(i + 1) * 128, :], writes=[Bx])
            for half in range(2):
                pt, Bp = pst[k % 2]
                k += 1
                for j in range(4):
                    c = half * 4 + j
                    tr(P, pt[:, j * 128:(j + 1) * 128], xt_[:, c * 128:(c + 1) * 128], cx.ident_f[:],
                       [Bx, cx.Bident_f], [Bp])
                dst = cx.xT[:, half * 4:half * 4 + 4, i * 128:(i + 1) * 128]
                src = pt[:].rearrange("p (j t) -> p j t", j=4)
                if half == 0:
                    vcopy(P, dst, src, [Bp], [cx.BxT])
                else:
                    act(P, dst, src, AF.Copy, [Bp], [cx.BxT])


def load_w(P, dst_tile, Bdst, w2d, c0, ncols, r0=0, nchunks=DC, eng="gpsimd"):
    src = w2d[r0:r0 + nchunks * 128, c0:c0 + ncols].rearrange("(c p) n -> p c n", p=128)
    P.dma(eng, dst_tile, src, writes=[Bdst])


def proj_fm(P, cx, dst_fn, W, BW, wcol0, M, evac, nchunks=DC, rhs=None, Brhs=None, tlen=T, pst=None, extra_reads=()):
    rhs = cx.xT if rhs is None else rhs
    Brhs = cx.BxT if Brhs is None else Brhs
    nb = tlen // 512
    for tb in range(nb):
        pt, Bp = pst[tb % len(pst)]
        for c in range(nchunks):
            mm(P, pt[0:M, :], W[:, c, wcol0:wcol0 + M], rhs[:, c, tb * 512:(tb + 1) * 512], c == 0, c == nchunks - 1,
               [BW, Brhs], [Bp])
        evac(tb, pt, Bp)


class Rot:
    def __init__(self, items):
        self.items = items
        self.i = 0

    def next(self):
        it = self.items[self.i % len(self.items)]
        self.i += 1
        return it


class AttnPipe:
    def __init__(self, P, pss, tmps, ets, depth=2):
        self.P, self.pss, self.tmps, self.ets, self.depth = P, pss, tmps, ets, depth
        self.staged = []

    def block(self, acc, Bacc, dvp, q_ap, Bq, k_ap, Bk, v_ap, Bv, plan, post=None):
        ctx = Ctx()
        ctx.acc, ctx.Bacc, ctx.dvp, ctx.v_ap, ctx.Bv = acc, Bacc, dvp, v_ap, Bv
        ctx.started, ctx.n, ctx.done, ctx.post = set(), len(plan), 0, post
        P = self.P
        for item in plan:
            (j, qlo, qhi, kind, arg, firsts, lasts) = item
            ncol = (qhi - qlo + 1) * 128
            ps_s, Bps = self.pss.next()
            mm(P, ps_s[:, 0:ncol], k_ap(j), q_ap(qlo, ncol), True, True, [Bk, Bq], [Bps])
            eT, Be = self.ets.next()
            if kind == "tbl":
                tmp, Bt = self.tmps.next()
                tab, Btab = arg
                stt(P, tmp[:, 0:ncol], ps_s[:, 0:ncol], 0.125, tab, ALU.mult, ALU.add, [Bps, Btab], [Bt])
                act(P, eT[:, 0:ncol], tmp[:, 0:ncol], AF.Exp, [Bt], [Be])
            elif kind == "far":
                col, Bcol = arg
                act(P, eT[:, 0:ncol], ps_s[:, 0:ncol], AF.Exp, [Bps, Bcol], [Be], bias=col, scale=0.125)
            else:
                act(P, eT[:, 0:ncol], ps_s[:, 0:ncol], AF.Exp, [Bps], [Be], scale=0.125)
            self.staged.append((ctx, item, eT, Be))
            while len(self.staged) > self.depth:
                self._stage_b()

    def _stage_b(self):
        P = self.P
        ctx, (j, qlo, qhi, kind, arg, firsts, lasts), eT, Be = self.staged.pop(0)
        slot_w = ctx.acc.shape[2]
        for s in range(qlo, qhi + 1):
            bank = (s * slot_w) // 512
            st = bank not in ctx.started
            ctx.started.add(bank)
            mm(P, ctx.acc[:, s, 0:ctx.dvp], eT[:, (s - qlo) * 128:(s - qlo + 1) * 128], ctx.v_ap(j), st, s in lasts,
               [Be, ctx.Bv], [ctx.Bacc])
        ctx.done += 1
        if ctx.done == ctx.n and ctx.post is not None:
            ctx.post()

    def flush(self):
        while self.staged:
            self._stage_b()


def ytok_to_dram(P, cx, ytok, Bytok, yT, ByT, pst, chunk):
    for q4 in range(4):
        pt, Bp = pst.next()
        ptb = pt[:].bitcast(BF16)
        for j in range(4):
            i = q4 * 4 + j
            tr(P, ptb[:, j * 128:(j + 1) * 128], ytok[:, i, :], cx.ident_b[:], [Bytok, cx.Bident_b], [Bp])
        if q4 % 2 == 0:
            vcopy(P, yT[:, q4 * 512:(q4 + 1) * 512], ptb[:, 0:512], [Bp], [ByT])
        else:
            act(P, yT[:, q4 * 512:(q4 + 1) * 512], ptb[:, 0:512], AF.Copy, [Bp], [ByT])
    P.dma("sync", cx.ybrT[chunk * 128:(chunk + 1) * 128, :], yT[:], reads=[ByT], writes=[cx.BybrT[chunk]])


def phase_window(P, cx, l):
    w_in = cx.w_in[l]
    with P.scope():
        W, BW = P.tile("winW", [128, DC, 1536], BF16)
        for k in range(3):
            load_w(P, W[:, :, k * 512:(k + 1) * 512], BW, w_in, O_WINQ + k * 512, 512)
        pst = Rot([P.ptile(f"wps{i}", [128, 512], F32) for i in range(2)])
        pss = Rot([P.ptile(f"wss{i}", [128, 512], F32) for i in range(3)])
        accs = Rot([P.ptile(f"wacc{i}", [128, 4, 128], F32) for i in range(2)])
        tmps = Rot([P.tile(f"wtmp{i}", [128, 512], F32) for i in range(3)])
        ets = Rot([P.tile(f"wet{i}", [128, 512], BF16) for i in range(4)])
        kT, BkT = P.tile("winkT", [64, 4, T], BF16)
        for h in range(4):
            def ev(tb, pt, Bp, h=h):
                vcopy(P, kT[:, h, tb * 512:(tb + 1) * 512], pt[0:64, :], [Bp], [BkT])
            proj_fm(P, cx, None, W, BW, 1024 + h * 64, 64, ev, pst=pst.items)
        vaug, Bva = P.tile("winv", [128, NT, 4, 65], BF16)
        memset(P, vaug[:, :, :, 64:65], 1.0, [Bva], eng="gpsimd")
        for i in range(NT):
            pt, Bp = pst.next()
            for c in range(DC):
                mm(P, pt[:, 0:256], cx.xT[:, c, i * 128:(i + 1) * 128], W[:, c, 1280:1536], c == 0, c == DC - 1,
                   [cx.BxT, BW], [Bp])
            act(P, vaug[:, i, :, 0:64], pt[:, 0:256].rearrange("p (h d) -> p h d", h=4), AF.Copy, [Bp], [Bva])
        sk, Bsk = P.tile("sinkb", [128, 16], F32)
        P.dma("sync", sk[:], cx.win_sink[l, :].partition_broadcast(128), writes=[Bsk])
        esk, Besk = P.tile("esink", [128, 16], F32)
        act(P, esk[:], sk[:], AF.Exp, [Bsk], [Besk])
        qTs = Rot([P.tile(f"winq{i}", [64, T], BF16) for i in range(2)])
        Gs = Rot([P.tile(f"winG{i}", [128, 1152], F32) for i in range(2)])
        ytoks = Rot([P.tile(f"wytok{i}", [128, NT, 128], BF16) for i in range(2)])
        yTs = Rot([P.tile(f"wyT{i}", [128, T], BF16) for i in range(2)])
        den, Bden = P.tile("wden", [128, 4], F32)
        pipe = AttnPipe(P, pss, tmps, ets)
        for g in range(8):
            ytok, Byt = ytoks.next()
            for hh in range(2):
                hq = 2 * g + hh
                hkv = hq // 4
                qT, BqT = qTs.next()

                def ev(tb, pt, Bp, qT=qT, BqT=BqT):
                    act(P, qT[:, tb * 512:(tb + 1) * 512], pt[0:64, :], AF.Copy, [Bp], [BqT])
                proj_fm(P, cx, None, W, BW, hq * 64, 64, ev, pst=pst.items)
                G, BG = Gs.next()
                P.dma("sync", G[:], cx.tblG[hq], writes=[BG])
                for Q in range(4):
                    acc, Bacc = accs.next()
                    plan = []
                    for delta in range(-1, 5):
                        j = 4 * Q + delta
                        if j < 0 or j > 15:
                            continue
                        qlo = max(0, delta - 1)
                        qhi = min(3, delta + 1)
                        firsts = [s for s in range(qlo, qhi + 1) if j == max(4 * Q + s - 1, 0)]
                        lasts = [s for s in range(qlo, qhi + 1) if j == min(4 * Q + s + 1, 15)]
                        c0 = 128 * (4 - delta) + qlo * 128
                        ncol = (qhi - qlo + 1) * 128
                        plan.append((j, qlo, qhi, "tbl", (G[:, c0:c0 + ncol], BG), firsts, lasts))
                    def post(acc=acc, Bacc=Bacc, hq=hq, Q=Q, hh=hh, ytok=ytok, Byt=Byt):
                        ts(P, den[:], acc[:, :, 64], esk[:, hq:hq + 1], ALU.add, [Bacc, Besk], [Bden])
                        P.op("vector", lambda e: e.reciprocal(out=den[:], in_=den[:]), reads=[Bden], writes=[Bden])
                        tt(P, ytok[:, Q * 4:(Q + 1) * 4, hh * 64:(hh + 1) * 64], acc[:, :, 0:64],
                           den[:].unsqueeze(2).to_broadcast([128, 4, 64]), ALU.mult, [Bacc, Bden], [Byt])
                    pipe.block(acc, Bacc, 65,
                               lambda qlo, ncol, qT=qT, Q=Q: qT[:, Q * 512 + qlo * 128:Q * 512 + qlo * 128 + ncol], BqT,
                               lambda j, hkv=hkv: kT[:, hkv, j * 128:(j + 1) * 128], BkT,
                               lambda j, hkv=hkv: vaug[:, j, hkv, :], Bva, plan, post)
            pipe.flush()
            yT, ByT = yTs.next()
            ytok_to_dram(P, cx, ytok, Byt, yT, ByT, pst, 8 + g)


def phase_diff(P, cx, l):
    w_in = cx.w_in[l]
    lam_init = 0.8 - 0.6 * math.exp(-0.3 * l)
    with P.scope():
        lamb, Blam = P.tile("lamb", [128, 4, 64], F32)
        P.dma("sync", lamb[:], cx.diff_lambda[l].partition_broadcast(128), writes=[Blam])
        lprod, Blp = P.tile("lprod", [128, 2, 64], F32)
        tt(P, lprod[:, 0, :], lamb[:, 0, :], lamb[:, 1, :], ALU.mult, [Blam], [Blp])
        tt(P, lprod[:, 1, :], lamb[:, 2, :], lamb[:, 3, :], ALU.mult, [Blam], [Blp])
        lsum, Bls = P.tile("lsum", [128, 2], F32)
        P.op("vector", lambda e: e.tensor_reduce(out=lsum[:], in_=lprod[:], axis=AX.X, op=ALU.add), reads=[Blp], writes=[Bls])
        act(P, lsum[:], lsum[:], AF.Exp, [Bls], [Bls])
        neglam, Bnl = P.tile("neglam", [128, 1], F32)
        tt(P, neglam[:], lsum[:, 1:2], lsum[:, 0:1], ALU.subtract, [Bls], [Bnl])
        ts(P, neglam[:], neglam[:], -lam_init, ALU.add, [Bnl], [Bnl])
        gsub, Bgs = P.tile("gsub", [128, 128], F32)
        P.dma("sync", gsub[:], cx.diff_subln_g[l, :].partition_broadcast(128), writes=[Bgs])
        ts(P, gsub[:], gsub[:], 1.0 - lam_init, ALU.mult, [Bgs], [Bgs])
        farb, Bfar = P.tile("farb_sb", [128, 16], F32)
        P.dma("sync", farb[:], cx.farb, writes=[Bfar])

        pst = Rot([P.ptile(f"dps{i}", [128, 512], F32) for i in range(1)])
        pss = Rot([P.ptile(f"dss{i}", [128, 512], F32) for i in range(3)])
        acc0, Bacc0 = P.ptile("dacc0", [128, 4, 256], F32)
        acc1, Bacc1 = P.ptile("dacc1", [128, 4, 256], F32)
        tmps = Rot([P.tile(f"dtmp{i}", [128, 512], F32) for i in range(3)])
        ets = Rot([P.tile(f"det{i}", [128, 512], BF16) for i in range(4)])
        Ws = Rot([P.tile(f"dW{i}", [128, DC, 384], BF16) for i in range(2)])
        qTs = Rot([P.tile(f"dq{i}", [64, 2, T], BF16) for i in range(2)])
        kTs = Rot([P.tile(f"dk{i}", [64, 2, T], BF16) for i in range(2)])
        vas = Rot([P.tile(f"dv{i}", [128, NT, 129], BF16) for i in range(2)])
        Gs = Rot([P.tile(f"dG{i}", [128, 1152], F32) for i in range(2)])
        ytoks = Rot([P.tile(f"dytok{i}", [128, NT, 128], BF16) for i in range(2)])
        yTs = Rot([P.tile(f"dyT{i}", [128, T], BF16) for i in range(2)])
        r0, Br0 = P.tile("dr0", [128, 4], F32)
        r1, Br1 = P.tile("dr1", [128, 4], F32)
        o0, Bo0 = P.tile("do0", [128, 4, 128], F32)
        o1, Bo1 = P.tile("do1", [128, 4, 128], F32)
        sq, Bsq = P.tile("dsq", [128, 4, 128], F32)
        ss, Bss = P.tile("dss_", [128, 4], F32)
        pipe = AttnPipe(P, pss, tmps, ets)
        for h in range(8):
            W, BW = Ws.next()
            load_w(P, W[:, :, 0:128], BW, w_in, O_DQ + h * 128, 128)
            load_w(P, W[:, :, 128:256], BW, w_in, O_DK + h * 128, 128)
            load_w(P, W[:, :, 256:384], BW, w_in, O_DV + h * 128, 128)
            qT, BqT = qTs.next()
            kT, BkT = kTs.next()
            for (dst, Bdst, wc) in ((qT, BqT, 0), (kT, BkT, 128)):
                def ev(tb, pt, Bp, dst=dst, Bdst=Bdst):
                    vcopy(P, dst[:, 0, tb * 512:(tb + 1) * 512], pt[0:64, :], [Bp], [Bdst])
                    act(P, dst[:, 1, tb * 512:(tb + 1) * 512], pt[64:128, :], AF.Copy, [Bp], [Bdst])
                proj_fm(P, cx, None, W, BW, wc, 128, ev, pst=pst.items)
            vaug, Bva = vas.next()
            memset(P, vaug[:, :, 128:129], 1.0, [Bva], eng="gpsimd")
            for i in range(NT):
                pt, Bp = pst.next()
                for c in range(DC):
                    mm(P, pt[:, 0:128], cx.xT[:, c, i * 128:(i + 1) * 128], W[:, c, 256:384], c == 0, c == DC - 1,
                       [cx.BxT, BW], [Bp])
                vcopy(P, vaug[:, i, 0:128], pt[:, 0:128], [Bp], [Bva])
            G, BG = Gs.next()
            P.dma("sync", G[:], cx.tblG[16 + h], writes=[BG])
            ytok, Byt = ytoks.next()
            for Q in range(4):
                for comp, (acc, Bacc) in enumerate(((acc0, Bacc0), (acc1, Bacc1))):
                    plan = []
                    for j in range(16):
                        delta = j - 4 * Q
                        fl = [0, 1, 2, 3] if j == 0 else []
                        ll = [0, 1, 2, 3] if j == 15 else []
                        if -1 <= delta <= 4:
                            c0 = 128 * (4 - delta)
                            plan.append((j, 0, 3, "tbl", (G[:, c0:c0 + 512], BG), fl, ll))
                        else:
                            ci = 2 * h + (1 if delta > 0 else 0)
                            plan.append((j, 0, 3, "far", (farb[:, ci:ci + 1], Bfar), fl, ll))
                    def post(Q=Q, ytok=ytok, Byt=Byt):
                        P.op("vector", lambda e: e.reciprocal(out=r0[:], in_=acc0[:, :, 128]), reads=[Bacc0], writes=[Br0])
                        P.op("vector", lambda e: e.reciprocal(out=r1[:], in_=acc1[:, :, 128]), reads=[Bacc1], writes=[Br1])
                        ts(P, r1[:], r1[:], neglam[:, 0:1], ALU.mult, [Br1, Bnl], [Br1])
                        tt(P, o0[:], acc0[:, :, 0:128], r0[:].unsqueeze(2).to_broadcast([128, 4, 128]), ALU.mult, [Bacc0, Br0], [Bo0])
                        tt(P, o1[:], acc1[:, :, 0:128], r1[:].unsqueeze(2).to_broadcast([128, 4, 128]), ALU.mult, [Bacc1, Br1], [Bo1])
                        tt(P, o0[:], o0[:], o1[:], ALU.add, [Bo0, Bo1], [Bo0], eng="gpsimd")
                        act(P, sq[:], o0[:], AF.Square, [Bo0], [Bsq])
                        P.op("vector", lambda e: e.tensor_reduce(out=ss[:], in_=sq[:], axis=AX.X, op=ALU.add), reads=[Bsq], writes=[Bss])
                        rsqrt(P, cx, ss[:], ss[:], Bss, 1.0 / 128.0, 1e-5)
                        tt(P, o0[:], o0[:], ss[:].unsqueeze(2).to_broadcast([128, 4, 128]), ALU.mult, [Bo0, Bss], [Bo0])
                        tt(P, ytok[:, Q * 4:(Q + 1) * 4, :], o0[:], gsub[:].unsqueeze(1).to_broadcast([128, 4, 128]), ALU.mult,
                           [Bo0, Bgs], [Byt], eng="gpsimd")
                    pipe.block(acc, Bacc, 129,
                               lambda qlo, ncol, qT=qT, Q=Q, comp=comp: qT[:, comp, Q * 512:(Q + 1) * 512], BqT,
                               lambda j, kT=kT, comp=comp: kT[:, comp, j * 128:(j + 1) * 128], BkT,
                               lambda j, vaug=vaug: vaug[:, j, :], Bva, plan, post if comp == 1 else None)
            pipe.flush()
            yT, ByT = yTs.next()
            ytok_to_dram(P, cx, ytok, Byt, yT, ByT, pst, 16 + h)


def phase_mem(P, cx, l):
    w_in = cx.w_in[l]
    with P.scope():
        Wkv, BWkv = P.tile("mWkv", [128, DC, 512], BF16)
        load_w(P, Wkv[:], BWkv, cx.mem_w_kv[l], 0, 512)
        Wq, BWq = P.tile("mWq", [128, DC, 256], BF16)
        load_w(P, Wq[:], BWq, w_in, O_MEMQ, 256)
        pst = Rot([P.ptile(f"mps{i}", [128, 512], F32) for i in range(2)])
        pss = Rot([P.ptile(f"mss{i}", [128, 512], F32) for i in range(3)])
        accs = Rot([P.ptile(f"macc{i}", [128, 4, 128], F32) for i in range(2)])
        ets = Rot([P.tile(f"met{i}", [128, 512], BF16) for i in range(4)])
        kmT, Bkm = P.tile("kmT", [64, 4, 256], BF16)
        for h in range(4):
            pt, Bp = pst.next()
            for c in range(DC):
                mm(P, pt[0:64, 0:256], Wkv[:, c, h * 64:(h + 1) * 64], cx.memT[:, c, :], c == 0, c == DC - 1,
                   [BWkv, cx.BmemT], [Bp])
            vcopy(P, kmT[:, h, :], pt[0:64, 0:256], [Bp], [Bkm])
        vm, Bvm = P.tile("vmaug", [128, 2, 4, 65], BF16)
        memset(P, vm[:, :, :, 64:65], 1.0, [Bvm], eng="gpsimd")
        for mt in range(2):
            pt, Bp = pst.next()
            for c in range(DC):
                mm(P, pt[:, 0:256], cx.memT[:, c, mt * 128:(mt + 1) * 128], Wkv[:, c, 256:512], c == 0, c == DC - 1,
                   [cx.BmemT, BWkv], [Bp])
            vcopy(P, vm[:, mt, :, 0:64], pt[:, 0:256].rearrange("p (h d) -> p h d", h=4), [Bp], [Bvm])
        qTs = Rot([P.tile(f"memq{i}", [64, T], BF16) for i in range(2)])
        ytoks = Rot([P.tile(f"mytok{i}", [128, NT, 128], BF16) for i in range(2)])
        yTs = Rot([P.tile(f"myT{i}", [128, T], BF16) for i in range(2)])
        den, Bden = P.tile("mden", [128, 4], F32)
        pipe = AttnPipe(P, pss, None, ets)
        for g in range(2):
            ytok, Byt = ytoks.next()
            for hh in range(2):
                h = 2 * g + hh
                qT, BqT = qTs.next()

                def ev(tb, pt, Bp, qT=qT, BqT=BqT):
                    act(P, qT[:, tb * 512:(tb + 1) * 512], pt[0:64, :], AF.Copy, [Bp], [BqT])
                proj_fm(P, cx, None, Wq, BWq, h * 64, 64, ev, pst=pst.items)
                for Q in range(4):
                    acc, Bacc = accs.next()
                    plan = [(mt, 0, 3, "none", None, [0, 1, 2, 3] if mt == 0 else [], [0, 1, 2, 3] if mt == 1 else [])
                            for mt in range(2)]
                    def post(acc=acc, Bacc=Bacc, Q=Q, hh=hh, ytok=ytok, Byt=Byt):
                        P.op("vector", lambda e, acc=acc: e.reciprocal(out=den[:], in_=acc[:, :, 64]), reads=[Bacc], writes=[Bden])
                        tt(P, ytok[:, Q * 4:(Q + 1) * 4, hh * 64:(hh + 1) * 64], acc[:, :, 0:64],
                           den[:].unsqueeze(2).to_broadcast([128, 4, 64]), ALU.mult, [Bacc, Bden], [Byt])
                    pipe.block(acc, Bacc, 65,
                               lambda qlo, ncol, qT=qT, Q=Q: qT[:, Q * 512:(Q + 1) * 512], BqT,
                               lambda j, h=h: kmT[:, h, j * 128:(j + 1) * 128], Bkm,
                               lambda j, h=h: vm[:, j, h, :], Bvm, plan, post)
            pipe.flush()
            yT, ByT = yTs.next()
            ytok_to_dram(P, cx, ytok, Byt, yT, ByT, pst, 24 + g)


def _t5_bucket_np(rel):
    half, exact = 16, 8
    n = np.abs(rel)
    nf = np.maximum(n, 1).astype(np.float32)
    large = exact + (np.log(nf / exact) / math.log(128 / exact) * (half - exact)).astype(np.int32)
    large = np.minimum(large, half - 1)
    return np.where(rel > 0, half, 0) + np.where(n < exact, n, large)


def host_tables(rel_bias):
    p = np.arange(128)[:, None]
    c = np.arange(1152)[None, :]
    rel = p - c + 512
    idx = _t5_bucket_np(rel)
    G = np.ascontiguousarray(np.transpose(rel_bias[idx], (2, 0, 1))).astype(np.float32)
    mask = (np.abs(rel) > 128)
    G[:16][:, mask] = -30000.0
    farb = np.empty((128, 16), np.float32)
    for h in range(8):
        farb[:, 2 * h] = rel_bias[15, 16 + h]
        farb[:, 2 * h + 1] = rel_bias[31, 16 + h]
    return G, farb


def declare_inputs(nc, cx, names_shapes):
    for nm, shp in names_shapes:
        setattr(cx, nm, nc.dram_tensor(nm, list(shp), F32, kind="ExternalInput").ap())


INPUT_SHAPES = [
    ("x", (T, D)), ("mem", (256, D)), ("w_in", (NL, D, 12288)), ("rwkv_w_up", (NL, 2, 64, 1024)),
    ("rwkv_a_up", (NL, 2, 64, 1024)), ("rwkv_g_up", (NL, 128, 1024)), ("win_sink", (NL, 16)),
    ("diff_lambda", (NL, 4, 64)), ("diff_subln_g", (NL, 128)), ("mem_w_kv", (NL, D, 512)),
    ("w_branch", (NL, 3328, D)), ("w_out", (NL, D, D)), ("ln1_g", (NL, D)), ("ln1_b", (NL, D)),
    ("router", (NL, D, 16)), ("exp_w_gate", (NL, 16, D, 2048)), ("exp_w_up", (NL, 16, D, 2048)),
    ("exp_w_down", (NL, 16, 2048, D)), ("ln2_g", (NL, D)), ("ln2_b", (NL, D)),
    ("tblG", (24, 128, 1152)), ("farb", (128, 16)), ("colp", (NL, 128, 160)),
]


def build_memT(P, cx):
    with P.scope():
        xin = [P.tile(f"min{i}", [128, D], F32) for i in range(2)]
        pst = [P.ptile(f"mtp{i}", [128, 512], F32) for i in range(2)]
        k = 0
        for i in range(2):
            xt_, Bx = xin[i]
            P.dma("sync", xt_[:], cx.mem[i * 128:(i + 1) * 128, :], writes=[Bx])
            for half in range(2):
                pt, Bp = pst[k % 2]
                k += 1
                for j in range(4):
                    c = half * 4 + j
                    tr(P, pt[:, j * 128:(j + 1) * 128], xt_[:, c * 128:(c + 1) * 128], cx.ident_f[:],
                       [Bx, cx.Bident_f], [Bp])
                vcopy(P, cx.memT[:, half * 4:half * 4 + 4, i * 128:(i + 1) * 128],
                      pt[:].rearrange("p (j t) -> p j t", j=4), [Bp], [cx.BmemT])


BR_CHUNKS = [(0, 8), (8, 16), (16, 24), (24, 26)]


def layer_norm_tile(P, cx, pre, Bpre, gb, Bgb, bb, Bbb, stats, Bst, mv, Bmv, out, Bout):
    for k in range(2):
        P.op("vector", lambda e, k=k: e.bn_stats(out=stats[:, k, :], in_=pre[:, k * 512:(k + 1) * 512]),
             reads=[Bpre], writes=[Bst])
    P.op("vector", lambda e: e.bn_aggr(out=mv[:, 0:2], in_=stats[:].rearrange("p a b -> p (a b)")), reads=[Bst], writes=[Bmv])
    ts(P, mv[:, 2:3], mv[:, 1:2], 1.0, ALU.mult, [Bmv], [Bmv], s2=1e-5, op1=ALU.add)
    act(P, mv[:, 2:3], mv[:, 2:3], AF.Sqrt, [Bmv], [Bmv])
    P.op("vector", lambda e: e.reciprocal(out=mv[:, 2:3], in_=mv[:, 2:3]), reads=[Bmv], writes=[Bmv])
    ts(P, pre, pre, mv[:, 0:1], ALU.subtract, [Bpre, Bmv], [Bpre], s2=mv[:, 2:3], op1=ALU.mult)
    tt(P, pre, pre, gb, ALU.mult, [Bpre, Bgb], [Bpre], eng="gpsimd")
    tt(P, out, pre, bb, ALU.add, [Bpre, Bbb], [Bout])


def phase_merge(P, cx, l, xres_dram):
    w_in = cx.w_in[l]
    with P.scope():
        Wo, BWo = P.tile("Wo", [128, DC, 1024], BF16)
        load_w(P, Wo[:, :, 0:512], BWo, cx.w_out[l], 0, 512)
        load_w(P, Wo[:, :, 512:1024], BWo, cx.w_out[l], 512, 512)
        Rb, BRb = P.tile("Rb", [128, DC, 16], BF16)
        load_w(P, Rb[:], BRb, cx.router[l], 0, 16)
        g1, Bg1 = P.tile("g1b", [128, D], F32)
        b1, Bb1 = P.tile("b1b", [128, D], F32)
        P.dma("sync", g1[:], cx.ln1_g[l, :].partition_broadcast(128), writes=[Bg1])
        P.dma("sync", b1[:], cx.ln1_b[l, :].partition_broadcast(128), writes=[Bb1])
        ybr, Bybr = P.tile("ybr", [128, 26, 1024], BF16)
        mT, BmT = P.tile("mergedT", [128, DC, 1024], BF16)
        Wbs = Rot([P.tile(f"Wb{i}", [128, 26, 128], BF16) for i in range(2)])
        Wgs = Rot([P.tile(f"Wg{i}", [128, DC, 4, 128], BF16) for i in range(2)])
        psZ = Rot([P.ptile(f"psZ{i}", [128, 512], F32) for i in range(2)])
        psG = Rot([P.ptile(f"psG{i}", [128, 512], F32) for i in range(2)])
        psO, BpsO = P.ptile("psO", [128, 1024], F32)
        psT, BpsT = P.ptile("psT", [128, 512], F32)
        psR, BpsR = P.ptile("psR", [128, 512], F32)
        sgs = Rot([P.tile(f"sg{i}", [128, 512], F32) for i in range(2)])
        macc, Bmacc = P.tile("macc", [128, 512], F32)
        mtmp, Bmtmp = P.tile("mtmp", [128, 512], F32)
        xrs = Rot([P.tile(f"xr{i}", [128, D], F32) for i in range(2)])
        pres = Rot([P.tile(f"pre{i}", [128, D], F32) for i in range(2)])
        x1s = Rot([P.tile(f"x1o{i}", [128, D], F32) for i in range(2)])
        x1b, Bx1b = P.tile("x1b_t", [128, D], BF16)
        x1T, Bx1T = P.tile("x1T_t", [128, DC, 128], BF16)
        stats, Bst = P.tile("lnstats", [128, 2, 6], F32)
        mv, Bmv = P.tile("lnmv", [128, 4], F32)
        sm, Bsm = P.tile("rsm", [128, 4], F32)
        ex, Bex = P.tile("rex", [128, 16], F32)
        ybr_src = cx.ybrT.rearrange("(c p) t -> p c t", p=128)
        for hf in range(2):
            t0 = hf * 1024
            for c0 in range(0, 26, 13):
                P.dma("sync", ybr[:, c0:c0 + 13, :], ybr_src[:, c0:c0 + 13, t0:t0 + 1024], reads=cx.BybrT, writes=[Bybr])
            for dc in range(DC):
                Wb, BWb = Wbs.next()
                load_w(P, Wb[:], BWb, cx.w_branch[l], dc * 128, 128, nchunks=26)
                Wg, BWg = Wgs.next()
                for b in range(4):
                    load_w(P, Wg[:, :, b, :], BWg, w_in, O_GATE + b * 1024 + dc * 128, 128)
                for tb in range(2):
                    for b in range(4):
                        ca, cb = BR_CHUNKS[b]
                        pz, Bpz = psZ.next()
                        for c in range(ca, cb):
                            mm(P, pz[:], Wb[:, c, :], ybr[:, c, tb * 512:(tb + 1) * 512], c == ca, c == cb - 1, [BWb, Bybr], [Bpz])
                        pg, Bpg = psG.next()
                        for c in range(DC):
                            mm(P, pg[:], Wg[:, c, b, :], cx.xT[:, c, t0 + tb * 512:t0 + (tb + 1) * 512], c == 0, c == DC - 1,
                               [BWg, cx.BxT], [Bpg])
                        sg, Bsg = sgs.next()
                        act(P, sg[:], pg[:], AF.Sigmoid, [Bpg], [Bsg])
                        if b == 0:
                            tt(P, macc[:], pz[:], sg[:], ALU.mult, [Bpz, Bsg], [Bmacc])
                        else:
                            tt(P, mtmp[:], pz[:], sg[:], ALU.mult, [Bpz, Bsg], [Bmtmp])
                            if b < 3:
                                tt(P, macc[:], macc[:], mtmp[:], ALU.add, [Bmacc, Bmtmp], [Bmacc], eng="gpsimd")
                            else:
                                tt(P, mT[:, dc, tb * 512:(tb + 1) * 512], macc[:], mtmp[:], ALU.add, [Bmacc, Bmtmp], [BmT],
                                   eng="gpsimd")
            for i in range(8):
                ti = hf * 8 + i
                for dh in range(2):
                    for dc in range(DC):
                        mm(P, psO[:, dh * 512:(dh + 1) * 512], mT[:, dc, i * 128:(i + 1) * 128], Wo[:, dc, dh * 512:(dh + 1) * 512],
                           dc == 0, dc == DC - 1, [BmT, BWo], [BpsO])
                xr, Bxr = xrs.next()
                P.dma("sync", xr[:], xres_dram[ti * 128:(ti + 1) * 128, :], writes=[Bxr])
                pre, Bpre = pres.next()
                stt(P, pre[:], xr[:], DN_ALPHA, psO[:], ALU.mult, ALU.add, [Bxr, BpsO], [Bpre])
                x1, Bx1 = x1s.next()
                layer_norm_tile(P, cx, pre[:], Bpre, g1[:], Bg1, b1[:], Bb1, stats, Bst, mv, Bmv, x1[:], Bx1)
                P.dma("sync", cx.x1res[ti * 128:(ti + 1) * 128, :], x1[:], reads=[Bx1], writes=[cx.Bx1res])
                act(P, x1b[:], x1[:], AF.Copy, [Bx1], [Bx1b])
                ptb = psT[:].bitcast(BF16)
                for c in range(DC):
                    tr(P, ptb[:, c * 128:(c + 1) * 128], x1b[:, c * 128:(c + 1) * 128], cx.ident_b[:], [Bx1b, cx.Bident_b], [BpsT])
                vcopy(P, x1T[:], ptb[:].rearrange("p (c t) -> p c t", c=DC), [BpsT], [Bx1T])
                for c in range(DC):
                    mm(P, psR[:, 0:16], x1T[:, c, :], Rb[:, c, :], c == 0, c == DC - 1, [Bx1T, BRb], [BpsR])
                P.op("vector", lambda e: e.tensor_reduce(out=sm[:, 0:1], in_=psR[:, 0:16], axis=AX.X, op=ALU.max),
                     reads=[BpsR], writes=[Bsm])
                ts(P, sm[:, 1:2], sm[:, 0:1], -1.0, ALU.mult, [Bsm], [Bsm])
                act(P, ex[:], psR[:, 0:16], AF.Exp, [BpsR, Bsm], [Bex], bias=sm[:, 1:2], scale=1.0)
                P.op("vector", lambda e: e.tensor_reduce(out=sm[:, 2:3], in_=ex[:], axis=AX.X, op=ALU.add), reads=[Bex], writes=[Bsm])
                P.op("vector", lambda e: e.reciprocal(out=sm[:, 2:3], in_=sm[:, 2:3]), reads=[Bsm], writes=[Bsm])
                ts(P, cx.aff_tok[:, ti, :], ex[:], sm[:, 2:3], ALU.mult, [Bex, Bsm], [cx.Baff_tok])


def phase_moe(P, cx, l, out_dram, Bout):
    with P.scope():
        acc, Bacc = P.tile("moe_acc", [128, NT, D], F32)
        x1b, Bx1b = P.tile("moe_x1b", [128, NT, D], BF16)
        posb, Bposb = P.tile("tk_posb", [16, T], BF16)
        pgen = Rot([P.ptile(f"pgen{i}", [128, 512], F32) for i in range(2)])
        ptok, Bptok = P.tile("tk_ptok", [128, NT, 16], F32)
        Eall, BEall = P.tile("tk_E", [16, 16, 128], BF16)
        jidx, Bjidx = P.tile("tk_jidx", [128, 2], F32)
        with P.scope():
            xrs = Rot([P.tile(f"mxr{i}", [128, D], F32) for i in range(2)])
            for i in range(NT):
                xr, Bxr = xrs.next()
                P.dma("sync", xr[:], cx.x1res[i * 128:(i + 1) * 128, :], reads=[cx.Bx1res], writes=[Bxr])
                act(P, acc[:, i, :], xr[:], AF.Copy, [Bxr], [Bacc], scale=DN_ALPHA)
                vcopy(P, x1b[:, i, :], xr[:], [Bxr], [Bx1b])
            affT, BaffT = P.tile("affT", [16, T], F32)
            for q4 in range(4):
                pt, Bp = pgen.next()
                for j in range(4):
                    ti = q4 * 4 + j
                    tr(P, pt[0:16, j * 128:(j + 1) * 128], cx.aff_tok[:, ti, :], cx.ident_f[:], [cx.Baff_tok, cx.Bident_f], [Bp])
                vcopy(P, affT[0:16, q4 * 512:(q4 + 1) * 512], pt[0:16, :], [Bp], [BaffT])
            lo, Blo = P.tile("tk_lo", [16, 1], F32)
            hi, Bhi = P.tile("tk_hi", [16, 1], F32)
            mid, Bmid = P.tile("tk_mid", [16, 1], F32)
            cnt, Bcnt = P.tile("tk_cnt", [16, 1], F32)
            ge, Bge = P.tile("tk_ge", [16, 1], F32)
            dd, Bdd = P.tile("tk_d", [16, 1], F32)
            junk, Bjunk = P.tile("tk_junk", [16, T], F32)
            memset(P, lo[:], 0.0, [Blo])
            memset(P, hi[:], 1.0, [Bhi])
            for it in range(34):
                tt(P, mid[:], lo[:], hi[:], ALU.add, [Blo, Bhi], [Bmid])
                ts(P, mid[:], mid[:], 0.5, ALU.mult, [Bmid], [Bmid])
                ts(P, junk[:], affT[0:16, :], mid[:, 0:1], ALU.is_ge, [BaffT, Bmid], [Bjunk, Bcnt], s2=0.0, op1=ALU.add, accum_out=cnt[:])
                ts(P, ge[:], cnt[:], 255.5, ALU.is_ge, [Bcnt], [Bge])
                tt(P, dd[:], mid[:], lo[:], ALU.subtract, [Bmid, Blo], [Bdd])
                stt(P, lo[:], dd[:], ge[:, 0:1], lo[:], ALU.mult, ALU.add, [Bdd, Bge, Blo], [Blo])
                tt(P, dd[:], hi[:], mid[:], ALU.subtract, [Bhi, Bmid], [Bdd])
                stt(P, hi[:], dd[:], ge[:, 0:1], mid[:], ALU.mult, ALU.add, [Bdd, Bge, Bmid], [Bhi])
            mask, Bmask = P.tile("tk_mask", [16, T], F32)
            ts(P, mask[:], affT[0:16, :], lo[:, 0:1], ALU.is_ge, [BaffT, Blo], [Bmask])
            ones16, Bo16 = P.tile("tk_ones", [16, T], F32)
            memset(P, ones16[:], 1.0, [Bo16])
            posm, Bposm = P.tile("tk_posm", [16, T], F32)
            P.op("vector", lambda e: e.tensor_tensor_scan(out=posm[:], data0=ones16[:], data1=mask[:], initial=0.0, op0=ALU.mult, op1=ALU.add),
                 reads=[Bo16, Bmask], writes=[Bposm])
            tt(P, posm[:], posm[:], mask[:], ALU.mult, [Bposm, Bmask], [Bposm])
            ts(P, posm[:], posm[:], -1.0, ALU.add, [Bposm], [Bposm])
            vcopy(P, posb[:], posm[:], [Bposm], [Bposb])
            for half in range(2):
                pt, Bp = pgen.next()
                for j in range(8):
                    i = half * 8 + j
                    tr(P, pt[:, j * 16:(j + 1) * 16], posm[0:16, i * 128:(i + 1) * 128], cx.ident_f[0:16, 0:16], [Bposm, cx.Bident_f], [Bp])
                vcopy(P, ptok[:, half * 8:(half + 1) * 8, :], pt[:, 0:128].rearrange("p (j e) -> p j e", j=8), [Bp], [Bptok])
            vcopy(P, Eall[:], cx.ident_b[0:16, 0:16].unsqueeze(2).to_broadcast([16, 16, 128]), [cx.Bident_b], [BEall])
            vcopy(P, jidx[:, 0:1], cx.pidx[:], [cx.Bpidx], [Bjidx])
            ts(P, jidx[:, 1:2], cx.pidx[:], 128.0, ALU.add, [cx.Bpidx], [Bjidx])

        with P.scope():
            Sel, BSel = P.tile("Sel", [128, NT, 256], BF16)
            SelT, BSelT = P.tile("SelT", [128, 2, T], BF16)
            xeT, BxeT = P.tile("xeT", [128, DC, 256], BF16)
            hT, BhT = P.tile("hT", [128, 16, 256], BF16)
            ye, Bye = P.tile("ye_sb", [128, 2, D], BF16)
            Wgs = Rot([P.tile(f"eWg{i}", [128, DC, 512], BF16) for i in range(2)])
            Wus = Rot([P.tile(f"eWu{i}", [128, DC, 512], BF16) for i in range(2)])
            Wds = Rot([P.tile(f"eWd{i}", [128, 4, D], BF16) for i in range(2)])
            pgu = Rot([P.ptile(f"pgu{i}", [128, 512], F32) for i in range(2)])
            psY = [[P.ptile(f"psY{i}_{j}", [128, 512], F32) for j in range(2)] for i in range(2)]
            pscat = Rot(pgen.items + [psY[0][0], psY[0][1], psY[1][0], psY[1][1]])
            sgs = Rot([P.tile(f"esg{i}", [128, 256], F32) for i in range(2)])
            for e in range(16):
                tt(P, Sel[:], cx.colidx[:, 0:256].unsqueeze(1).to_broadcast([128, NT, 256]),
                   ptok[:, :, e:e + 1].to_broadcast([128, NT, 256]), ALU.is_equal, [cx.Bcolidx, Bptok], [BSel])
                for c in range(DC):
                    pt, Bp = pgen.next()
                    for i in range(NT):
                        mm(P, pt[:, 0:256], x1b[:, i, c * 128:(c + 1) * 128], Sel[:, i, :], i == 0, i == NT - 1, [Bx1b, BSel], [Bp])
                    if c % 2 == 0:
                        vcopy(P, xeT[:, c, :], pt[:, 0:256], [Bp], [BxeT])
                    else:
                        act(P, xeT[:, c, :], pt[:, 0:256], AF.Copy, [Bp], [BxeT])
                for tb in range(4):
                    pt, Bp = pgen.next()
                    mm(P, pt[:], Eall[:, e, :], posb[0:16, tb * 512:(tb + 1) * 512], True, True, [BEall, Bposb], [Bp])
                    for jt in range(2):
                        ts(P, SelT[:, jt, tb * 512:(tb + 1) * 512], pt[:], jidx[:, jt:jt + 1], ALU.is_equal, [Bp, Bjidx], [BSelT])
                for fq in range(4):
                    Wg, BWg = Wgs.next()
                    Wu, BWu = Wus.next()
                    Wd, BWd = Wds.next()
                    load_w(P, Wg[:], BWg, cx.exp_w_gate[l, e], fq * 512, 512)
                    load_w(P, Wu[:], BWu, cx.exp_w_up[l, e], fq * 512, 512)
                    load_w(P, Wd[:], BWd, cx.exp_w_down[l, e], 0, 1024, r0=fq * 512, nchunks=4)
                    for fc in range(4):
                        F = fq * 4 + fc
                        pG, BpG = pgu.next()
                        for c in range(DC):
                            mm(P, pG[:, 0:256], Wg[:, c, fc * 128:(fc + 1) * 128], xeT[:, c, :], c == 0, c == DC - 1, [BWg, BxeT], [BpG])
                        pU, BpU = pgu.next()
                        for c in range(DC):
                            mm(P, pU[:, 0:256], Wu[:, c, fc * 128:(fc + 1) * 128], xeT[:, c, :], c == 0, c == DC - 1, [BWu, BxeT], [BpU])
                        sg, Bsg = sgs.next()
                        act(P, sg[:], pG[:, 0:256], AF.Silu, [BpG], [Bsg])
                        tt(P, hT[:, F, :], pU[:, 0:256], sg[:], ALU.mult, [BpU, Bsg], [BhT])
                    for jt in range(2):
                        for dh in range(2):
                            pY, BpY = psY[jt][dh]
                            for fc in range(4):
                                mm(P, pY[:], hT[:, fq * 4 + fc, jt * 128:(jt + 1) * 128],
                                   Wd[:, fc, dh * 512:(dh + 1) * 512], fq == 0 and fc == 0, fq == 3 and fc == 3, [BhT, BWd], [BpY])
                for jt in range(2):
                    for dh in range(2):
                        pY, BpY = psY[jt][dh]
                        if dh == 0:
                            vcopy(P, ye[:, jt, dh * 512:(dh + 1) * 512], pY[:], [BpY], [Bye])
                        else:
                            act(P, ye[:, jt, dh * 512:(dh + 1) * 512], pY[:], AF.Copy, [BpY], [Bye])
                for i in range(NT):
                    for dh in range(2):
                        pt, Bp = pscat.next()
                        for jt in range(2):
                            mm(P, pt[:], SelT[:, jt, i * 128:(i + 1) * 128], ye[:, jt, dh * 512:(dh + 1) * 512], jt == 0, jt == 1,
                               [BSelT, Bye], [Bp])
                        stt(P, acc[:, i, dh * 512:(dh + 1) * 512], pt[:], cx.aff_tok[:, i, e:e + 1], acc[:, i, dh * 512:(dh + 1) * 512],
                            ALU.mult, ALU.add, [Bp, cx.Baff_tok, Bacc], [Bacc])
        g2, Bg2 = P.tile("g2b", [128, D], F32)
        b2, Bb2 = P.tile("b2b", [128, D], F32)
        P.dma("sync", g2[:], cx.ln2_g[l, :].partition_broadcast(128), writes=[Bg2])
        P.dma("sync", b2[:], cx.ln2_b[l, :].partition_broadcast(128), writes=[Bb2])
        stats, Bst = P.tile("ln2stats", [128, 2, 6], F32)
        mv, Bmv = P.tile("ln2mv", [128, 4], F32)
        outs = Rot([P.tile(f"x2o{i}", [128, D], F32) for i in range(2)])
        for i in range(NT):
            o, Bo = outs.next()
            layer_norm_tile(P, cx, acc[:, i, :], Bacc, g2[:], Bg2, b2[:], Bb2, stats, Bst, mv, Bmv, o[:], Bo)
            P.dma("sync", out_dram[i * 128:(i + 1) * 128, :], o[:], reads=[Bo], writes=[Bout])


CH = 64
NCH = T // CH
DBG = {"groups": 8, "stop": 99}


def project_shift(P, cx, rc, W, BW, wcol, ch, dst, Bdst, pst):
    raw, Braw = rc.raw, rc.Braw

    def ev(tb, pt, Bp):
        act(P, raw[:, tb * 512:(tb + 1) * 512], pt[:, :], AF.Copy, [Bp], [Braw])
    proj_fm(P, cx, None, W, BW, wcol, 128, ev, pst=pst)
    ts(P, dst[:, :], raw[:, :], rc.cmix[:, ch:ch + 1], ALU.mult, [Braw, rc.Bcmix], [Bdst])
    stt(P, dst[:, 1:T], raw[:, 0:T - 1], rc.colp[:, ch:ch + 1], dst[:, 1:T], ALU.mult, ALU.add, [Braw, rc.Bcolp, Bdst], [Bdst])
    stt(P, dst[:, 0:T - 1], raw[:, 1:T], rc.colp[:, 26 + ch:27 + ch], dst[:, 0:T - 1], ALU.mult, ALU.add,
        [Braw, rc.Bcolp, Bdst], [Bdst])


def phase_rwkv(P, cx, l):
    w_in = cx.w_in[l]
    rc = Ctx()
    with P.scope():
        rc.colp, rc.Bcolp = P.tile("colp", [128, 160], F32)
        P.dma("sync", rc.colp[:], cx.colp[l], writes=[rc.Bcolp])
        colp = rc.colp
        Bcolp = rc.Bcolp
        rc.cmix, rc.Bcmix = P.tile("cmix", [128, 26], F32)
        tt(P, rc.cmix[:], colp[:, 0:26], colp[:, 26:52], ALU.add, [Bcolp], [rc.Bcmix])
        ts(P, rc.cmix[:], rc.cmix[:], -1.0, ALU.mult, [rc.Bcmix], [rc.Bcmix], s2=1.0, op1=ALU.add)
        omka, Bomka = P.tile("omka", [128, 8], F32)
        ts(P, omka[:], colp[:, 92:100], -1.0, ALU.mult, [Bcolp], [Bomka], s2=1.0, op1=ALU.add)
        d64i, Bd64i = P.tile("d64i", [128, 64], I32)
        P.op("gpsimd", lambda e: e.iota(d64i[0:64, :], pattern=[[1, 64]], base=0, channel_multiplier=-1), writes=[Bd64i])
        P.op("gpsimd", lambda e: e.iota(d64i[64:128, :], pattern=[[1, 64]], base=0, channel_multiplier=-1), writes=[Bd64i])
        d64, Bd64 = P.tile("d64f", [128, 64], F32)
        vcopy(P, d64[:], d64i[:], [Bd64i], [Bd64])
        mk, Bmk = P.tile("mk4", [128, 4, 64], F32)
        ts(P, mk[:, 0, :], d64[:], 0.0, ALU.is_gt, [Bd64], [Bmk])
        ts(P, mk[:, 1, :], d64[:], 0.0, ALU.is_ge, [Bd64], [Bmk])
        ts(P, mk[:, 2, :], d64[:], 0.0, ALU.is_lt, [Bd64], [Bmk])
        ts(P, mk[:, 3, :], d64[:], 0.0, ALU.is_le, [Bd64], [Bmk])
        identblk, Bidb = P.tile("identblk", [128, 64], F32)
        ts(P, identblk[:], d64[:], 0.0, ALU.is_equal, [Bd64], [Bidb])
        maskA, BmaskA = P.tile("maskA", [128, 2, 2, 4, 64], F32)
        maskB, BmaskB = P.tile("maskB", [128, 2, 2, 64], F32)
        for z in range(2):
            st_i, in_i = (0, 1) if z == 0 else (2, 3)
            ot_i = 2 if z == 0 else 0
            for hh in range(2):
                ts(P, maskA[:, z, hh, 0, :], mk[:, st_i, :], -1.0, ALU.mult, [Bmk], [BmaskA])
                ts(P, maskA[:, z, hh, 1, :], mk[:, in_i, :], -1.0, ALU.mult, [Bmk], [BmaskA])
                vcopy(P, maskA[:, z, hh, 2, :], mk[:, st_i, :], [Bmk], [BmaskA])
                vcopy(P, maskA[:, z, hh, 3, :], mk[:, in_i, :], [Bmk], [BmaskA])
                ts(P, maskB[:, z, hh, :], mk[:, ot_i, :], -1.0, ALU.mult, [Bmk], [BmaskB])
        rst, Brst = P.tile("rst", [128, 512], F32)
        memset(P, rst[:], 1.0, [Brst])
        memset(P, rst[:].rearrange("p (n c) -> p n c", c=CH)[:, :, 0:1], 0.0, [Brst])
        wa_up, Bwa = P.tile("wa_up", [128, 2, 1024], BF16)
        for z in range(2):
            P.dma("gpsimd", wa_up[0:64, z, :], cx.rwkv_w_up[l, z], writes=[Bwa])
            P.dma("gpsimd", wa_up[64:128, z, :], cx.rwkv_a_up[l, z], writes=[Bwa])
        g_up, Bgup = P.tile("g_up", [128, 1024], BF16)
        P.dma("gpsimd", g_up[:], cx.rwkv_g_up[l], writes=[Bgup])
        lin, Blin = P.tile("lin", [128, T], BF16)
        sdg, Bsdg = P.tile("sdg", [128, T], BF16)
        with P.scope():
            rc.raw, rc.Braw = P.tile("raw", [128, T], F32)
            Wl, BWl = P.tile("Wl", [128, DC, 256], BF16)
            load_w(P, Wl[:], BWl, w_in, 3072, 256)
            pst = [P.ptile(f"lps{i}", [128, 512], F32) for i in range(2)]
            sh, Bsh = P.tile("lsh", [128, T], F32)
            project_shift(P, cx, rc, Wl, BWl, 0, 24, sh, Bsh, pst)
            act(P, lin[0:64, :], sh[0:64, :], AF.Tanh, [Bsh], [Blin])
            vcopy(P, lin[64:128, :], sh[64:128, :], [Bsh], [Blin])
            project_shift(P, cx, rc, Wl, BWl, 128, 25, sh, Bsh, pst)
            act(P, sdg[:], sh[:], AF.Sigmoid, [Bsh], [Bsdg])
        for g in range(DBG["groups"]):
            if DBG["stop"] < 1:
                break
            rwkv_group(P, cx, rc, l, g, maskA, BmaskA, maskB, BmaskB, identblk, Bidb, rst, Brst, omka, Bomka,
                       wa_up, Bwa, g_up, Bgup, lin, Blin, sdg, Bsdg)


def rwkv_group(P, cx, rc, l, g, maskA, BmaskA, maskB, BmaskB, identblk, Bidb, rst, Brst, omka, Bomka,
               wa_up, Bwa, g_up, Bgup, lin, Blin, sdg, Bsdg):
    w_in = cx.w_in[l]
    colp, Bcolp = rc.colp, rc.Bcolp
    gc = slice(g * 128, (g + 1) * 128)
    with P.scope():
        AR = [P.tile(f"AR{z}", [128, NCH, 2, CH], BF16) for z in range(2)]
        KT = [P.tile(f"KT{z}", [128, T], BF16) for z in range(2)]
        BT = [P.tile(f"BT{z}", [128, T], BF16) for z in range(2)]
        Ktok = [P.tile(f"Ktok{z}", [128, NCH, CH], BF16) for z in range(2)]
        nBtok = [P.tile(f"nBtok{z}", [128, NCH, CH], BF16) for z in range(2)]
        Vtok, BVtok = P.tile("Vtok", [128, NCH, CH], BF16)
        gamC = [P.tile(f"gamC{z}", [128, NCH], F32) for z in range(2)]
        bonus, Bbonus = P.tile("bonus", [128, T], F32)
        gate, Bgate = P.tile("gate_g", [128, T], BF16)
        with P.scope():
            Wr, BWr = P.tile("Wr", [128, DC, 3, 128], BF16)
            for j in range(3):
                load_w(P, Wr[:, :, j, :], BWr, w_in, j * 1024 + g * 128, 128)
            pst = [P.ptile(f"gps{i}", [128, 512], F32) for i in range(2)]
            psA = Rot([P.ptile(f"gpa{i}", [128, 512], F32) for i in range(3)])
            psTb = Rot([P.ptile(f"gpt{i}", [128, 512], F32) for i in range(2)])
            r_s, Br = P.tile("r_s", [128, T], F32)
            k_s, Bk = P.tile("k_s", [128, T], F32)
            v_s, Bv = P.tile("v_s", [128, T], F32)
            Wr2 = Wr[:].rearrange("p c j n -> p c (j n)")
            with P.scope():
                rc.raw, rc.Braw = P.tile("raw", [128, T], F32)
                project_shift(P, cx, rc, Wr2, BWr, 0, g, r_s, Br, pst)
                project_shift(P, cx, rc, Wr2, BWr, 128, 8 + g, k_s, Bk, pst)
                project_shift(P, cx, rc, Wr2, BWr, 256, 16 + g, v_s, Bv, pst)
            kkc = colp[:, 84 + g:85 + g]
            kac = colp[:, 92 + g:93 + g]
            rkc = colp[:, 100 + g:101 + g]
            tmp = {}
            for nm in ("sq", "rn", "kk", "sg", "az", "cs", "incl", "e1", "e2", "t1", "kd", "kd0", "kka"):
                tmp[nm] = P.tile("g_" + nm, [128, 512], F32)
            tmp["u"] = tmp["sq"]
            tmpr = {nm: Rot([tmp[nm], P.tile("g2_" + nm, [128, 512], F32)]) for nm in ("sg", "az", "cs", "incl", "e1", "e2", "t1", "kka")}
            tb16 = Rot([P.tile(f"g_tb16_{i}", [128, 512], BF16) for i in range(2)])
            for tb in range(4):
                sl = slice(tb * 512, (tb + 1) * 512)
                cs8 = slice(tb * 8, (tb + 1) * 8)
                sq, Bsq = tmp["sq"]
                act(P, sq[:], k_s[:, sl], AF.Square, [Bk, Bcolp], [Bsq], scale=kkc)
                pa, Bpa = psA.next()
                mm(P, pa[:], cx.blk[:], sq[:], True, True, [cx.Bblk, Bsq], [Bpa])
                rn, Brn = tmp["rn"]
                ts(P, rn[:], pa[:], 1e-12, ALU.max, [Bpa], [Brn])
                act(P, rn[:], rn[:], AF.Sqrt, [Brn], [Brn])
                P.op("vector", lambda e, rn=rn: e.reciprocal(out=rn[:], in_=rn[:]), reads=[Brn], writes=[Brn])
                kk, Bkk = tmp["kk"]
                stt(P, kk[:], k_s[:, sl], kkc, rn[:], ALU.mult, ALU.mult, [Bk, Bcolp, Brn], [Bkk])
                kd0, Bkd0 = tmp["kd0"]
                def chain(z, tb=tb, sl=sl, cs8=cs8, kk=kk, Bkk=Bkk, kd0=kd0, Bkd0=Bkd0):
                    ARt, BAR = AR[z]
                    sg, Bsg = tmpr["sg"].next()
                    az, Baz = tmpr["az"].next()
                    pa, Bpa = psA.next()
                    mm(P, pa[:], wa_up[0:64, z, gc], lin[0:64, sl], True, True, [Bwa, Blin], [Bpa])
                    act(P, sg[:], pa[:], AF.Sigmoid, [Bpa, Bcolp], [Bsg], bias=colp[:, 52 + z * 8 + g:53 + z * 8 + g], scale=1.0)
                    pa, Bpa = psA.next()
                    mm(P, pa[:], wa_up[64:128, z, gc], lin[64:128, sl], True, True, [Bwa, Blin], [Bpa])
                    act(P, az[:], pa[:], AF.Sigmoid, [Bpa, Bcolp], [Baz], bias=colp[:, 68 + z * 8 + g:69 + z * 8 + g], scale=1.0)
                    yield
                    cs, Bcs = tmpr["cs"].next()
                    P.op("vector", lambda e, cs=cs, sg=sg: e.tensor_tensor_scan(out=cs[:], data0=rst[:], data1=sg[:], initial=0.0,
                                                                          op0=ALU.mult, op1=ALU.add),
                         reads=[Brst, Bsg], writes=[Bcs])
                    cs3 = cs[:].rearrange("p (n c) -> p n c", c=CH)
                    totb = cs3[:, :, CH - 1:CH].to_broadcast([128, 8, CH])
                    gC, BgC = gamC[z]
                    act(P, gC[:, cs8], cs3[:, :, CH - 1], AF.Exp, [Bcs], [BgC], scale=-C_DECAY)
                    yield
                    if z == 0:
                        incl, Bincl = cs, Bcs
                    else:
                        incl, Bincl = tmpr["incl"].next()
                        i3 = incl[:].rearrange("p (n c) -> p n c", c=CH)
                        tt(P, i3, totb, cs3, ALU.subtract, [Bcs], [Bincl])
                        tt(P, incl[:], incl[:], sg[:], ALU.add, [Bincl, Bsg], [Bincl], eng="gpsimd")
                    i3 = incl[:].rearrange("p (n c) -> p n c", c=CH)
                    e1, Be1 = tmpr["e1"].next()
                    e2, Be2 = tmpr["e2"].next()
                    t1, Bt1 = tmpr["t1"].next()
                    kd, Bkd = (kd0, Bkd0) if z == 0 else tmp["kd"]
                    kka, Bkka = tmpr["kka"].next()
                    ts(P, t1[:], az[:], kac, ALU.mult, [Baz, Bcolp, Bomka], [Bt1], s2=omka[:, g:g + 1], op1=ALU.add)
                    tt(P, kd[:], t1[:], k_s[:, sl], ALU.mult, [Bt1, Bk], [Bkd])
                    tt(P, kka[:], az[:], kk[:], ALU.mult, [Baz, Bkk], [Bkka], eng="gpsimd")
                    yield
                    act(P, e1[:], incl[:], AF.Exp, [Bincl], [Be1], scale=-C_DECAY)
                    tt(P, ARt[:, cs8, 1, :], r_s[:, sl].rearrange("p (n c) -> p n c", c=CH), e1[:].rearrange("p (n c) -> p n c", c=CH),
                       ALU.mult, [Br, Be1], [BAR])
                    act(P, e2[:], incl[:], AF.Exp, [Bincl], [Be2], scale=C_DECAY)
                    tt(P, KT[z][0][:, sl], kd[:], e2[:], ALU.mult, [Bkd, Be2], [KT[z][1]])
                    tt(P, BT[z][0][:, sl], kka[:], e2[:], ALU.mult, [Bkka, Be2], [BT[z][1]], eng="gpsimd")
                    yield
                    tt(P, t1[:], incl[:], sg[:], ALU.subtract, [Bincl, Bsg], [Bt1])
                    act(P, e1[:], t1[:], AF.Exp, [Bt1], [Be1], scale=-C_DECAY)
                    tt(P, ARt[:, cs8, 0, :], kk[:].rearrange("p (n c) -> p n c", c=CH), e1[:].rearrange("p (n c) -> p n c", c=CH),
                       ALU.mult, [Bkk, Be1], [BAR])
                    yield
                    t13 = t1[:].rearrange("p (n c) -> p n c", c=CH)
                    tt(P, t13, totb, i3, ALU.subtract, [Bcs, Bincl], [Bt1])
                    act(P, e2[:], t1[:], AF.Exp, [Bt1], [Be2], scale=-C_DECAY)
                    yield
                    for which in range(2):
                        hb, Bhb = tb16.next()
                        if which == 0:
                            tt(P, hb[:], kd[:], e2[:], ALU.mult, [Bkd, Be2], [Bhb])
                            dstt, Bdst = Ktok[z]
                        else:
                            stt(P, hb[:], kka[:], -1.0, e2[:], ALU.mult, ALU.mult, [Bkka, Be2], [Bhb])
                            dstt, Bdst = nBtok[z]
                        pt, Bp = psTb.next()
                        ptb = pt[:].bitcast(BF16)
                        for c in range(8):
                            for hh in range(2):
                                hs = slice(hh * 64, hh * 64 + 64)
                                tr(P, ptb[hs, c * CH:(c + 1) * CH], hb[hs, c * CH:(c + 1) * CH], cx.ident_b[hs, hs], [Bhb, cx.Bident_b], [Bp])
                        act(P, dstt[:, tb * 8:(tb + 1) * 8, :], ptb[:, 0:512].rearrange("p (j c) -> p j c", j=8), AF.Copy, [Bp], [Bdst])
                    if z == 1:
                        tt(P, kd[:], kd[:], kd0[:], ALU.add, [Bkd, Bkd0], [Bkd], eng="gpsimd")
                        u, Bu = tmp["u"]
                        stt(P, u[:], kd[:], rkc, r_s[:, sl], ALU.mult, ALU.mult, [Bkd, Bcolp, Br], [Bu])
                        pa, Bpa = psA.next()
                        mm(P, pa[:], cx.blk[:], u[:], True, True, [cx.Bblk, Bu], [Bpa])
                        tt(P, bonus[:, sl], pa[:], v_s[:, sl], ALU.mult, [Bpa, Bv], [Bbonus])

                gens = [chain(0), chain(1)]
                while gens:
                    for g_ in list(gens):
                        try:
                            next(g_)
                        except StopIteration:
                            gens.remove(g_)
                hb, Bhb = tb16.next()
                vcopy(P, hb[:], v_s[:, sl], [Bv], [Bhb])
                pt, Bp = psTb.next()
                ptb = pt[:].bitcast(BF16)
                for c in range(8):
                    for hh in range(2):
                        hs = slice(hh * 64, hh * 64 + 64)
                        tr(P, ptb[hs, c * CH:(c + 1) * CH], hb[hs, c * CH:(c + 1) * CH], cx.ident_b[hs, hs], [Bhb, cx.Bident_b], [Bp])
                vcopy(P, Vtok[:, tb * 8:(tb + 1) * 8, :], ptb[:, 0:512].rearrange("p (j c) -> p j c", j=8), [Bp], [BVtok])
                pa, Bpa = psA.next()
                mm(P, pa[:], g_up[:, gc], sdg[:, sl], True, True, [Bgup, Bsdg], [Bpa])
                act(P, gate[:, sl], pa[:], AF.Copy, [Bpa], [Bgate])
        if DBG["stop"] < 2:
            return
        X, BX = P.tile("Xm", [128, NCH, 2, 4, CH], BF16)
        with P.scope():
            NT0, BNT0 = P.tile("NT0", [128, NCH, 2, CH], BF16)
            with P.scope():
                pcA = Rot([P.ptile(f"pcA{i}", [128, 2, 256], F32) for i in range(3)])
                pcB = Rot([P.ptile(f"pcB{i}", [128, 2, CH], F32) for i in range(3)])
                for i in range(NCH // 2):
                    for z in range(2):
                        pa, Bpa = pcA.next()
                        pb_, Bpb = pcB.next()
                        ARt, BAR = AR[z]
                        for j in range(2):
                            n = 2 * i + j
                            ns = slice(n * CH, (n + 1) * CH)
                            for hh in range(2):
                                hs = slice(hh * 64, hh * 64 + 64)
                                arr = ARt[hs, n, :, :].rearrange("p a c -> p (a c)")
                                mm(P, pa[hs, j, 0:128], BT[z][0][hs, ns], arr, True, True, [BT[z][1], BAR], [Bpa])
                                mm(P, pa[hs, j, 128:256], KT[z][0][hs, ns], arr, True, True, [KT[z][1], BAR], [Bpa])
                                mm(P, pb_[hs, j, :], ARt[hs, n, 0, :], BT[z][0][hs, ns], True, True, [BAR, BT[z][1]], [Bpb])
                        tt(P, X[:, 2 * i:2 * i + 2, z, :, :], pa[:].rearrange("p j (w c) -> p j w c", w=4), maskA[:, z, :, :, :], ALU.mult,
                           [Bpa, BmaskA], [BX])
                        tt(P, NT0[:, 2 * i:2 * i + 2, z, :], pb_[:], maskB[:, z, :, :], ALU.mult, [Bpb, BmaskB], [BNT0])
            if DBG["stop"] < 3:
                return
            with P.scope():
                NB = 8
                NSTR = 2
                Xm = X[:].rearrange("p n z w c -> p (n z) w c")
                NT0m = NT0[:].rearrange("p n z c -> p (n z) c")
                streams = []
                for si in range(NSTR):
                    st_ = Ctx()
                    st_.psN = P.ptile(f"psN{si}", [128, NB, CH], F32)
                    st_.psNT = P.ptile(f"psNT{si}", [128, NB, CH], F32)
                    st_.psI = P.ptile(f"psI{si}", [128, NB, CH], F32)
                    st_.Ns = Rot([P.tile(f"Ncur{si}_{i}", [128, NB, CH], BF16) for i in range(2)])
                    st_.NTs = Rot([P.tile(f"NTcur{si}_{i}", [128, NB, CH], BF16) for i in range(2)])
                    st_.Invs = Rot([P.tile(f"Inv{si}_{i}", [128, NB, CH], BF16) for i in range(2)])
                    streams.append(st_)
                nbatch = 64 // NB
                for b0 in range(0, nbatch, NSTR):
                    act_streams = []
                    for si in range(NSTR):
                        bi = b0 + si
                        st_ = streams[si]
                        st_.ms = slice(bi * NB, (bi + 1) * NB)
                        st_.Nprev = (lambda m, hs, bi=bi: Xm[hs, bi * NB + m, 0, :])
                        st_.NTprev = (lambda m, hs, bi=bi: NT0m[hs, bi * NB + m, :])
                        st_.BNprev, st_.BNTprev = BX, BNT0
                        st_.Inv, st_.BInv = st_.Invs.next()
                        tt(P, st_.Inv[:], Xm[:, st_.ms, 0, :], identblk[:].unsqueeze(1).to_broadcast([128, NB, CH]), ALU.add,
                           [BX, Bidb], [st_.BInv])
                        act_streams.append(st_)
                    for lev in range(1, 6):
                        for st_ in act_streams:
                            st_.Nn, st_.BNn = st_.Ns.next()
                            st_.NTn, st_.BNTn = st_.NTs.next()
                            for m in range(NB):
                                for hh in range(2):
                                    hs = slice(hh * 64, hh * 64 + 64)
                                    if lev < 5:
                                        mm(P, st_.psN[0][hs, m, :], st_.NTprev(m, hs), st_.Nprev(m, hs), True, True,
                                           [st_.BNprev, st_.BNTprev], [st_.psN[1]])
                                    mm(P, st_.psNT[0][hs, m, :], st_.Nprev(m, hs), st_.NTprev(m, hs), True, True,
                                       [st_.BNprev, st_.BNTprev], [st_.psNT[1]])
                            if lev < 5:
                                act(P, st_.Nn[:], st_.psN[0][:], AF.Copy, [st_.psN[1]], [st_.BNn])
                            vcopy(P, st_.NTn[:], st_.psNT[0][:], [st_.psNT[1]], [st_.BNTn])
                        for st_ in act_streams:
                            for m in range(NB):
                                for hh in range(2):
                                    hs = slice(hh * 64, hh * 64 + 64)
                                    mm(P, st_.psI[0][hs, m, :], st_.NTn[hs, m, :], st_.Inv[hs, m, :], True, True,
                                       [st_.BNTn, st_.BInv], [st_.psI[1]])
                            if lev < 5:
                                Inv2, BInv2 = st_.Invs.next()
                                tt(P, Inv2[:], st_.psI[0][:], st_.Inv[:], ALU.add, [st_.psI[1], st_.BInv], [BInv2])
                                st_.Inv, st_.BInv = Inv2, BInv2
                            else:
                                tt(P, Xm[:, st_.ms, 0, :], st_.psI[0][:], st_.Inv[:], ALU.add, [st_.psI[1], st_.BInv], [BX])
                            st_.Nprev = (lambda m, hs, Nn=st_.Nn: Nn[hs, m, :])
                            st_.NTprev = (lambda m, hs, NTn=st_.NTn: NTn[hs, m, :])
                            st_.BNprev, st_.BNTprev = st_.BNn, st_.BNTn
        if DBG["stop"] < 4:
            return
        Yz, BYz = P.tile("Yz", [128, 2, T], F32)
        with P.scope():
            ST, BST = P.tile("ST", [128, 2, CH], F32)
            STb, BSTb = P.tile("STb", [128, 2, CH], BF16)
            memset(P, ST[:], 0.0, [BST])
            memset(P, STb[:], 0.0, [BSTb])
            psWs = Rot([P.ptile(f"psW{i}", [128, 2, CH], F32) for i in range(2)])
            psPs = Rot([P.ptile(f"psP{i}", [128, 2, CH], F32) for i in range(2)])
            psYs = Rot([P.ptile(f"psYs{i}", [128, 2, CH], F32) for i in range(2)])
            psSs = Rot([P.ptile(f"psS{i}", [128, 2, CH], F32) for i in range(2)])
            Wsbs = Rot([P.tile(f"Wsb{i}", [128, 2, CH], BF16) for i in range(2)])
            Psbs = Rot([P.tile(f"Psb{i}", [128, 2, CH], BF16) for i in range(2)])
            H = [slice(0, 64), slice(64, 128)]
            for n in range(NCH):
                czs = [n, NCH - 1 - n]
                pW, BpW = psWs.next()
                pP, BpP = psPs.next()
                pY, BpY = psYs.next()
                pS, BpS = psSs.next()
                Wsb, BWsb = Wsbs.next()
                Psb, BPsb = Psbs.next()
                for z in range(2):
                    cz = czs[z]
                    for hs in H:
                        mm(P, pW[hs, z, :], AR[z][0][hs, cz, 0, :], STb[hs, z, :], True, False, [AR[z][1], BSTb], [BpW])
                        mm(P, pW[hs, z, :], X[hs, cz, z, 2, :], Vtok[hs, cz, :], False, True, [BX, BVtok], [BpW])
                vcopy(P, Wsb[:], pW[:], [BpW], [BWsb])
                for z in range(2):
                    cz = czs[z]
                    for hs in H:
                        mm(P, pP[hs, z, :], X[hs, cz, z, 0, :], Wsb[hs, z, :], True, True, [BX, BWsb], [BpP])
                act(P, Psb[:], pP[:], AF.Copy, [BpP], [BPsb])
                for z in range(2):
                    cz = czs[z]
                    for hs in H:
                        mm(P, pS[hs, z, :], Ktok[z][0][hs, cz, :], Vtok[hs, cz, :], True, False, [Ktok[z][1], BVtok], [BpS])
                        mm(P, pS[hs, z, :], nBtok[z][0][hs, cz, :], Psb[hs, z, :], False, True, [nBtok[z][1], BPsb], [BpS])
                for z in range(2):
                    cz = czs[z]
                    for hs in H:
                        mm(P, pY[hs, z, :], STb[hs, z, :], AR[z][0][hs, cz, 1, :], True, False, [BSTb, AR[z][1]], [BpY])
                        mm(P, pY[hs, z, :], Vtok[hs, cz, :], X[hs, cz, z, 3, :], False, False, [BVtok, BX], [BpY])
                        mm(P, pY[hs, z, :], Psb[hs, z, :], X[hs, cz, z, 1, :], False, True, [BPsb, BX], [BpY])
                for z in range(2):
                    cz = czs[z]
                    ts(P, ST[:, z, :], ST[:, z, :], gamC[z][0][:, cz:cz + 1], ALU.mult, [BST, gamC[z][1]], [BST], eng="gpsimd")
                tt(P, ST[:], ST[:], pS[:], ALU.add, [BST, BpS], [BST])
                act(P, STb[:], ST[:], AF.Copy, [BST], [BSTb])
                for z in range(2):
                    cz = czs[z]
                    if z == 0:
                        act(P, Yz[:, z, cz * CH:(cz + 1) * CH], pY[:, z, :], AF.Copy, [BpY], [BYz])
                    else:
                        vcopy(P, Yz[:, z, cz * CH:(cz + 1) * CH], pY[:, z, :], [BpY], [BYz])
        if DBG["stop"] < 5:
            return
        with P.scope():
            psA = Rot([P.ptile(f"opa{i}", [128, 512], F32) for i in range(4)])
            bufs4 = Rot([tuple(P.tile(f"o{nm}{i}", [128, 512], F32) for nm in ("ysum", "yc", "sq", "rs")) for i in range(2)])
            yT, ByT = P.tile("ryT", [128, T], BF16)

            def chain4(tb):
                sl = slice(tb * 512, (tb + 1) * 512)
                (ysum, Bys), (yc, Byc), (sq, Bsq), (rs, Brs) = bufs4.next()
                tt(P, ysum[:], Yz[:, 0, sl], Yz[:, 1, sl], ALU.add, [BYz], [Bys])
                pa, Bpa = psA.next()
                mm(P, pa[:], cx.blk64[:], ysum[:], True, True, [cx.Bblk64, Bys], [Bpa])
                yield
                tt(P, yc[:], ysum[:], pa[:], ALU.subtract, [Bys, Bpa], [Byc])
                act(P, sq[:], yc[:], AF.Square, [Byc], [Bsq])
                pa, Bpa = psA.next()
                mm(P, pa[:], cx.blk64[:], sq[:], True, True, [cx.Bblk64, Bsq], [Bpa])
                yield
                ts(P, rs[:], pa[:], 64e-5, ALU.add, [Bpa], [Brs])
                act(P, rs[:], rs[:], AF.Sqrt, [Brs], [Brs])
                yield
                P.op("vector", lambda e: e.reciprocal(out=rs[:], in_=rs[:]), reads=[Brs], writes=[Brs])
                tt(P, yc[:], yc[:], rs[:], ALU.mult, [Byc, Brs], [Byc])
                yield
                ts(P, yc[:], yc[:], colp[:, 108 + g:109 + g], ALU.mult, [Byc, Bcolp], [Byc], s2=colp[:, 116 + g:117 + g], op1=ALU.add)
                tt(P, yc[:], yc[:], bonus[:, sl], ALU.add, [Byc, Bbonus], [Byc], eng="gpsimd")
                tt(P, yT[:, sl], yc[:], gate[:, sl], ALU.mult, [Byc, Bgate], [ByT])
            for pair in range(2):
                gens = [chain4(2 * pair), chain4(2 * pair + 1)]
                while gens:
                    for g_ in list(gens):
                        try:
                            next(g_)
                        except StopIteration:
                            gens.remove(g_)
            P.dma("sync", cx.ybrT[g * 128:(g + 1) * 128, :], yT[:], reads=[ByT], writes=[cx.BybrT[g]])


def host_colp(inputs, nl):
    out = np.zeros((nl, 128, 160), np.float32)
    for l in range(nl):
        mu = inputs["rwkv_mu"][l]
        out[l, :, 0:26] = mu[0].reshape(26, 128).T
        out[l, :, 26:52] = mu[1].reshape(26, 128).T
        for z in range(2):
            out[l, :, 52 + z * 8:60 + z * 8] = inputs["rwkv_w0"][l, z].reshape(8, 128).T
            out[l, :, 68 + z * 8:76 + z * 8] = inputs["rwkv_a0"][l, z].reshape(8, 128).T
        out[l, :, 84:92] = inputs["rwkv_k_k"][l].reshape(8, 128).T
        out[l, :, 92:100] = inputs["rwkv_k_a"][l].reshape(8, 128).T
        out[l, :, 100:108] = inputs["rwkv_r_k"][l].reshape(8, 128).T
        out[l, :, 108:116] = inputs["rwkv_gn_g"][l].reshape(8, 128).T
        out[l, :, 116:124] = inputs["rwkv_gn_b"][l].reshape(8, 128).T
    return out


def build_program(nl=NL, phases=("rwkv", "win", "diff", "mem", "merge", "moe"), dbg=False):
    nc = bass.Bass("TRN2", target_bir_lowering=False)
    cx = Ctx()
    shapes = [(nm, ((nl,) + shp[1:]) if (shp[0] == NL and nm not in ("x",)) else shp) for nm, shp in INPUT_SHAPES]
    declare_inputs(nc, cx, shapes)
    cx.y = nc.dram_tensor("y", [T, D], F32, kind="ExternalOutput").ap()
    kind = "ExternalOutput" if dbg else "Internal"
    cx.ybrT = nc.dram_tensor("ybrT", [26 * 128, T], BF16, kind=kind).ap()
    cx.x1res = nc.dram_tensor("x1res", [T, D], F32, kind=kind).ap()
    cx.xres = nc.dram_tensor("xres", [T, D], F32, kind=kind).ap()
    P = Prog(nc)
    cx.BybrT = [P.buf(f"ybr{i}") for i in range(26)]
    cx.Bx1res = P.buf("x1res")
    cx.Bxres = P.buf("xres")
    cx.By = P.buf("y")
    setup_consts(P, cx)
    cx.memT, cx.BmemT = P.tile("memT", [128, DC, 256], BF16)
    cx.aff_tok, cx.Baff_tok = P.tile("aff_tok", [128, NT, 16], F32)
    build_memT(P, cx)
    for l in range(nl):
        src = cx.x if l == 0 else cx.xres
        with P.scope():
            cx.xT, cx.BxT = P.tile("xT", [128, DC, T], BF16)
            build_xT(P, cx, src)
            if "rwkv" in phases:
                phase_rwkv(P, cx, l)
            if "win" in phases:
                phase_window(P, cx, l)
            if "diff" in phases:
                phase_diff(P, cx, l)
            if "mem" in phases:
                phase_mem(P, cx, l)
            if "merge" in phases:
                phase_merge(P, cx, l, src)
        if "moe" in phases:
            last = (l == nl - 1)
            phase_moe(P, cx, l, cx.y if last else cx.xres, cx.By if last else cx.Bxres)
    P.barrier()
    P.emit()
    return nc, P


def make_in_maps(inputs, nl=NL, cores=NCORES):
    G, farb = host_tables(np.asarray(inputs["rel_bias"], np.float32))
    colp = host_colp(inputs, nl)
    shared = {"tblG": G, "farb": farb, "colp": colp}
    for nm, shp in INPUT_SHAPES:
        if nm in ("x", "mem", "tblG", "farb", "colp"):
            continue
        a = np.asarray(inputs[nm], np.float32)
        shared[nm] = np.ascontiguousarray(a[:nl]) if shp[0] == NL else a
    maps = []
    for c in range(cores):
        m = dict(shared)
        m["x"] = np.ascontiguousarray(np.asarray(inputs["x"][c], np.float32))
        m["mem"] = np.ascontiguousarray(np.asarray(inputs["mem"][c], np.float32))
        maps.append(m)
    return maps


_CACHE = {}


def kernel(**inputs):
    if "nc" not in _CACHE:
        _CACHE["nc"] = build_program()[0]
    nc = _CACHE["nc"]
    maps = make_in_maps(inputs)
    res = run_bass_kernel_spmd(nc, maps, core_ids=list(range(NCORES)))
    out = np.stack([np.asarray(r["y"], np.float32) for r in res.results], axis=0)
    return out
```
